# Optimizing a Trainium2 kernel written in Bass

```python
import math
import jax, jax.numpy as jnp
from jax import lax
import numpy as np

D_MODEL = 1024
BATCH = 16
SEQ = 256
DEPTH = 2
DEC_BATCH = 4
DEC_SEQ = 1024
PAST_LEN = 256

GRID_W = 64
HEAD_DIM = 64
DA_HEADS = 4
DA_QK = 2 * HEAD_DIM
DA_V = 2 * HEAD_DIM
WG_Q_HEADS = 8
WG_KV_HEADS = 2
WINDOW = 128
RT_HEADS = 4
RT_DK = HEAD_DIM
RT_DV = 2 * HEAD_DIM
RT_CHUNK = 128
BRANCH_W = 512
SPLITS = (DA_HEADS * DA_QK, DA_HEADS * DA_QK, DA_HEADS * DA_V,
          WG_Q_HEADS * HEAD_DIM, WG_KV_HEADS * HEAD_DIM, WG_KV_HEADS * HEAD_DIM,
          RT_HEADS * RT_DK, RT_HEADS * RT_DK, RT_HEADS * RT_DV, RT_HEADS * RT_DV)
D_IN = sum(SPLITS)
D_FF = 4 * D_MODEL
Q_BLOCK = 128
ROPE_BASE = 10000.0
LN_EPS = 1e-5
NEG_INF = -1e30
ALPHA = (2 * DEPTH) ** 0.25
BETA = (8 * DEPTH) ** -0.25

kernel_name = "hybrid_diffattn_swa_retention_dit_step"

F32 = jnp.float32


def layer_norm(x, g, b):
    xf = x.astype(F32)
    mu = jnp.mean(xf, axis=-1, keepdims=True)
    var = jnp.mean(jnp.square(xf - mu), axis=-1, keepdims=True)
    y = (xf - mu) * lax.rsqrt(var + LN_EPS)
    return (y * g.astype(F32) + b.astype(F32)).astype(x.dtype)


def head_rms_norm(x, g):
    xf = x.astype(F32)
    y = xf * lax.rsqrt(jnp.mean(jnp.square(xf), axis=-1, keepdims=True) + LN_EPS)
    return (y * g.astype(F32)).astype(x.dtype)


def head_layer_norm(x, g):
    xf = x.astype(F32)
    mu = jnp.mean(xf, axis=-1, keepdims=True)
    var = jnp.mean(jnp.square(xf - mu), axis=-1, keepdims=True)
    return ((xf - mu) * lax.rsqrt(var + LN_EPS) * g.astype(F32)).astype(x.dtype)


def axial_rope_angles(rows, dim):
    r, col = jnp.meshgrid(jnp.arange(rows), jnp.arange(GRID_W), indexing="ij")
    r = r.reshape(-1).astype(F32)
    col = col.reshape(-1).astype(F32)
    nf = dim // 4
    inv = ROPE_BASE ** (-jnp.arange(nf, dtype=F32) / nf)
    return r[:, None] * inv[None, :], col[:, None] * inv[None, :]


def apply_axial_rope(x, ang):
    ang_r, ang_c = ang

    def rot(u, a):
        cos = jnp.cos(a)[None, :, None, :]
        sin = jnp.sin(a)[None, :, None, :]
        u1, u2 = jnp.split(u.astype(F32), 2, axis=-1)
        return jnp.concatenate([u1 * cos - u2 * sin, u1 * sin + u2 * cos], axis=-1)

    xr, xc = jnp.split(x, 2, axis=-1)
    return jnp.concatenate([rot(xr, ang_r), rot(xc, ang_c)], axis=-1).astype(x.dtype)


def rope_subheads(x, ang):
    b, t, h, _ = x.shape
    return apply_axial_rope(x.reshape(b, t, h * 2, HEAD_DIM), ang).reshape(b, t, h, 2 * HEAD_DIM)


def to_blocks(t, size):
    b, n = t.shape[:2]
    return jnp.moveaxis(t.reshape(b, n // size, size, *t.shape[2:]), 1, 0)


def from_blocks(o):
    o = jnp.moveaxis(o, 0, 1)
    return o.reshape(o.shape[0], o.shape[1] * o.shape[2], *o.shape[3:])


def diff_attention(q, k, v, lam):
    scale = HEAD_DIM ** -0.5
    k1, k2 = k[..., :HEAD_DIM], k[..., HEAD_DIM:]

    def block(qb):
        q1, q2 = qb[..., :HEAD_DIM], qb[..., HEAD_DIM:]
        s1 = jnp.einsum("bqhd,bkhd->bhqk", q1, k1).astype(F32) * scale
        s2 = jnp.einsum("bqhd,bkhd->bhqk", q2, k2).astype(F32) * scale
        a = jax.nn.softmax(s1, axis=-1) - lam * jax.nn.softmax(s2, axis=-1)
        return jnp.einsum("bhqk,bkhe->bqhe", a.astype(v.dtype), v)

    return from_blocks(lax.map(block, to_blocks(q, Q_BLOCK)))


def sink_logits(sink, shape):
    hkv, g = shape[1], shape[2]
    return jnp.broadcast_to(sink.astype(F32).reshape(hkv, g)[None, :, :, None, None], shape[:-1] + (1,))


def dense_sink_attention(q, k, v, sink):
    b, t, hq, d = q.shape
    hkv = k.shape[2]
    q = q.reshape(b, t, hkv, hq // hkv, d)
    scale = d ** -0.5

    def block(qb):
        s = jnp.einsum("bqhgd,bkhd->bhgqk", qb, k).astype(F32) * scale
        p = jax.nn.softmax(jnp.concatenate([s, sink_logits(sink, s.shape)], axis=-1), axis=-1)[..., :-1]
        return jnp.einsum("bhgqk,bkhd->bqhgd", p.astype(v.dtype), v)

    return from_blocks(lax.map(block, to_blocks(q, Q_BLOCK))).reshape(b, t, hq * d)


def window_sink_attention(q, k, v, k_ctx, v_ctx, sink):
    b, t, hq, d = q.shape
    hkv = k.shape[2]
    nb = t // WINDOW
    q = q.reshape(b, t, hkv, hq // hkv, d)
    scale = d ** -0.5
    pad = ((0, 0), (WINDOW, WINDOW), (0, 0), (0, 0))
    kp = jnp.pad(k, pad)
    vp = jnp.pad(v, pad)
    qi = jnp.arange(WINDOW)[:, None]
    kj = jnp.arange(3 * WINDOW)[None, :]
    rel = kj - WINDOW - qi

    def block(args):
        n, qb = args
        kb = lax.dynamic_slice_in_dim(kp, n * WINDOW, 3 * WINDOW, axis=1)
        vb = lax.dynamic_slice_in_dim(vp, n * WINDOW, 3 * WINDOW, axis=1)
        kpos = n * WINDOW - WINDOW + kj
        valid = (jnp.abs(rel) <= WINDOW) & (kpos >= 0) & (kpos < t)
        s_loc = jnp.einsum("bqhgd,bkhd->bhgqk", qb, kb).astype(F32) * scale
        s_loc = jnp.where(valid, s_loc, NEG_INF)
        s_ctx = jnp.einsum("bqhgd,bchd->bhgqc", qb, k_ctx).astype(F32) * scale
        p = jax.nn.softmax(jnp.concatenate([s_loc, s_ctx, sink_logits(sink, s_loc.shape)], axis=-1), axis=-1)
        p_loc = p[..., :3 * WINDOW].astype(v.dtype)
        p_ctx = p[..., 3 * WINDOW:-1].astype(v.dtype)
        return (jnp.einsum("bhgqk,bkhd->bqhgd", p_loc, vb)
                + jnp.einsum("bhgqc,bchd->bqhgd", p_ctx, v_ctx))

    o = lax.map(block, (jnp.arange(nb), to_blocks(q, WINDOW)))
    return from_blocks(o).reshape(b, t, hq * d)


def retention_direction(q, k, v, log_g, s0):
    c = RT_CHUNK
    pos = jnp.arange(c, dtype=F32)
    diff = pos[:, None] - pos[None, :]
    decay_in = jnp.where(diff[None] >= 0,
                         jnp.exp(jnp.maximum(diff, 0.0)[None] * log_g[:, None, None]), 0.0)
    xi = jnp.exp((pos + 1.0)[None, :] * log_g[:, None]).T[None, :, :, None]
    zeta = jnp.exp((c - 1.0 - pos)[None, :] * log_g[:, None])
    g_chunk = jnp.exp(c * log_g)[None, :, None, None]

    def chunk(s, inp):
        qc, kc, vc = inp
        a = jnp.einsum("bqhd,bkhd->bhqk", qc, kc) * decay_in
        inner = jnp.einsum("bhqk,bkhe->bqhe", a, vc)
        cross = jnp.einsum("bqhd,bhde->bqhe", qc, s) * xi
        s_new = g_chunk * s + jnp.einsum("bkhd,bkhe,hk->bhde", kc, vc, zeta)
        return s_new, inner + cross

    s_fin, out = lax.scan(chunk, s0, (to_blocks(q, c), to_blocks(k, c), to_blocks(v, c)))
    return from_blocks(out), s_fin


def bidirectional_retention(q, k, v, log_g2, s0):
    q, k, v = q.astype(F32), k.astype(F32), v.astype(F32)
    s0 = s0.astype(F32)
    y_f, s_f = retention_direction(q, k, v, log_g2[0], s0[:, 0])
    flip = lambda a: jnp.flip(a, axis=1)
    y_b, s_b = retention_direction(flip(q), flip(k), flip(v), log_g2[1], s0[:, 1])
    return y_f + flip(y_b), jnp.stack([s_f, s_b], axis=1)


def token_mixer(h, P, lam_init, ctx, rope_ang):
    b, t, _ = h.shape
    z = h @ P["w_in"]
    parts = []
    off = 0
    for s in SPLITS:
        parts.append(z[..., off:off + s])
        off += s
    aq, ak, av, bq, bk, bv, cq, ck, cv, cg = parts
    aq = aq.reshape(b, t, DA_HEADS, DA_QK)
    ak = ak.reshape(b, t, DA_HEADS, DA_QK)
    av = av.reshape(b, t, DA_HEADS, DA_V)
    bq = bq.reshape(b, t, WG_Q_HEADS, HEAD_DIM)
    bk = bk.reshape(b, t, WG_KV_HEADS, HEAD_DIM)
    bv = bv.reshape(b, t, WG_KV_HEADS, HEAD_DIM)
    cq = cq.reshape(b, t, RT_HEADS, RT_DK)
    ck = ck.reshape(b, t, RT_HEADS, RT_DK) * (RT_DK ** -0.5)
    cv = cv.reshape(b, t, RT_HEADS, RT_DV)
    if rope_ang is not None:
        aq = rope_subheads(aq, rope_ang)
        ak = rope_subheads(ak, rope_ang)
        bq = apply_axial_rope(bq, rope_ang)
        bk = apply_axial_rope(bk, rope_ang)
    lv = P["diff_lam"].astype(F32)
    lam = jnp.exp(jnp.sum(lv[0] * lv[1])) - jnp.exp(jnp.sum(lv[2] * lv[3])) + lam_init
    if ctx is None:
        oa = diff_attention(aq, ak, av, lam)
        ob = dense_sink_attention(bq, bk, bv, P["win_sink"])
        s0 = jnp.zeros((b, 2, RT_HEADS, RT_DK, RT_DV), F32)
    else:
        ctx_ak, ctx_av, ctx_bk, ctx_bv, s0 = ctx
        oa = diff_attention(aq, jnp.concatenate([ak, ctx_ak], axis=1),
                            jnp.concatenate([av, ctx_av], axis=1), lam)
        ob = window_sink_attention(bq, bk, bv, ctx_bk, ctx_bv, P["win_sink"])
    oa = (head_rms_norm(oa, P["diff_norm_g"]) * (1.0 - lam_init)).reshape(b, t, BRANCH_W)
    log_g2 = jax.nn.log_sigmoid(P["ret_decay"].astype(F32))
    yc, s_fin = bidirectional_retention(cq, ck, cv, log_g2, s0)
    oc = head_layer_norm(yc, P["ret_norm_g"]).astype(h.dtype).reshape(b, t, BRANCH_W) * jax.nn.silu(cg)
    gates = jax.nn.sigmoid(h @ P["w_gate"] + P["b_gate"])
    ga, gb, gc = jnp.split(gates, 3, axis=-1)
    merged = ga * (oa @ P["w_pa"]) + gb * (ob @ P["w_pb"]) + gc * (oc @ P["w_pc"])
    out = merged @ P["w_o"]
    new_ctx = (ak, av, bk, bv, s_fin.astype(h.dtype)) if ctx is None else None
    return out, new_ctx


def trunk_layer(x, mod, P, lam_init, ctx, rope_ang):
    sh1, sc1, g1, sh2, sc2, g2 = jnp.split(mod, 6, axis=-1)
    h = x * (1.0 + sc1) + sh1
    y, new_ctx = token_mixer(h, P, lam_init, ctx, rope_ang)
    x = layer_norm(ALPHA * x + g1 * y, P["ln1_g"], P["ln1_b"])
    h = x * (1.0 + sc2) + sh2
    f = jnp.square(jax.nn.relu(h @ P["w_ff1"])) @ P["w_ff2"]
    x = layer_norm(ALPHA * x + g2 * f, P["ln2_g"], P["ln2_b"])
    return x, new_ctx


def setup_inputs(seed: int = 0) -> dict:
    key = jax.random.key(seed)
    ks = jax.random.split(key, 40)
    nrm = lambda k, shape, s: jax.random.normal(k, shape, F32) * s
    D = D_MODEL
    p = 2.0 ** (-(5.0 + jnp.arange(RT_HEADS, dtype=F32)))
    base_logit = jnp.log1p(-p) - jnp.log(p)
    return {
        "x_prompt": nrm(ks[0], (BATCH, SEQ, D), 1.0),
        "x_sample": nrm(ks[1], (DEC_BATCH, DEC_SEQ, D), 1.0),
        "c": nrm(ks[2], (DEC_BATCH, D), 1.0),
        "cache_diff_k": nrm(ks[3], (DEC_BATCH, DEPTH, PAST_LEN, DA_HEADS, DA_QK), 1.0),
        "cache_diff_v": nrm(ks[4], (DEC_BATCH, DEPTH, PAST_LEN, DA_HEADS, DA_V), 1.0),
        "cache_win_k": nrm(ks[5], (DEC_BATCH, DEPTH, PAST_LEN, WG_KV_HEADS, HEAD_DIM), 1.0),
        "cache_win_v": nrm(ks[6], (DEC_BATCH, DEPTH, PAST_LEN, WG_KV_HEADS, HEAD_DIM), 1.0),
        "state_ret": nrm(ks[7], (DEC_BATCH, DEPTH, 2, RT_HEADS, RT_DK, RT_DV), 1.0),
        "c_ctx": nrm(ks[8], (D,), 1.0),
        "w_mod": nrm(ks[9], (DEPTH, D, 6 * D), D ** -0.5),
        "b_mod": nrm(ks[10], (DEPTH, 6 * D), 0.02),
        "w_in": nrm(ks[11], (DEPTH, D, D_IN), D ** -0.5),
        "diff_lam": nrm(ks[12], (DEPTH, 4, HEAD_DIM), 0.1),
        "diff_norm_g": 1.0 + nrm(ks[13], (DEPTH, DA_V), 0.02),
        "win_sink": nrm(ks[14], (DEPTH, WG_Q_HEADS), 0.5),
        "ret_decay": base_logit[None, None, :] + nrm(ks[15], (DEPTH, 2, RT_HEADS), 0.05),
        "ret_norm_g": 1.0 + nrm(ks[16], (DEPTH, RT_DV), 0.02),
        "w_pa": nrm(ks[17], (DEPTH, BRANCH_W, D), BRANCH_W ** -0.5),
        "w_pb": nrm(ks[18], (DEPTH, BRANCH_W, D), BRANCH_W ** -0.5),
        "w_pc": nrm(ks[19], (DEPTH, BRANCH_W, D), BRANCH_W ** -0.5),
        "w_gate": nrm(ks[20], (DEPTH, D, 3 * D), D ** -0.5),
        "b_gate": nrm(ks[21], (DEPTH, 3 * D), 0.02),
        "w_o": nrm(ks[22], (DEPTH, D, D), BETA * D ** -0.5),
        "ln1_g": 1.0 + nrm(ks[23], (DEPTH, D), 0.02),
        "ln1_b": nrm(ks[24], (DEPTH, D), 0.02),
        "w_ff1": nrm(ks[25], (DEPTH, D, D_FF), D ** -0.5),
        "w_ff2": nrm(ks[26], (DEPTH, D_FF, D), BETA * D_FF ** -0.5),
        "ln2_g": 1.0 + nrm(ks[27], (DEPTH, D), 0.02),
        "ln2_b": nrm(ks[28], (DEPTH, D), 0.02),
    }


def reference(x_prompt, x_sample, c, cache_diff_k, cache_diff_v, cache_win_k, cache_win_v, state_ret,
              c_ctx, w_mod, b_mod, w_in, diff_lam, diff_norm_g, win_sink, ret_decay, ret_norm_g,
              w_pa, w_pb, w_pc, w_gate, b_gate, w_o, ln1_g, ln1_b, w_ff1, w_ff2, ln2_g, ln2_b):
    rows = x_sample.shape[1] // GRID_W
    rope_ang = axial_rope_angles(rows, HEAD_DIM)
    xp = x_prompt
    xs = x_sample
    diff_k, diff_v, win_k, win_v, ret_s = [], [], [], [], []
    for l in range(DEPTH):
        P = {
            "w_in": w_in[l], "diff_lam": diff_lam[l], "diff_norm_g": diff_norm_g[l],
            "win_sink": win_sink[l], "ret_decay": ret_decay[l], "ret_norm_g": ret_norm_g[l],
            "w_pa": w_pa[l], "w_pb": w_pb[l], "w_pc": w_pc[l], "w_gate": w_gate[l],
            "b_gate": b_gate[l], "w_o": w_o[l], "ln1_g": ln1_g[l], "ln1_b": ln1_b[l],
            "w_ff1": w_ff1[l], "w_ff2": w_ff2[l], "ln2_g": ln2_g[l], "ln2_b": ln2_b[l],
        }
        lam_init = 0.8 - 0.6 * math.exp(-0.3 * l)
        mod_ctx = (jax.nn.silu(c_ctx) @ w_mod[l] + b_mod[l])[None, None, :]
        xp, ctx_l = trunk_layer(xp, mod_ctx, P, lam_init, None, None)
        diff_k.append(ctx_l[0])
        diff_v.append(ctx_l[1])
        win_k.append(ctx_l[2])
        win_v.append(ctx_l[3])
        ret_s.append(ctx_l[4])
        mod_lat = (jax.nn.silu(c) @ w_mod[l] + b_mod[l])[:, None, :]
        cache_l = (cache_diff_k[:, l], cache_diff_v[:, l], cache_win_k[:, l], cache_win_v[:, l], state_ret[:, l])
        xs, _ = trunk_layer(xs, mod_lat, P, lam_init, cache_l, rope_ang)
    new_diff_k = jnp.stack(diff_k, axis=1)
    new_diff_v = jnp.stack(diff_v, axis=1)
    new_win_k = jnp.stack(win_k, axis=1)
    new_win_v = jnp.stack(win_v, axis=1)
    new_state_ret = jnp.stack(ret_s, axis=1)
    return (xp, xs, new_diff_k, new_diff_v, new_win_k, new_win_v, new_state_ret)
```

```python
import contextlib
import numpy as np
import concourse.bass as bass
import concourse.mybir as mybir
from concourse.bass_utils import run_bass_kernel_spmd

F32 = mybir.dt.float32
BF16 = mybir.dt.bfloat16
AF = mybir.ActivationFunctionType
ALU = mybir.AluOpType
AX = mybir.AxisListType

PHASES = []
DEPTH = 2
T = 1024
NT = 8
D = 1024
LN_EPS = 1e-5
ALPHA = (2 * DEPTH) ** 0.25
NEG = -30000.0
NW = 3


class Atom:
    def __init__(s, name, excl=False):
        s.name = name
        s.lw = None
        s.rd = []
        s.excl = excl
        s.dcnt = 0


class Buf:
    def __init__(s, ap, atoms):
        s.ap = ap
        s.atoms = atoms

    def __getitem__(s, idx):
        return s.ap[idx]


class Rec:
    def __init__(s):
        s.calls = []

    def __getattr__(s, name):
        def f(*a, **k):
            s.calls.append((name, a, k))
            return s
        return f


class Trk:
    def __init__(s, nc, es):
        s.nc = nc
        s.es = es
        s.ops = {k: [] for k in ("pe", "act", "dve", "pool", "sp")}
        s.cnt = {k: 0 for k in s.ops}
        s.seen = {k: {} for k in s.ops}
        s.sems = {}
        s.stores = {}
        for k in ("pe", "act", "dve", "pool"):
            s.sems[k] = es.enter_context(nc.semaphore("s_" + k))

    def dsem(s, atom):
        key = "d_" + atom.name
        if key not in s.sems:
            s.sems[key] = s.es.enter_context(s.nc.semaphore(key))
        return key

    def _waits(s, eng, reads, writes):
        deps = []
        for b in reads:
            for a in b.atoms:
                if a.lw is not None:
                    deps.append(a.lw)
                if a.excl:
                    deps.extend(a.rd)
        for b in writes:
            for a in b.atoms:
                if a.lw is not None:
                    deps.append(a.lw)
                deps.extend(a.rd)
        best = {}
        for (k, v) in deps:
            if k == "pe" and eng == "pe":
                continue
            if v > best.get(k, 0):
                best[k] = v
        out = []
        for k, v in best.items():
            if s.seen[eng].get(k, 0) >= v:
                continue
            s.seen[eng][k] = v
            out.append((k, v))
        return out

    def _mark(s, tok, reads, writes):
        for b in reads:
            for a in b.atoms:
                if a.excl:
                    a.lw = tok
                    a.rd = []
                else:
                    a.rd.append(tok)
        for b in writes:
            for a in b.atoms:
                a.lw = tok
                a.rd = []

    def op(s, eng, fn, reads=(), writes=()):
        w = s._waits(eng, reads, writes)
        s.cnt[eng] += 1
        tok = (eng, s.cnt[eng])
        rec = Rec()
        fn(rec)
        s.ops[eng].append((w, rec.calls, (eng, 1)))
        s._mark(tok, reads, writes)

    def dma(s, q, out, in_, buf, load, after=()):
        a0 = buf.atoms[0]
        key = s.dsem(a0)
        w = s._waits(q, tuple(after) if load else (buf,), (buf,) if load else ())
        a0.dcnt += 1
        tok = (key, 16 * a0.dcnt)
        s.ops[q].append((w, [("dma_start", (), dict(out=out, in_=in_))], (key, 16)))
        s._mark(tok, () if load else (buf,), (buf,) if load else ())
        if not load:
            s.stores[key] = 16 * a0.dcnt

    def emit(s):
        nc = s.nc
        fin = [(k, v) for k, v in s.stores.items() if s.seen["sp"].get(k, 0) < v]
        with nc.Block() as block:
            def run(e, name, final=None):
                for (w, calls, inc) in s.ops[name]:
                    for (k, v) in w:
                        e.wait_ge(s.sems[k], v)
                    ins = None
                    for (nm, a, kw) in calls:
                        ins = getattr(e, nm)(*a, **kw)
                    ins.then_inc(s.sems[inc[0]], inc[1])
                if final:
                    for (k, v) in final:
                        e.wait_ge(s.sems[k], v)

            @block.tensor
            def _(e):
                run(e, "pe")

            @block.scalar
            def _(e):
                run(e, "act")

            @block.vector
            def _(e):
                run(e, "dve")

            @block.gpsimd
            def _(e):
                run(e, "pool")

            @block.sync
            def _(e):
                run(e, "sp", fin)


def _build(debug=False, stop=None, plan=None):
    wreqs = []
    nc = bass.Bass("TRN2", target_bir_lowering=False)
    dbg_out = {}
    with contextlib.ExitStack() as es:
        tk = Trk(nc, es)

        def din(name, shape, dt=F32):
            return nc.dram_tensor(name, list(shape), dt, kind="ExternalInput").ap()

        def dout(name, shape):
            return nc.dram_tensor(name, list(shape), F32, kind="ExternalOutput").ap()

        x_d = din("x", [T, D])
        cdk_d = din("cdk", [DEPTH, 256, 512])
        cdv_d = din("cdv", [DEPTH, 256, 512])
        cwk_d = din("cwk", [DEPTH, 256, 128])
        cwv_d = din("cwv", [DEPTH, 256, 128])
        stin_d = din("stin", [DEPTH, 2, 128, 256])
        TABS = (("ident", 128), ("ropec", 512), ("ropes", 512), ("dbias", 40), ("tsm", 32), ("rel", 512), ("qrow", 256),
                ("bmodc", DEPTH * 48), ("bgatec", DEPTH * 24), ("cvec", 8))
        NTA = sum(n for _, n in TABS)
        tabsA_d = din("tabsA", [128, NTA])
        wmask_d = din("wmask", [128, 2 * 8 * 128])
        PV = (("diff_lam", 256), ("diff_norm_g", 128), ("win_sink", 8), ("ret_decay", 8), ("ret_norm_g", 128))
        NPV = DEPTH * sum(n for _, n in PV)
        pvec_d = din("pvec", [1, NPV])
        bmod_d = din("b_mod", [DEPTH, 6 * D])
        ln_d = {n: din(n, [DEPTH, D]) for n in ("ln1_g", "ln1_b", "ln2_g", "ln2_b")}
        wmod_d = din("w_mod", [DEPTH, D, 6 * D])
        win_d = din("w_in", [DEPTH, D, 3840])
        wgate_d = din("w_gate", [DEPTH, D, 3 * D])
        wp_d = [din(n, [DEPTH, 512, D]) for n in ("w_pa", "w_pb", "w_pc")]
        wo_d = din("w_o", [DEPTH, D, D])
        wff1_d = din("w_ff1", [DEPTH, D, 4 * D])
        wff2_d = din("w_ff2", [DEPTH, 4 * D, D])
        y_d = dout("y", [T, D])
        nk_d = dout("nk", [DEPTH, T, 512])
        nv_d = dout("nv", [DEPTH, T, 512])
        nwk_d = dout("nwk", [DEPTH, T, 128])
        nwv_d = dout("nwv", [DEPTH, T, 128])
        nst_d = dout("nst", [DEPTH, 2, 4, 128, 256])

        cnt = [0]

        def sb(shape, dt, name=None, excl=False):
            cnt[0] += 1
            name = "sb_" + (name or ("b%d" % cnt[0]))
            t = es.enter_context(nc.sbuf_tensor(name, list(shape), dt))
            return Buf(t.ap(), [Atom(name, excl)])

        xb = sb([128, NT, D], F32, "x")
        xa = [Atom("x%d" % i) for i in range(NT)]
        xb.atoms = xa
        xt = [Buf(xb.ap[:, i, :], [xa[i]]) for i in range(NT)]
        hT = sb([128, 8, T], BF16, "hT")
        AR_N = 4096 + 5120 + 5184 + 5120 + 5120 + 12288
        ar_t = es.enter_context(nc.sbuf_tensor("arena", [128, AR_N], BF16))
        ar = ar_t.ap()
        offs = {}
        o = 0
        for nm, sz in (("qT", 4096), ("kT", 5120), ("vTM", 5184), ("E0", 5120), ("E1a", 2560), ("E1b", 2560), ("oT", 12288)):
            offs[nm] = (o, sz)
            o += sz
        A_at = {nm: Atom("ar_" + nm) for nm in offs}

        def arv(nm, a=0, n=None):
            o0, sz = offs[nm]
            n = sz - a if n is None else n
            return ar[:, o0 + a:o0 + a + n]

        qT = Buf(arv("qT").rearrange("p (c t) -> p c t", c=4), [A_at["qT"]])
        kT = Buf(arv("kT").rearrange("p (c t) -> p c t", c=4), [A_at["kT"]])
        vTM = Buf(arv("vTM"), [A_at["vTM"]])
        Eb = [Buf(arv("E0").rearrange("p (j q) -> p j q", j=10), [A_at["E0"]]),
              Buf(ar[:, offs["E1a"][0]:offs["E1a"][0] + 5120].rearrange("p (j q) -> p j q", j=10), [A_at["E1a"], A_at["E1b"]])]
        pslots = [Buf(arv("E1a", 0, 2048), [A_at["E1a"]]), Buf(arv("E1b", 0, 2048), [A_at["E1b"]])]
        oT = Buf(arv("oT").rearrange("p (b c t) -> p b c t", b=3, c=4), [A_at["oT"]])
        mergedT = Buf(ar[:, 0:8192].rearrange("p (c t) -> p c t", c=8), [A_at["qT"], A_at["kT"]])
        macc = Buf(ar[:, 9216:9216 + 8192].bitcast(F32).rearrange("p (c t) -> p c t", c=4),
                   [A_at["vTM"], A_at["E0"]])
        fT = Buf(ar[:, 0:32768].rearrange("p (c t) -> p c t", c=32), [A_at[n] for n in offs])
        rQxf = Buf(arv("E0", 0, 2048).rearrange("p (c t) -> p c t", c=2), [A_at["E0"]])
        rQxb = Buf(arv("E0", 2048, 2048).rearrange("p (c t) -> p c t", c=2), [A_at["E0"]])
        rSall = Buf(ar[:, offs["E1a"][0]:offs["E1a"][0] + 4096].rearrange("p (d n r e) -> p d n r e", d=2, n=8, r=2),
                    [A_at["E1a"], A_at["E1b"]])
        rKzf = Buf(arv("qT", 2048, 2048).rearrange("p (n f) -> p n f", n=8), [A_at["qT"]])
        rKzb = Buf(arv("kT", 2560, 2048).rearrange("p (n f) -> p n f", n=8), [A_at["kT"]])

        wslots = []
        for i in range(NW):
            wslots.append(sb([128, 4096], BF16, "w%d" % i))
        rowtab = sb([128, 4, D], F32, "rowtab")
        rt_at = [Atom("rt%d" % i) for i in range(4)]
        rowtab.atoms = rt_at
        rts = [Buf(rowtab.ap[:, i, :], [rt_at[i]]) for i in range(4)]
        cgs_all = Buf(rowtab.ap[:, 2:4, :].rearrange("p a d -> p (a d)").bitcast(BF16).rearrange("p (n f) -> p n f", n=8),
                      [rt_at[2], rt_at[3]])
        TAB = Atom("TAB")
        tabsA = sb([128, NTA], F32, "tabsA")
        tabsA.atoms = [TAB]
        tviews = {}
        o_ = 0
        for nm, n in TABS:
            tviews[nm] = Buf(tabsA.ap[:, o_:o_ + n], [TAB])
            o_ += n
        identf = tviews["ident"]
        ropec = Buf(tviews["ropec"].ap.rearrange("p (t f) -> p t f", t=8), [TAB])
        ropes = Buf(tviews["ropes"].ap.rearrange("p (t f) -> p t f", t=8), [TAB])
        dbias, tsm, rel, qrow, bmodc, bgatec, cvec = (tviews[k] for k in ("dbias", "tsm", "rel", "qrow", "bmodc", "bgatec", "cvec"))
        PVA = Atom("PVEC")
        pvec = sb([128, NPV], F32, "pvec")
        pvec.atoms = [PVA]
        pviews = {}
        o_ = 0
        for nm, n in PV:
            pviews[nm] = Buf(pvec.ap[:, o_:o_ + DEPTH * n], [PVA])
            o_ += DEPTH * n
        dlam, dng, wsink, rdec, rng = (pviews[k] for k in ("diff_lam", "diff_norm_g", "win_sink", "ret_decay", "ret_norm_g"))
        identb = sb([128, 128], BF16, "identb")
        wmask = sb([128, 2, 8, 128], BF16, "wmaskb")
        silb = sb([128, 8], BF16, "silb")
        silf = sb([128, 8], F32, "silf")
        silrep = sb([128, 8, 128], BF16, "silrep")
        modcol = sb([128, 2, 48], F32, "modcol")
        sm = sb([128, 64], F32, "sm")
        gA = sb([128, 128], F32, "gA")
        sinkexp = sb([128, 8], F32, "sinkexp")
        lgb = sb([128, 8], F32, "lgb")
        lgcol = sb([128, 4], F32, "lgcol")
        gccar = sb([128, 2, 8, 2], F32, "gccar")
        carcol = tsm
        Dsum = sb([128, 4, 128], F32, "Dsum")
        xif = sb([128, 2, 128], F32, "xif")
        xib = sb([128, 2, 128], F32, "xib")
        zet = sb([128, 2, 4], F32, "zet")
        Sfp = [sb([128, 2, 128], F32, "Sfp%d" % i) for i in range(3)]
        stage = [sb([128, 512], F32, "stage%d" % i) for i in range(2)]
        tmpf = [sb([128, 512], F32, "tmpf%d" % i) for i in range(3)]
        tmpb = [sb([128, 512], BF16, "tmpb%d" % i) for i in range(3)]
        smallf = [sb([128, 16], F32, "smallf%d" % i) for i in range(4)]
        otm = [sb([128, 2, 512], BF16, "otm%d" % i) for i in range(2)]
        stage4 = stage + [Buf(o.ap.rearrange("p a b -> p (a b)").bitcast(F32), o.atoms) for o in otm]
        lnstats = [sb([128, 2, 6], F32, "lnstat%d" % i) for i in range(2)]
        lnmv = sb([128, 4], F32, "lnmv")
        lnmv2 = sb([128, 4], F32, "lnmv2")
        adt = [sb([128, 4, 128], BF16, "adt%d" % i) for i in range(2)]

        banks = []
        pairs = []
        for i in range(4):
            t = es.enter_context(nc.psum_tensor("ps%d" % i, [128, 1024], F32))
            a0, a1 = Atom("ps%da" % i, excl=True), Atom("ps%db" % i, excl=True)
            banks.append(Buf(t.ap()[:, 0:512], [a0]))
            banks.append(Buf(t.ap()[:, 512:1024], [a1]))
            pairs.append(Buf(t.ap(), [a0, a1]))
        bctr = [0]

        psm = banks[7]
        bsp = banks[6]

        def nb():
            b = banks[bctr[0] % 6]
            bctr[0] += 1
            return b

        def nb2():
            if bctr[0] % 2:
                bctr[0] += 1
            i = bctr[0] % 6
            bctr[0] += 2
            return banks[i], banks[i + 1], pairs[i // 2]

        rot = {}

        def nxt(lst, key):
            i = rot.get(key, 0)
            rot[key] = i + 1
            return lst[i % len(lst)]

        def V(fn, r, w):
            tk.op("dve", fn, r, w)

        def AC(fn, r, w):
            tk.op("act", fn, r, w)

        def PE(fns, r, w):
            tk.op("pe", lambda e, fs=fns: [f(e) for f in fs][-1], r, w)

        def mm(out, lhsT, rhs, start=True, stop=True):
            return lambda e: e.matmul(out, lhsT=lhsT, rhs=rhs, start=start, stop=stop)

        def tr(out, in_, ident):
            return lambda e: e.transpose(out, in_, ident)

        def load(q, out, in_, buf, after=()):
            tk.dma(q, out, in_, buf, True, after)

        def store(out, in_, buf):
            tk.dma("sp", out, in_, buf, False)

        wctr = [0]

        wdram = {"w_mod": wmod_d, "w_in": win_d, "w_gate": wgate_d, "w_pa": wp_d[0], "w_pb": wp_d[1], "w_pc": wp_d[2],
                 "w_o": wo_d, "w_ff1": wff1_d, "w_ff2": wff2_d}
        wissued = set()

        def w_issue(j, req):
            (dk, l_, r0, kc, c0, cols) = req
            slot = wslots[j % NW]
            view = slot.ap[:, 0:kc * cols].rearrange("p (k c) -> p k c", k=kc)
            src = wdram[dk][l_][r0:r0 + kc * 128, c0:c0 + cols].rearrange("(k p) n -> p k n", p=128)
            load("pool", view, src, slot)

        def W(dk, l_, r0, kc, c0, cols, deep=True):
            i = wctr[0]
            wctr[0] += 1
            req = (dk, l_, r0, kc, c0, cols)
            wreqs.append(req)
            if plan is None:
                w_issue(i, req)
            else:
                assert plan[i] == req
                for j in ((i, i + 1, i + 2) if deep else (i, i + 1)):
                    if j < len(plan) and j not in wissued:
                        wissued.add(j)
                        w_issue(j, plan[j])
            slot = wslots[i % NW]
            return slot, slot.ap[:, 0:kc * cols].rearrange("p (k c) -> p k c", k=kc)

        def dump(name, ap, shape, buf):
            if not debug:
                return
            d = nc.dram_tensor("dbg_" + name, list(shape), ap.dtype, kind="ExternalOutput").ap()
            dbg_out[name] = shape
            store(d, ap, buf)

        load("sp", tabsA.ap, tabsA_d, tabsA)
        load("sp", pvec.ap, pvec_d.partition_broadcast(128), pvec)
        load("pool", wmask.ap.rearrange("p a n q -> p (a n q)"), wmask_d, wmask)
        V(lambda e: e.tensor_copy(out=identb.ap, in_=identf.ap), [identf], [identb])
        AC(lambda e: e.activation(out=silf.ap, in_=cvec.ap, func=AF.Silu), [cvec], [silf])
        V(lambda e: e.tensor_copy(out=silb.ap, in_=silf.ap), [silf], [silb])
        V(lambda e: e.tensor_copy(out=silrep.ap, in_=silf.ap.unsqueeze(2).broadcast_to([128, 8, 128])), [silf], [silrep])

        def bc(ap, shape):
            return ap.broadcast_to(list(shape))

        modq = []

        def mod_tiles(l, vs):
            par = l % 2
            out = []
            for v in vs:
                for half in range(2):
                    if v in (0, 1, 3, 4):
                        def f(v=v, half=half):
                            slot, wv = W("w_mod", l, 0, 8, v * D + half * 512, 512)
                            fns = []
                            for j in range(4):
                                col = v * 8 + half * 4 + j
                                for k in range(8):
                                    fns.append(mm(psm.ap[:, col:col + 1], wv[:, k, j * 128:(j + 1) * 128], silb.ap[:, k:k + 1],
                                                  start=(k == 0), stop=(k == 7)))
                            PE(fns, [slot, silb], [psm])
                            c0 = v * 8 + half * 4
                            V(lambda e: e.tensor_tensor(out=modcol.ap[:, par, c0:c0 + 4], in0=psm.ap[:, c0:c0 + 4],
                                                        in1=bmodc.ap[:, l * 48 + c0:l * 48 + c0 + 4], op=ALU.add), [psm, bmodc], [modcol])
                            if v in (1, 4):
                                V(lambda e: e.tensor_scalar(out=modcol.ap[:, par, c0:c0 + 4], in0=modcol.ap[:, par, c0:c0 + 4],
                                                            scalar1=1.0, scalar2=None, op0=ALU.add), [modcol], [modcol])
                    else:
                        def f(v=v, half=half):
                            rt = rts[0] if v == 2 else rts[1]
                            if half == 0:
                                load("sp", rt.ap, bmod_d[l:l + 1, v * D:(v + 1) * D].partition_broadcast(128), rt)
                            slot, wv = W("w_mod", l, 0, 8, v * D + half * 512, 512)
                            ps = nb()
                            PE([mm(ps.ap, silrep.ap[:, k, :], wv[:, k, :], start=(k == 0), stop=(k == 7)) for k in range(8)],
                               [slot, silrep], [ps])
                            V(lambda e: e.tensor_tensor(out=rt.ap[:, half * 512:(half + 1) * 512], in0=ps.ap,
                                                        in1=rt.ap[:, half * 512:(half + 1) * 512], op=ALU.add), [ps, rt], [rt])
                    out.append(f)
            return out

        def mod_hook():
            if modq:
                modq.pop(0)()

        def mod_drain():
            while modq:
                modq.pop(0)()

        def make_hT(par, scv, shv):
            for tg in range(2):
                for k in range(8):
                    ps = nb()
                    PE([tr(ps.ap[:, j * 128:(j + 1) * 128], xb.ap[:, tg * 4 + j, k * 128:(k + 1) * 128], identf.ap)
                        for j in range(4)], [xt[tg * 4 + j] for j in range(4)] + [identf], [ps])
                    AC(lambda e, ps=ps, k=k, tg=tg: e.activation(
                        out=hT.ap[:, k, tg * 512:(tg + 1) * 512], in_=ps.ap, func=AF.Identity,
                        scale=modcol.ap[:, par, scv * 8 + k:scv * 8 + k + 1], bias=modcol.ap[:, par, shv * 8 + k:shv * 8 + k + 1]),
                       [ps, modcol], [hT])

        def zproj(slot, wv, t, width):
            ps = nb()
            PE([mm(ps.ap[:, 0:width], hT.ap[:, k, t * 128:(t + 1) * 128], wv[:, k, 0:width], start=(k == 0), stop=(k == 7))
                for k in range(8)], [slot, hT], [ps])
            return ps

        def rope(ps, c0, nsub, t, outap, outbuf, dup=False, cp=None):
            n = nsub * 64
            t1 = nxt(tmpf, "tmpf")
            t2 = nxt(tmpf, "tmpf")
            src = ps.ap[:, c0:c0 + n]
            V(lambda e: e.tensor_tensor(out=t1.ap[:, 0:n].rearrange("p (s f) -> p s f", f=64),
                                        in0=src.rearrange("p (s f) -> p s f", f=64),
                                        in1=bc(ropec.ap[:, t:t + 1, :], [128, nsub, 64]), op=ALU.mult),
              [ps, ropec], [t1])
            s5 = src.rearrange("p (s r u i) -> p s r u i", r=2, u=2, i=16)
            o5 = t2.ap[:, 0:n].rearrange("p (s r u i) -> p s r u i", r=2, u=2, i=16)
            sn = ropes.ap[:, t, :].rearrange("p (r u i) -> p r u i", r=2, u=2)
            for u in range(2):
                V(lambda e, u=u: e.tensor_tensor(
                    out=o5[:, :, :, u, :], in0=s5[:, :, :, 1 - u, :],
                    in1=bc(sn[:, :, u, :].unsqueeze(1), [128, nsub, 2, 16]), op=ALU.mult), [ps, ropes], [t2])
            if dup:
                V(lambda e: e.tensor_tensor(
                    out=outap.rearrange("p (s d f) -> p s d f", d=2, f=64),
                    in0=bc(t1.ap[:, 0:n].rearrange("p (s f) -> p s f", f=64).unsqueeze(2), [128, nsub, 2, 64]),
                    in1=bc(t2.ap[:, 0:n].rearrange("p (s f) -> p s f", f=64).unsqueeze(2), [128, nsub, 2, 64]),
                    op=ALU.add), [t1, t2], [outbuf])
            else:
                V(lambda e: e.tensor_tensor(out=outap, in0=t1.ap[:, 0:n], in1=t2.ap[:, 0:n], op=ALU.add),
                  [t1, t2], [outbuf])

        def to_fm(src_buf, src_ap, nchunks, dst_buf, dst_ap, on_dve=False):
            ps = nb()
            pb = ps.ap.bitcast(BF16)
            PE([tr(pb[:, j * 128:(j + 1) * 128], src_ap[:, j * 128:(j + 1) * 128], identb.ap) for j in range(nchunks)],
               [src_buf, identb], [ps])
            if on_dve:
                V(lambda e: e.tensor_copy(out=dst_ap, in_=pb[:, 0:nchunks * 128].rearrange("p (c t) -> p c t", c=nchunks)), [ps], [dst_buf])
            else:
                AC(lambda e: e.activation(out=dst_ap, in_=pb[:, 0:nchunks * 128].rearrange("p (c t) -> p c t", c=nchunks),
                                          func=AF.Copy), [ps], [dst_buf])

        def rope2(ps, c0, nsub, t, dup=False):
            n = nsub * 64
            nd = n * (2 if dup else 1)
            buf = nxt(tmpf, "tmpf")
            bb = buf.ap.bitcast(BF16)
            A = bb[:, 0:nd]
            B = bb[:, 512:512 + nd]
            src = ps.ap[:, c0:c0 + n]
            s3 = src.rearrange("p (s f) -> p s f", f=64)
            s5 = src.rearrange("p (s r u i) -> p s r u i", r=2, u=2, i=16)
            sn = ropes.ap[:, t, :].rearrange("p (r u i) -> p r u i", r=2, u=2)
            for d in range(2 if dup else 1):
                if dup:
                    Ad = A.rearrange("p (s d f) -> p s d f", d=2, f=64)[:, :, d, :]
                    Bd = B.rearrange("p (s d r u i) -> p s d r u i", d=2, r=2, u=2, i=16)[:, :, d]
                else:
                    Ad = A.rearrange("p (s f) -> p s f", f=64)
                    Bd = B.rearrange("p (s r u i) -> p s r u i", r=2, u=2, i=16)
                V(lambda e, Ad=Ad: e.tensor_tensor(out=Ad, in0=s3, in1=bc(ropec.ap[:, t:t + 1, :], [128, nsub, 64]), op=ALU.mult),
                  [ps, ropec], [buf])
                for u in range(2):
                    V(lambda e, u=u, Bd=Bd: e.tensor_tensor(
                        out=Bd[:, :, :, u, :], in0=s5[:, :, :, 1 - u, :],
                        in1=bc(sn[:, :, u, :].unsqueeze(1), [128, nsub, 2, 16]), op=ALU.mult), [ps, ropes], [buf])
            return buf, A, B

        def to_fm2(buf, A, B, nchunks, dst_buf, dst_ap):
            ps = nb()
            fns = []
            for j in range(nchunks):
                o_ = ps.ap[:, j * 128:(j + 1) * 128]
                fns.append(mm(o_, A[:, j * 128:(j + 1) * 128], identb.ap, start=True, stop=False))
                fns.append(mm(o_, B[:, j * 128:(j + 1) * 128], identb.ap, start=False, stop=True))
            PE(fns, [buf, identb], [ps])
            AC(lambda e: e.activation(out=dst_ap, in_=ps.ap[:, 0:nchunks * 128].rearrange("p (c t) -> p c t", c=nchunks), func=AF.Copy),
               [ps], [dst_buf])

        deferred = []
        obuf_ctr = [0]

        def defer(fn):
            deferred.append(fn)

        def flush(keep=0):
            while len(deferred) > keep:
                deferred.pop(0)()

        def rsqrt(out_ap, in_ap, scale, rbufs, wbufs):
            AC(lambda e: e.activation(out=out_ap, in_=in_ap, func=AF.Ln, scale=scale, bias=sm.ap[:, 8:9]), list(rbufs) + [sm], wbufs)
            AC(lambda e: e.activation(out=out_ap, in_=out_ap, func=AF.Exp, scale=-0.5), wbufs, wbufs)

        def out_fp32(ps, c0, n, dram_ap):
            st = nxt(stage4, "stage4")
            AC(lambda e: e.activation(out=st.ap[:, 0:n], in_=ps.ap[:, c0:c0 + n], func=AF.Copy), [ps], [st])
            store(dram_ap, st.ap[:, 0:n], st)
            return st

        def mixer_A(l, lam_init):
            tsl = slice(None)
            lv = dlam.ap[:, l * 256:(l + 1) * 256]
            s = sm
            V(lambda e: e.tensor_tensor(out=tmpf[0].ap[:, 0:64], in0=lv[:, 0:64], in1=lv[:, 64:128], op=ALU.mult), [dlam], [tmpf[0]])
            V(lambda e: e.tensor_reduce(out=s.ap[:, 0:1], in_=tmpf[0].ap[:, 0:64], axis=AX.X, op=ALU.add), [tmpf[0]], [s])
            V(lambda e: e.tensor_tensor(out=tmpf[0].ap[:, 0:64], in0=lv[:, 128:192], in1=lv[:, 192:256], op=ALU.mult), [dlam], [tmpf[0]])
            V(lambda e: e.tensor_reduce(out=s.ap[:, 1:2], in_=tmpf[0].ap[:, 0:64], axis=AX.X, op=ALU.add), [tmpf[0]], [s])
            AC(lambda e: e.activation(out=s.ap[:, 2:4], in_=s.ap[:, 0:2], func=AF.Exp), [s], [s])
            V(lambda e: e.tensor_tensor(out=s.ap[:, 4:5], in0=s.ap[:, 3:4], in1=s.ap[:, 2:3], op=ALU.subtract), [s], [s])
            V(lambda e: e.tensor_scalar(out=s.ap[:, 5:6], in0=s.ap[:, 4:5], scalar1=-lam_init, scalar2=None, op0=ALU.add), [s], [s])
            V(lambda e: e.tensor_scalar(out=gA.ap, in0=dng.ap[:, l * 128:(l + 1) * 128], scalar1=(1.0 - lam_init), scalar2=None,
                                        op0=ALU.mult), [dng], [gA])
            v4 = vTM.ap[:, 0:5160].rearrange("p (j h e) -> p j h e", j=10, h=4)
            V(lambda e: e.memset(v4[:, :, :, 128:129], 1.0), [], [vTM])
            for j in range(2):
                st = nxt(stage, "stage")
                load("sp", st.ap, cdk_d[l, j * 128:(j + 1) * 128, :], st)
                tb = nxt(tmpb, "tmpb")
                V(lambda e, st=st, tb=tb: e.tensor_copy(out=tb.ap, in_=st.ap), [st], [tb])
                to_fm(tb, tb.ap, 4, kT, kT.ap[:, :, 1024 + j * 128:1024 + (j + 1) * 128])
                st2 = nxt(stage, "stage")
                load("sp", st2.ap, cdv_d[l, j * 128:(j + 1) * 128, :], st2)
                V(lambda e, st2=st2, j=j: e.tensor_copy(out=v4[:, 8 + j, :, 0:128],
                                                        in_=st2.ap.rearrange("p (h e) -> p h e", h=4)), [st2], [vTM])
            for gi in range(3):
                mod_hook()
                slot, wv = W("w_in", l, 0, 8, gi * 512, 512)
                for t in range(NT):
                    ps = zproj(slot, wv, t, 512)
                    if gi != 1:
                        flush()
                    if gi == 0:
                        rb, rA, rB = rope2(ps, 0, 8, t)
                        defer(lambda rb=rb, rA=rA, rB=rB, t=t: to_fm2(rb, rA, rB, 4, qT, qT.ap[:, :, t * 128:(t + 1) * 128]))
                    elif gi == 1:
                        out_fp32(ps, 0, 512, nk_d[l, t * 128:(t + 1) * 128, :])
                        rb, rA, rB = rope2(ps, 0, 8, t)
                        flush()
                        defer(lambda rb=rb, rA=rA, rB=rB, t=t: to_fm2(rb, rA, rB, 4, kT, kT.ap[:, :, t * 128:(t + 1) * 128]))
                    else:
                        out_fp32(ps, 0, 512, nv_d[l, t * 128:(t + 1) * 128, :])
                        V(lambda e, ps=ps, t=t: e.tensor_copy(out=v4[:, t, :, 0:128],
                                                              in_=ps.ap.rearrange("p (h e) -> p h e", h=4)), [ps], [vTM])
            flush()
            if debug and l == 0:
                dump("qTa", qT.ap, [128, 4, 1024], qT)
                dump("kTa", kT.ap, [128, 4, 1280], kT)
            if stop == "A0":
                return

            def qk_steps(r, h, E):
                return [lambda jp=jp: qk_step(r, h, E, jp) for jp in range(5)]

            def qk(r, h, E):
                for f in qk_steps(r, h, E):
                    f()

            def qk_step(r, h, E, jp):
                if True:
                    b0, b1, pr = nb2()
                    PE([mm((b0, b1)[m].ap[:, jj * 256:(jj + 1) * 256], kT.ap[m * 64:(m + 1) * 64, h, (2 * jp + jj) * 128:(2 * jp + jj + 1) * 128],
                           qT.ap[m * 64:(m + 1) * 64, h, r * 256:(r + 1) * 256]) for jj in range(2) for m in range(2)], [kT, qT], [b0, b1])
                    AC(lambda e, pr=pr, jp=jp: e.activation(
                        out=E.ap[:, 2 * jp:2 * jp + 2, :].rearrange("p j (m q) -> p m j q", m=2),
                        in_=pr.ap.rearrange("p (m j q) -> p m j q", m=2, j=2),
                        func=AF.Exp, scale=0.125, bias=dbias.ap[:, r * 10 + 2 * jp:r * 10 + 2 * jp + 1]), [pr, dbias], [E])

            def pv_groups(r, h, E):
                pacc = [bsp, psm]

                def grp(m, sblk):
                    PE([mm(pacc[m].ap[:, sblk * 129:(sblk + 1) * 129],
                           E.ap[:, j, m * 256 + sblk * 128:m * 256 + (sblk + 1) * 128],
                           v4[:, j, h, :], start=(j == 0), stop=(j == 9)) for j in range(10)], [E, vTM], [pacc[m]])
                return [lambda m=m, sblk=sblk: grp(m, sblk) for m in range(2) for sblk in range(2)]

            def pv(r, h, E, ot):
                pacc = [bsp, psm]
                sf = nxt(smallf, "smallf")
                pa3 = pacc[0].ap[:, 0:258].rearrange("p (s e) -> p s e", s=2)
                pb3 = pacc[1].ap[:, 0:258].rearrange("p (s e) -> p s e", s=2)
                V(lambda e: e.reciprocal(out=sf.ap[:, 0:2], in_=pa3[:, :, 128]), [pacc[0]], [sf])
                V(lambda e: e.reciprocal(out=sf.ap[:, 2:4], in_=pb3[:, :, 128]), [pacc[1]], [sf])
                V(lambda e: e.tensor_scalar(out=sf.ap[:, 2:4], in0=sf.ap[:, 2:4], scalar1=sm.ap[:, 5:6], scalar2=None, op0=ALU.mult),
                  [sf, sm], [sf])
                o1 = stage[obuf_ctr[0] % 2]
                obuf_ctr[0] += 1
                o2 = nxt(tmpf, "tmpf")
                o13 = o1.ap[:, 0:256].rearrange("p (s e) -> p s e", s=2)
                o23 = o2.ap[:, 0:256].rearrange("p (s e) -> p s e", s=2)
                V(lambda e: e.tensor_tensor(out=o13, in0=pa3[:, :, 0:128], in1=bc(sf.ap[:, 0:2].unsqueeze(2), [128, 2, 128]),
                                            op=ALU.mult), [pacc[0], sf], [o1])
                V(lambda e: e.tensor_tensor(out=o23, in0=pb3[:, :, 0:128], in1=bc(sf.ap[:, 2:4].unsqueeze(2), [128, 2, 128]),
                                            op=ALU.mult), [pacc[1], sf], [o2])
                V(lambda e: e.tensor_tensor(out=o1.ap[:, 0:256], in0=o1.ap[:, 0:256], in1=o2.ap[:, 0:256], op=ALU.add), [o1, o2], [o1])
                V(lambda e: e.tensor_tensor(out=o2.ap[:, 0:256], in0=o1.ap[:, 0:256], in1=o1.ap[:, 0:256], op=ALU.mult), [o1], [o2])
                V(lambda e: e.tensor_reduce(out=sf.ap[:, 4:6], in_=o23, axis=AX.X, op=ALU.add), [o2], [sf])

                def tail():
                    rsqrt(sf.ap[:, 8:10], sf.ap[:, 4:6], 1.0 / 128.0, [sf], [sf])
                    V(lambda e: e.tensor_tensor(out=o13, in0=o13, in1=bc(sf.ap[:, 8:10].unsqueeze(2), [128, 2, 128]), op=ALU.mult),
                      [o1, sf], [o1])
                    V(lambda e: e.tensor_tensor(out=ot.ap[:, :, h * 128:(h + 1) * 128], in0=o13,
                                                in1=bc(gA.ap.unsqueeze(1), [128, 2, 128]), op=ALU.mult), [o1, gA], [ot])
                    if h == 3:
                        for sblk in range(2):
                            qb = r * 2 + sblk
                            defer(lambda sblk=sblk, qb=qb: to_fm(ot, ot.ap[:, sblk, :], 4, oT, oT.ap[:, 0, :, qb * 128:(qb + 1) * 128], on_dve=True))
                return tail

            seq = [(r, h) for r in range(4) for h in range(4)]
            qk(seq[0][0], seq[0][1], Eb[0])
            pend_tail = None
            for i, (r, h) in enumerate(seq):
                qs = qk_steps(seq[i + 1][0], seq[i + 1][1], Eb[(i + 1) % 2]) if i + 1 < len(seq) else []
                pgs = pv_groups(r, h, Eb[i % 2])
                for k_ in range(5):
                    if k_ < len(qs):
                        qs[k_]()
                    if k_ == 2 and pend_tail is not None:
                        pend_tail()
                        pend_tail = None
                    if k_ < 4:
                        pgs[k_]()
                if pend_tail is not None:
                    pend_tail()
                ot = otm[r % 2]
                pend_tail = pv(r, h, Eb[i % 2], ot)
                flush()
            pend_tail()
            flush()
            flush()

        def mixer_B(l):
            v3 = vTM.ap[:, 0:1300].rearrange("p (j g e) -> p j g e", j=10, g=2)
            V(lambda e: e.memset(v3[:, :, :, 64:65], 1.0), [], [vTM])
            AC(lambda e: e.activation(out=sinkexp.ap, in_=wsink.ap[:, l * 8:(l + 1) * 8], func=AF.Exp), [wsink], [sinkexp])
            for j in range(2):
                st = nxt(stage, "stage")
                load("sp", st.ap[:, 0:128], cwk_d[l, j * 128:(j + 1) * 128, :], st)
                tb = nxt(tmpb, "tmpb")
                V(lambda e, st=st, tb=tb: e.tensor_copy(
                    out=tb.ap[:, 0:256].rearrange("p (s d f) -> p s d f", d=2, f=64),
                    in_=bc(st.ap[:, 0:128].rearrange("p (s f) -> p s f", f=64).unsqueeze(2), [128, 2, 2, 64])), [st], [tb])
                to_fm(tb, tb.ap, 2, kT, kT.ap[:, 0:2, 1024 + j * 128:1024 + (j + 1) * 128])
                st2 = nxt(stage, "stage")
                load("sp", st2.ap[:, 0:128], cwv_d[l, j * 128:(j + 1) * 128, :], st2)
                V(lambda e, st2=st2, j=j: e.tensor_copy(out=v3[:, 8 + j, :, 0:64],
                                                        in_=st2.ap[:, 0:128].rearrange("p (g e) -> p g e", g=2)), [st2], [vTM])
            mod_hook()
            slot, wv = W("w_in", l, 0, 8, 1536, 512)
            for t in range(NT):
                ps = zproj(slot, wv, t, 512)
                flush()
                rb, rA, rB = rope2(ps, 0, 8, t)
                defer(lambda rb=rb, rA=rA, rB=rB, t=t: to_fm2(rb, rA, rB, 4, qT, qT.ap[:, :, t * 128:(t + 1) * 128]))
            mod_hook()
            slot, wv = W("w_in", l, 0, 8, 2048, 256)
            for t in range(NT):
                ps = zproj(slot, wv, t, 256)
                out_fp32(ps, 0, 128, nwk_d[l, t * 128:(t + 1) * 128, :])
                out_fp32(ps, 128, 128, nwv_d[l, t * 128:(t + 1) * 128, :])
                rb, rA, rB = rope2(ps, 0, 2, t, dup=True)
                flush()
                defer(lambda rb=rb, rA=rA, rB=rB, t=t: to_fm2(rb, rA, rB, 2, kT, kT.ap[:, 0:2, t * 128:(t + 1) * 128]))
                V(lambda e, ps=ps, t=t: e.tensor_copy(out=v3[:, t, :, 0:64],
                                                      in_=ps.ap[:, 128:256].rearrange("p (g e) -> p g e", g=2)), [ps], [vTM])
            flush()
            if debug and l == 0:
                dump("qTb", qT.ap, [128, 4, 1024], qT)
                dump("kTb", kT.ap, [128, 4, 1280], kT)

            def tiles_of(n):
                tl = []
                if n >= 1:
                    tl.append((n - 1, 0))
                tl.append((n, None))
                if n <= 6:
                    tl.append((n + 1, 1))
                tl.append((8, "c"))
                tl.append((9, "c"))
                return tl

            def groups_of(tl):
                own = [i for i, (ch, kind) in enumerate(tl) if kind != "c"]
                ctx = [i for i, (ch, kind) in enumerate(tl) if kind == "c"]
                return [own[i:i + 2] for i in range(0, len(own), 2)] + [ctx]

            def ecol(tl, ti, hh):
                for grp in groups_of(tl):
                    if ti in grp:
                        ng = len(grp)
                        return grp[0] * 512 + (hh % 2) * ng * 256 + grp.index(ti) * 256 + (hh // 2) * 128
                raise AssertionError

            def qk_steps(n, g, E):
                tl = tiles_of(n)
                Ef = E.ap.rearrange("p j q -> p (j q)")
                own = [i for i, (ch, kind) in enumerate(tl) if kind != "c"]
                ctx = [i for i, (ch, kind) in enumerate(tl) if kind == "c"]
                groups = [own[i:i + 2] for i in range(0, len(own), 2)] + [ctx]
                return [lambda grp=grp: qk_group(n, g, E, tl, Ef, grp) for grp in groups]

            def qk(n, g, E):
                for f in qk_steps(n, g, E):
                    f()

            def qk_group(n, g, E, tl, Ef, grp):
                if True:
                    isctx = tl[grp[0]][1] == "c"
                    ng = len(grp)
                    b0, b1, pr = nb2()
                    fns = []
                    for gi2, ti in enumerate(grp):
                        ch = tl[ti][0]
                        for i2 in range(2):
                            for half in range(2):
                                head = 4 * g + 2 * i2 + half
                                c = head // 2
                                fns.append(mm((b0, b1)[half].ap[:, (gi2 * 2 + i2) * 128:(gi2 * 2 + i2 + 1) * 128],
                                              kT.ap[half * 64:(half + 1) * 64, g, ch * 128:(ch + 1) * 128],
                                              qT.ap[half * 64:(half + 1) * 64, c, n * 128:(n + 1) * 128]))
                    PE(fns, [kT, qT], [b0, b1])
                    eo = Ef[:, grp[0] * 512:grp[0] * 512 + ng * 512].rearrange("p (h x) -> p h x", h=2)
                    pin = pr.ap.rearrange("p (h x) -> p h x", h=2)[:, :, 0:ng * 256]
                    if isctx:
                        AC(lambda e, pin=pin, eo=eo: e.activation(out=eo, in_=pin, func=AF.Exp, scale=0.125, bias=tsm.ap[:, 0:1]),
                           [pr, tsm], [E])
                    else:
                        AC(lambda e, pin=pin, eo=eo: e.activation(out=eo, in_=pin, func=AF.Exp, scale=0.125), [pr], [E])
                    for gi2, ti in enumerate(grp):
                        kind = tl[ti][1]
                        if kind is not None and kind != "c":
                            blk = Ef[:, grp[0] * 512:grp[0] * 512 + ng * 512].rearrange("p (h x) -> p h x", h=2)[
                                :, :, gi2 * 256:(gi2 + 1) * 256].rearrange("p h (i q) -> p h i q", i=2)
                            V(lambda e, blk=blk, kind=kind: e.tensor_tensor(
                                out=blk, in0=blk, in1=bc(wmask.ap[:, kind, n:n + 1, :].unsqueeze(1), [128, 2, 2, 128]), op=ALU.mult),
                              [E, wmask], [E])

            def pv_groups(n, g, E):
                tl = tiles_of(n)

                def grp(h2):
                    fns = []
                    for hh in (2 * h2, 2 * h2 + 1):
                        for ti, (ch, kind) in enumerate(tl):
                            c0 = ecol(tl, ti, hh)
                            fns.append(mm(bsp.ap[:, hh * 65:(hh + 1) * 65], E.ap.rearrange("p j q -> p (j q)")[:, c0:c0 + 128],
                                          v3[:, ch, g, :], start=(ti == 0), stop=(ti == len(tl) - 1)))
                    PE(fns, [E, vTM], [bsp])
                return [lambda h2=h2: grp(h2) for h2 in range(2)]

            def pv(n, g, E, ot):
                tl = tiles_of(n)
                pacc = bsp
                sf = nxt(smallf, "smallf")
                p3 = pacc.ap[:, 0:260].rearrange("p (h e) -> p h e", h=4)
                V(lambda e: e.tensor_tensor(out=sf.ap[:, 0:4], in0=p3[:, :, 64], in1=sinkexp.ap[:, 4 * g:4 * g + 4], op=ALU.add),
                  [pacc, sinkexp], [sf])
                V(lambda e: e.reciprocal(out=sf.ap[:, 4:8], in_=sf.ap[:, 0:4]), [sf], [sf])
                V(lambda e: e.tensor_tensor(out=ot.ap[:, 0, g * 256:(g + 1) * 256].rearrange("p (h e) -> p h e", h=4),
                                            in0=p3[:, :, 0:64], in1=bc(sf.ap[:, 4:8].unsqueeze(2), [128, 4, 64]), op=ALU.mult),
                  [pacc, sf], [ot])

            seq = [(n, g) for n in range(8) for g in range(2)]
            qk(0, 0, Eb[0])
            for i, (n, g) in enumerate(seq):
                qs = qk_steps(seq[i + 1][0], seq[i + 1][1], Eb[(i + 1) % 2]) if i + 1 < len(seq) else []
                pgs = pv_groups(n, g, Eb[i % 2])
                for k_ in range(max(len(qs), 2)):
                    if k_ < len(qs):
                        qs[k_]()
                    if k_ < 2:
                        pgs[k_]()
                ot = otm[n % 2]
                pv(n, g, Eb[i % 2], ot)
                flush()
                if g == 1:
                    defer(lambda ot=ot, n=n: to_fm(ot, ot.ap[:, 0, :], 4, oT, oT.ap[:, 1, :, n * 128:(n + 1) * 128], on_dve=True))
            flush()

        def mixer_C(l):
            rd = rdec.ap[:, l * 8:(l + 1) * 8]
            AC(lambda e: e.activation(out=lgb.ap, in_=rd, func=AF.Exp, scale=-1.0), [rdec], [lgb])
            AC(lambda e: e.activation(out=lgb.ap, in_=lgb.ap, func=AF.Ln, bias=sm.ap[:, 9:10]), [lgb, sm], [lgb])
            V(lambda e: e.tensor_scalar(out=lgb.ap, in0=lgb.ap, scalar1=-1.0, scalar2=None, op0=ALU.mult), [lgb], [lgb])
            lg4 = lgb.ap.rearrange("p (d r m) -> p d r m", d=2, r=2)
            lc3 = lgcol.ap.rearrange("p (d r) -> p d r", d=2)
            V(lambda e: e.tensor_copy(out=lc3[0:64, :, :], in_=lg4[0:64, :, :, 0]), [lgb], [lgcol])
            V(lambda e: e.tensor_copy(out=lc3[64:128, :, :], in_=lg4[64:128, :, :, 1]), [lgb], [lgcol])
            relf = rel.ap[:, 0:128]
            relb = rel.ap[:, 128:256]
            mf = rel.ap[:, 256:384]
            mb = rel.ap[:, 384:512]
            dtb = nxt(tmpf, "tmpf")
            dtv = dtb.ap.rearrange("p (h q) -> p h q", h=4)
            for h in range(4):
                AC(lambda e, h=h: e.activation(out=Dsum.ap[:, h, :], in_=relf, func=AF.Exp, scale=lgb.ap[:, h:h + 1]), [rel, lgb], [Dsum])
                AC(lambda e, h=h: e.activation(out=dtv[:, h, :], in_=relb, func=AF.Exp, scale=lgb.ap[:, 4 + h:5 + h]), [rel, lgb], [dtb])
            V(lambda e: e.tensor_tensor(out=Dsum.ap, in0=Dsum.ap, in1=bc(mf.unsqueeze(1), [128, 4, 128]), op=ALU.mult), [Dsum, rel], [Dsum])
            V(lambda e: e.tensor_tensor(out=dtv, in0=dtv, in1=bc(mb.unsqueeze(1), [128, 4, 128]), op=ALU.mult), [dtb, rel], [dtb])
            V(lambda e: e.tensor_tensor(out=Dsum.ap, in0=Dsum.ap, in1=dtv, op=ALU.add), [Dsum, dtb], [Dsum])
            for r in range(2):
                AC(lambda e, r=r: e.activation(out=xif.ap[:, r, :], in_=qrow.ap[:, 0:128], func=AF.Exp, scale=lgcol.ap[:, r:r + 1]),
                   [qrow, lgcol], [xif])
                AC(lambda e, r=r: e.activation(out=xib.ap[:, r, :], in_=qrow.ap[:, 128:256], func=AF.Exp, scale=lgcol.ap[:, 2 + r:3 + r]),
                   [qrow, lgcol], [xib])
            AC(lambda e: e.activation(out=zet.ap[:, 0, :], in_=lgb.ap[:, 0:4], func=AF.Exp, scale=tsm.ap[:, 17:18]), [lgb, tsm], [zet])
            AC(lambda e: e.activation(out=zet.ap[:, 1, :], in_=lgb.ap[:, 4:8], func=AF.Exp, scale=tsm.ap[:, 18:19]), [lgb, tsm], [zet])
            AC(lambda e: e.activation(out=sm.ap[:, 12:16], in_=lgcol.ap, func=AF.Exp, scale=128.0), [lgcol], [sm])
            car3 = tsm.ap[:, 1:17].rearrange("p (d n) -> p d n", d=2)
            V(lambda e: e.tensor_tensor(out=gccar.ap, in0=bc(sm.ap[:, 12:16].rearrange("p (d r) -> p d r", d=2).unsqueeze(2), [128, 2, 8, 2]),
                                        in1=bc(car3.unsqueeze(3), [128, 2, 8, 2]), op=ALU.mult), [sm, tsm], [gccar])
            mod_hook()
            slot, wv = W("w_in", l, 0, 8, 2304, 512)
            kz = {0: rKzf, 1: rKzb}
            for t in range(NT):
                ps = zproj(slot, wv, t, 512)
                tb = nxt(tmpb, "tmpb")
                AC(lambda e, ps=ps, tb=tb: e.activation(out=tb.ap[:, 0:256], in_=ps.ap[:, 0:256], func=AF.Copy), [ps], [tb])
                AC(lambda e, ps=ps, tb=tb: e.activation(out=tb.ap[:, 256:512], in_=ps.ap[:, 256:512], func=AF.Copy, scale=0.125), [ps], [tb])
                flush()
                defer(lambda tb=tb, t=t: to_fm(tb, tb.ap[:, 0:256], 2, qT, qT.ap[:, 0:2, t * 128:(t + 1) * 128]))
                defer(lambda tb=tb, t=t: to_fm(tb, tb.ap[:, 256:512], 2, kT, kT.ap[:, 0:2, t * 128:(t + 1) * 128]))
                for d in range(2):
                    V(lambda e, tb=tb, t=t, d=d: e.tensor_tensor(
                        out=kz[d].ap[:, t, :].rearrange("p (h f) -> p h f", h=4),
                        in0=tb.ap[:, 256:512].rearrange("p (h f) -> p h f", h=4),
                        in1=bc(zet.ap[:, d, :].unsqueeze(2), [128, 4, 64]), op=ALU.mult), [tb, zet], [kz[d]])
            slot, wv = W("w_in", l, 0, 8, 2816, 512)
            vr = vTM.ap[:, 0:4096].rearrange("p (n f) -> p n f", n=8)
            for t in range(NT):
                ps = zproj(slot, wv, t, 512)
                AC(lambda e, ps=ps, t=t: e.activation(out=vr[:, t, :], in_=ps.ap, func=AF.Copy), [ps], [vTM])
            flush()
            slot_g, wv_g = W("w_in", l, 0, 8, 3328, 512)
            cgq = []
            for t in range(NT):
                def cgf(t=t):
                    psg = zproj(slot_g, wv_g, t, 512)
                    AC(lambda e: e.activation(out=cgs_all.ap[:, t, :], in_=psg.ap, func=AF.Silu), [psg], [cgs_all])
                    V(lambda e: e.tensor_tensor(out=cgs_all.ap[:, t, :].rearrange("p (h e) -> p h e", h=4),
                                                in0=cgs_all.ap[:, t, :].rearrange("p (h e) -> p h e", h=4),
                                                in1=bc(rng.ap[:, l * 128:(l + 1) * 128].unsqueeze(1), [128, 4, 128]), op=ALU.mult),
                      [cgs_all, rng], [cgs_all])
                cgq.append(cgf)
            for (dst, xi) in ((rQxf, xif), (rQxb, xib)):
                for r in range(2):
                    V(lambda e, dst=dst, xi=xi, r=r: e.tensor_tensor(
                        out=dst.ap[:, r, :].rearrange("p (n q) -> p n q", n=8),
                        in0=qT.ap[:, r, :].rearrange("p (n q) -> p n q", n=8),
                        in1=bc(xi.ap[:, r:r + 1, :], [128, 8, 128]), op=ALU.mult), [qT, xi], [dst])
            for d in range(2):
                order = list(range(8)) if d == 0 else list(range(7, -1, -1))
                sprev = nxt(Sfp, "Sfp")
                load("sp", sprev.ap.rearrange("p r e -> p (r e)"), stin_d[l, d], sprev)
                for n in order:
                    V(lambda e, sprev=sprev, d=d, n=n: e.tensor_scalar(
                        out=rSall.ap[:, d, n], in0=sprev.ap, scalar1=tsm.ap[:, 1 + d * 8 + n:2 + d * 8 + n], scalar2=None,
                        op0=ALU.mult), [sprev, tsm], [rSall])
                    ps = nb()
                    fns = []
                    for r in range(2):
                        for m in range(2):
                            hd = 2 * r + m
                            fns.append(mm(ps.ap[m * 64:(m + 1) * 64, r * 128:(r + 1) * 128],
                                          kz[d].ap[:, n, hd * 64:(hd + 1) * 64], vr[:, n, hd * 128:(hd + 1) * 128]))
                    PE(fns, [kz[d], vTM], [ps])
                    snew = nxt(Sfp, "Sfp")
                    V(lambda e, sprev=sprev, snew=snew, d=d, n=n: e.tensor_tensor(
                        out=snew.ap, in0=sprev.ap, in1=bc(gccar.ap[:, d, n, :].unsqueeze(2), [128, 2, 128]), op=ALU.mult),
                      [sprev, gccar], [snew])
                    V(lambda e, snew=snew, ps=ps: e.tensor_tensor(
                        out=snew.ap, in0=snew.ap, in1=ps.ap[:, 0:256].rearrange("p (r e) -> p r e", r=2), op=ALU.add),
                      [snew, ps], [snew])
                    if (d == 0 and n % 2 == 1) or (d == 1 and n % 2 == 0):
                        store(nst_d[l, d, n // 2], snew.ap.rearrange("p r e -> p (r e)"), snew)
                    sprev = snew
                    if n % 2 == 1 and cgq:
                        cgq.pop(0)()
            while cgq:
                cgq.pop(0)()
            def qk_c(n):
                ad = adt[n % 2]
                for m in range(2):
                    psa = nb()
                    PE([mm(psa.ap[:, r * 128:(r + 1) * 128], kT.ap[m * 64:(m + 1) * 64, r, n * 128:(n + 1) * 128],
                           qT.ap[m * 64:(m + 1) * 64, r, n * 128:(n + 1) * 128]) for r in range(2)], [kT, qT], [psa])
                    V(lambda e, psa=psa, m=m: e.tensor_tensor(
                        out=ad.ap.rearrange("p (r t) q -> p r t q", t=2)[:, :, m, :],
                        in0=psa.ap[:, 0:256].rearrange("p (r q) -> p r q", r=2),
                        in1=Dsum.ap.rearrange("p (r t) q -> p r t q", t=2)[:, :, m, :], op=ALU.mult), [psa, Dsum], [ad])

            def pv_c(n):
                ad = adt[n % 2]
                psy = bsp
                fns = []
                for hd in range(4):
                    r, m = hd // 2, hd % 2
                    o_ = psy.ap[:, hd * 128:(hd + 1) * 128]
                    fns.append(mm(o_, ad.ap[:, hd, :], vr[:, n, hd * 128:(hd + 1) * 128], start=True, stop=False))
                    fns.append(mm(o_, rQxf.ap[m * 64:(m + 1) * 64, r, n * 128:(n + 1) * 128], rSall.ap[m * 64:(m + 1) * 64, 0, n, r, :],
                                  start=False, stop=False))
                    fns.append(mm(o_, rQxb.ap[m * 64:(m + 1) * 64, r, n * 128:(n + 1) * 128], rSall.ap[m * 64:(m + 1) * 64, 1, n, r, :],
                                  start=False, stop=True))
                PE(fns, [ad, vTM, rQxf, rQxb, rSall], [psy])
                return psy

            def ln_c(n, psy):
                y3 = psy.ap.rearrange("p (h e) -> p h e", h=4)
                sf = nxt(smallf, "smallf")
                sq = nxt(tmpf, "tmpf")
                t1 = nxt(tmpf, "tmpf")
                V(lambda e: e.tensor_reduce(out=sf.ap[:, 0:4], in_=y3, axis=AX.X, op=ALU.add), [psy], [sf])
                AC(lambda e: e.activation(out=sq.ap, in_=psy.ap, func=AF.Square), [psy], [sq])
                t13 = t1.ap.rearrange("p (h e) -> p h e", h=4)
                V(lambda e: e.tensor_scalar(out=sf.ap[:, 0:4], in0=sf.ap[:, 0:4], scalar1=1.0 / 128.0, scalar2=None, op0=ALU.mult), [sf], [sf])
                V(lambda e: e.tensor_tensor(out=t13, in0=y3, in1=bc(sf.ap[:, 0:4].unsqueeze(2), [128, 4, 128]), op=ALU.subtract),
                  [psy, sf], [t1])
                flush()
                V(lambda e: e.tensor_reduce(out=sf.ap[:, 4:8], in_=sq.ap.rearrange("p (h e) -> p h e", h=4), axis=AX.X, op=ALU.add),
                  [sq], [sf])
                V(lambda e: e.tensor_scalar(out=sf.ap[:, 4:8], in0=sf.ap[:, 4:8], scalar1=1.0 / 128.0, scalar2=None, op0=ALU.mult), [sf], [sf])
                V(lambda e: e.tensor_tensor(out=sf.ap[:, 8:12], in0=sf.ap[:, 0:4], in1=sf.ap[:, 0:4], op=ALU.mult), [sf], [sf])
                V(lambda e: e.tensor_tensor(out=sf.ap[:, 4:8], in0=sf.ap[:, 4:8], in1=sf.ap[:, 8:12], op=ALU.subtract), [sf], [sf])
                rsqrt(sf.ap[:, 12:16], sf.ap[:, 4:8], 1.0, [sf], [sf])
                V(lambda e: e.tensor_tensor(out=t13, in0=t13, in1=bc(sf.ap[:, 12:16].unsqueeze(2), [128, 4, 128]), op=ALU.mult),
                  [t1, sf], [t1])
                ot = otm[n % 2]
                V(lambda e: e.tensor_tensor(out=ot.ap[:, 0, :], in0=t1.ap, in1=cgs_all.ap[:, n, :], op=ALU.mult), [t1, cgs_all], [ot])
                defer(lambda: to_fm(ot, ot.ap[:, 0, :], 4, oT, oT.ap[:, 2, :, n * 128:(n + 1) * 128]))

            mod_drain()
            if l + 1 < DEPTH:
                modq.extend(mod_tiles(l + 1, (0, 1, 3, 4)))
            qk_c(0)
            for n in range(8):
                mod_hook()
                psy = pv_c(n)
                if n + 1 < 8:
                    qk_c(n + 1)
                flush()
                ln_c(n, psy)
            flush()

        lnm = [lnmv, lnmv2]

        def ln_load(l, gname, bname):
            load("sp", rts[2].ap, ln_d[gname][l:l + 1, :].partition_broadcast(128), rts[2])
            load("sp", rts[3].ap, ln_d[bname][l:l + 1, :].partition_broadcast(128), rts[3])

        def ln_stats(t):
            mv = lnm[t % 2]
            st = lnstats[t % 2]
            for hf in range(2):
                V(lambda e, hf=hf: e.bn_stats(out=st.ap[:, hf, :], in_=xb.ap[:, t, hf * 512:(hf + 1) * 512]), [xt[t]], [st])
            V(lambda e: e.bn_aggr(out=mv.ap[:, 0:2], in_=st.ap.rearrange("p a b -> p (a b)")), [st], [mv])
            rsqrt(mv.ap[:, 2:3], mv.ap[:, 1:2], 1.0, [mv], [mv])
            V(lambda e: e.scalar_tensor_tensor(out=mv.ap[:, 3:4], in0=mv.ap[:, 0:1], scalar=-1.0, in1=mv.ap[:, 2:3],
                                               op0=ALU.mult, op1=ALU.mult), [mv], [mv])
            AC(lambda e: e.activation(out=xb.ap[:, t, :], in_=xb.ap[:, t, :], func=AF.Identity, scale=mv.ap[:, 2:3],
                                      bias=mv.ap[:, 3:4]), [xt[t], mv], [xt[t]])

        def ln_affine(t):
            V(lambda e: e.tensor_tensor(out=xb.ap[:, t, :], in0=xb.ap[:, t, :], in1=rts[2].ap, op=ALU.mult), [xt[t], rts[2]], [xt[t]])
            V(lambda e: e.tensor_tensor(out=xb.ap[:, t, :], in0=xb.ap[:, t, :], in1=rts[3].ap, op=ALU.add), [xt[t], rts[3]], [xt[t]])

        def ln_step(t):
            ln_stats(t)
            if t >= 1:
                ln_affine(t - 1)
            if t == NT - 1:
                ln_affine(t)

        def merge_phase(l):
            ln_load(l, "ln1_g", "ln1_b")
            mod_drain()
            modq.extend(mod_tiles(l, (5,)))
            units = [(G, b) for G in range(2) for b in range(3)]

            def p_issue(i):
                G_, b_ = units[i]
                sl = pslots[i % 2]
                load("pool", sl.ap.rearrange("p (k c) -> p k c", k=4),
                     wp_d[b_][l][0:512, G_ * 512:(G_ + 1) * 512].rearrange("(k p) n -> p k n", p=128), sl)

            p_issue(0)
            for ui, (G, b) in enumerate(units):
                if True:
                    mod_hook()
                    gslot, gwv = W("w_gate", l, 0, 8, b * D + G * 512, 512)
                    if ui + 1 < len(units):
                        p_issue(ui + 1)
                    pslot = pslots[ui % 2]
                    pwv = pslot.ap.rearrange("p (k c) -> p k c", k=4)
                    for j in range(4):
                        col = l * 24 + b * 8 + G * 4 + j
                        for hf in range(2):
                            tsel = slice(hf * 512, (hf + 1) * 512)
                            psg = nb()
                            PE([mm(psg.ap, gwv[:, k, j * 128:(j + 1) * 128], hT.ap[:, k, tsel], start=(k == 0), stop=(k == 7))
                                for k in range(8)], [gslot, hT], [psg])
                            gt = nxt(tmpb, "tmpb")
                            AC(lambda e, psg=psg, gt=gt, col=col: e.activation(out=gt.ap, in_=psg.ap, func=AF.Sigmoid,
                                                                               bias=bgatec.ap[:, col:col + 1]), [psg, bgatec], [gt])
                            psp = nb()
                            PE([mm(psp.ap, pwv[:, k, j * 128:(j + 1) * 128], oT.ap[:, b, k, tsel], start=(k == 0), stop=(k == 3))
                                for k in range(4)], [pslot, oT], [psp])
                            if b == 0:
                                V(lambda e, psp=psp, gt=gt, j=j, tsel=tsel: e.tensor_tensor(out=macc.ap[:, j, tsel], in0=psp.ap, in1=gt.ap,
                                                                                           op=ALU.mult), [psp, gt], [macc])
                            else:
                                tmq = nxt(tmpf, "tmpf")
                                V(lambda e, psp=psp, gt=gt, tmq=tmq: e.tensor_tensor(out=tmq.ap, in0=psp.ap, in1=gt.ap, op=ALU.mult),
                                  [psp, gt], [tmq])
                                if b == 1:
                                    V(lambda e, tmq=tmq, j=j, tsel=tsel: e.tensor_tensor(out=macc.ap[:, j, tsel], in0=macc.ap[:, j, tsel],
                                                                                        in1=tmq.ap, op=ALU.add), [macc, tmq], [macc])
                                else:
                                    V(lambda e, tmq=tmq, j=j, tsel=tsel, G=G: e.tensor_tensor(
                                        out=mergedT.ap[:, G * 4 + j, tsel], in0=macc.ap[:, j, tsel], in1=tmq.ap, op=ALU.add),
                                      [macc, tmq], [mergedT])
            if debug and l == 0:
                dump("mergedT", mergedT.ap, [128, 8, 1024], mergedT)
            mod_drain()
            wo_t = [W("w_o", l, 0, 8, cg * 512, 512, deep=(cg == 0)) for cg in range(2)]
            for half in range(2):
                for cg in range(2):
                    slot, wv = wo_t[cg]
                    for t in range(half * 4, half * 4 + 4):
                        ps = nb()
                        PE([mm(ps.ap, mergedT.ap[:, k, t * 128:(t + 1) * 128], wv[:, k, :], start=(k == 0), stop=(k == 7)) for k in range(8)],
                           [slot, mergedT], [ps])
                        tmq = nxt(tmpf, "tmpf")
                        csl = slice(cg * 512, (cg + 1) * 512)
                        V(lambda e, ps=ps, tmq=tmq, csl=csl: e.tensor_tensor(out=tmq.ap, in0=ps.ap, in1=rts[0].ap[:, csl], op=ALU.mult),
                          [ps, rts[0]], [tmq])
                        V(lambda e, tmq=tmq, t=t, csl=csl: e.scalar_tensor_tensor(out=xb.ap[:, t, csl], in0=xb.ap[:, t, csl], scalar=ALPHA,
                                                                                  in1=tmq.ap, op0=ALU.mult, op1=ALU.add), [xt[t], tmq], [xt[t]])
                        if cg == 1:
                            ln_step(t)

        def mlp_phase(l):
            make_hT(l % 2, 4, 3)
            ln_load(l, "ln2_g", "ln2_b")
            if l + 1 < DEPTH:
                modq.extend(mod_tiles(l + 1, (2,)))
            for cgp in range(8):
                slot, wv = W("w_ff1", l, 0, 8, cgp * 512, 512)
                for j in range(4):
                    for hf in range(2):
                        ps = nb()
                        PE([mm(ps.ap, wv[:, k, j * 128:(j + 1) * 128], hT.ap[:, k, hf * 512:(hf + 1) * 512], start=(k == 0), stop=(k == 7))
                            for k in range(8)], [slot, hT], [ps])
                        tmq = nxt(tmpf, "tmpf")
                        AC(lambda e, ps=ps, tmq=tmq: e.activation(out=tmq.ap, in_=ps.ap, func=AF.Relu), [ps], [tmq])
                        V(lambda e, tmq=tmq, c=cgp * 4 + j, hf=hf: e.tensor_tensor(out=fT.ap[:, c, hf * 512:(hf + 1) * 512], in0=tmq.ap,
                                                                                  in1=tmq.ap, op=ALU.mult), [tmq], [fT])
                mod_hook()
            def ff2_pass(slot, wv, cg, hg, tiles, last):
                csl = slice(cg * 512, (cg + 1) * 512)
                for t in tiles:
                    ps = nb()
                    PE([mm(ps.ap, fT.ap[:, hg * 8 + k, t * 128:(t + 1) * 128], wv[:, k, :], start=(k == 0), stop=(k == 7)) for k in range(8)],
                       [slot, fT], [ps])
                    tmq = nxt(tmpf, "tmpf")
                    V(lambda e, ps=ps, tmq=tmq: e.tensor_tensor(out=tmq.ap, in0=ps.ap, in1=rts[1].ap[:, csl], op=ALU.mult),
                      [ps, rts[1]], [tmq])
                    if hg == 0:
                        V(lambda e, tmq=tmq, t=t: e.scalar_tensor_tensor(out=xb.ap[:, t, csl], in0=xb.ap[:, t, csl], scalar=ALPHA,
                                                                         in1=tmq.ap, op0=ALU.mult, op1=ALU.add), [xt[t], tmq], [xt[t]])
                    else:
                        V(lambda e, tmq=tmq, t=t: e.tensor_tensor(out=xb.ap[:, t, csl], in0=xb.ap[:, t, csl], in1=tmq.ap, op=ALU.add),
                          [xt[t], tmq], [xt[t]])
                    if last:
                        ln_step(t)

            for cg in range(2):
                for hg in range(3):
                    mod_hook()
                    slot, wv = W("w_ff2", l, hg * 1024, 8, cg * 512, 512)
                    ff2_pass(slot, wv, cg, hg, range(NT), False)
            mod_hook()
            mod_hook()
            last_t = [W("w_ff2", l, 3 * 1024, 8, cg * 512, 512, deep=(cg == 0)) for cg in range(2)]
            for half in range(2):
                for cg in range(2):
                    ff2_pass(last_t[cg][0], last_t[cg][1], cg, 3, range(half * 4, half * 4 + 4), cg == 1)
            mod_drain()

        V(lambda e: e.memset(modcol.ap, 0.0), [], [modcol])
        V(lambda e: e.memset(sm.ap, 0.0), [], [sm])
        V(lambda e: e.memset(sm.ap[:, 8:9], LN_EPS), [sm], [sm])
        V(lambda e: e.memset(sm.ap[:, 9:10], 1.0), [sm], [sm])
        import math
        def mark(name):
            PHASES.append((name, sum(len(c) for (_, c, _) in tk.ops["pe"])))

        def run_layer(l):
            lam_init = 0.8 - 0.6 * math.exp(-0.3 * l)
            mark("L%d mod" % l)
            if l == 0:
                for fi, f in enumerate(mod_tiles(0, (0, 1))):
                    f()
                    if fi == 0:
                        for i in range(NT):
                            load("act", xt[i].ap, x_d[i * 128:(i + 1) * 128, :], xt[i], after=[wslots[0]])
                modq.extend(mod_tiles(0, (3, 4, 2)))
            if stop == "mod":
                mod_drain()
                dump("modcol", modcol.ap[:, 0, :], [128, 48], modcol)
                dump("g1", rts[0].ap, [128, 1024], rts[0])
                return False
            mark("L%d hT" % l)
            make_hT(l % 2, 1, 0)
            if debug and l == 0:
                dump("hT", hT.ap, [128, 8, 1024], hT)
            if stop == "hT":
                return False
            mark("L%d A" % l)
            mixer_A(l, lam_init)
            if stop == "A0":
                return False
            if stop == "A":
                dump("oTa", oT.ap[:, 0], [128, 4, 1024], oT)
                return False
            mark("L%d B" % l)
            mixer_B(l)
            if stop == "B":
                dump("oTb", oT.ap[:, 1], [128, 4, 1024], oT)
                return False
            mark("L%d C" % l)
            mixer_C(l)
            if stop == "C":
                dump("oT", oT.ap, [128, 3, 4, 1024], oT)
                return False
            mark("L%d merge" % l)
            merge_phase(l)
            if debug and l == 0:
                dump("x1", xb.ap, [128, 8, 1024], xb)
            if stop == "merge":
                return False
            mark("L%d mlp" % l)
            mlp_phase(l)
            if debug and l == 0:
                dump("x2", xb.ap, [128, 8, 1024], xb)
            return True

        for l in range(DEPTH):
            if not run_layer(l):
                break
        mark("end")
        for i in range(NT):
            store(y_d[i * 128:(i + 1) * 128, :], xt[i].ap, xt[i])
        if plan is not None:
            tk.emit()
    return nc, dbg_out, wreqs


def build_program(debug=False, stop=None):
    _, _, reqs = _build(debug, stop, None)
    PHASES.clear()
    nc, dbg, _ = _build(debug, stop, reqs)
    return nc, dbg


def _const_tables(role):
    f = np.float32
    tabs = {}
    tabs["ident"] = np.eye(128, dtype=f)
    p = np.arange(128)
    tt = (np.arange(8)[None, :] * 128 + p[:, None])
    if role == "sample":
        row = (tt // 64).astype(f)
        col = (tt % 64).astype(f)
        inv = (np.float32(10000.0) ** (-(np.arange(16, dtype=f)) / np.float32(16))).astype(f)
        ang_r = (row[..., None] * inv).astype(f)
        ang_c = (col[..., None] * inv).astype(f)
        cos = np.zeros((128, 8, 64), f)
        sin = np.zeros((128, 8, 64), f)
        for rc, ang in enumerate((ang_r, ang_c)):
            for u in range(2):
                sl = slice(rc * 32 + u * 16, rc * 32 + u * 16 + 16)
                cos[:, :, sl] = np.cos(ang)
                sin[:, :, sl] = (-np.sin(ang)) if u == 0 else np.sin(ang)
        tabs["ropec"], tabs["ropes"] = cos, sin
    else:
        tabs["ropec"] = np.ones((128, 8, 64), f)
        tabs["ropes"] = np.zeros((128, 8, 64), f)
    db = np.zeros((128, 40), f)
    if role == "prompt":
        for r in range(4):
            for j in range(10):
                if j not in (2 * r, 2 * r + 1):
                    db[:, r * 10 + j] = NEG
    tabs["dbias"] = db
    wm = np.zeros((128, 2, 8, 128), f)
    kk = p[:, None]
    qq = p[None, :]
    for n in range(8):
        if role == "sample":
            if n >= 1:
                wm[:, 0, n, :] = (kk >= qq)
            if n <= 6:
                wm[:, 1, n, :] = (kk <= qq)
        else:
            if n % 2 == 1:
                wm[:, 0, n, :] = 1.0
            else:
                wm[:, 1, n, :] = 1.0
    tabs["wmask"] = wm.reshape(128, -1)
    tsm = np.zeros((128, 32), f)
    tsm[:, 0] = 0.0 if role == "sample" else NEG
    for n in range(8):
        if role == "sample":
            tsm[:, 1 + n] = 1.0
            tsm[:, 9 + n] = 1.0
        else:
            tsm[:, 1 + n] = 1.0 if n % 2 == 1 else 0.0
            tsm[:, 9 + n] = 1.0 if n % 2 == 0 else 0.0
    tsm[:, 17] = 127.0 - p
    tsm[:, 18] = p
    tabs["tsm"] = tsm
    rel = np.zeros((128, 4, 128), f)
    rel[:, 0] = np.maximum(qq - kk, 0)
    rel[:, 1] = np.maximum(kk - qq, 0)
    rel[:, 2] = (qq >= kk)
    rel[:, 3] = (kk >= qq)
    tabs["rel"] = rel.reshape(128, -1)
    qr = np.zeros((128, 2, 128), f)
    qr[:, 0] = (qq + 1)
    qr[:, 1] = (128 - qq)
    tabs["qrow"] = qr.reshape(128, -1)
    return tabs


_PROG = {}


def _get_prog(debug=False):
    if debug not in _PROG:
        _PROG[debug] = build_program(debug)
    return _PROG[debug]


def make_in_maps(inputs):
    f = np.float32
    g = {k: np.ascontiguousarray(np.asarray(v, dtype=f)) for k, v in inputs.items()}
    shared = {}
    for n in ("w_mod", "w_in", "w_gate", "w_pa", "w_pb", "w_pc", "w_o", "w_ff1", "w_ff2", "b_mod",
              "ln1_g", "ln1_b", "ln2_g", "ln2_b"):
        shared[n] = g[n]
    bmodc = np.ascontiguousarray(g["b_mod"].reshape(DEPTH, 48, 128).transpose(2, 0, 1).reshape(128, DEPTH * 48))
    bgatec = np.ascontiguousarray(g["b_gate"].reshape(DEPTH, 24, 128).transpose(2, 0, 1).reshape(128, DEPTH * 24))
    shared["pvec"] = np.concatenate([g[n].reshape(-1) for n in ("diff_lam", "diff_norm_g", "win_sink", "ret_decay", "ret_norm_g")]).reshape(1, -1)
    ctab = {r: _const_tables(r) for r in ("prompt", "sample")}
    maps = []
    for core in range(8):
        m = dict(shared)
        if core < 4:
            m["x"] = g["x_prompt"][4 * core:4 * core + 4].reshape(T, D)
            cv = g["c_ctx"]
            m["cdk"] = np.zeros((DEPTH, 256, 512), f)
            m["cdv"] = np.zeros((DEPTH, 256, 512), f)
            m["cwk"] = np.zeros((DEPTH, 256, 128), f)
            m["cwv"] = np.zeros((DEPTH, 256, 128), f)
            m["stin"] = np.zeros((DEPTH, 2, 128, 256), f)
        else:
            b = core - 4
            m["x"] = g["x_sample"][b]
            cv = g["c"][b]
            m["cdk"] = g["cache_diff_k"][b].reshape(DEPTH, 256, 512)
            m["cdv"] = g["cache_diff_v"][b].reshape(DEPTH, 256, 512)
            m["cwk"] = g["cache_win_k"][b].reshape(DEPTH, 256, 128)
            m["cwv"] = g["cache_win_v"][b].reshape(DEPTH, 256, 128)
            s = g["state_ret"][b].reshape(DEPTH, 2, 2, 2, 64, 128).transpose(0, 1, 3, 4, 2, 5)
            m["stin"] = np.ascontiguousarray(s.reshape(DEPTH, 2, 128, 256))
        cvec = np.ascontiguousarray(cv.reshape(8, 128).T)
        ct = ctab["prompt" if core < 4 else "sample"]
        parts = {"ident": ct["ident"], "ropec": ct["ropec"].reshape(128, -1), "ropes": ct["ropes"].reshape(128, -1), "dbias": ct["dbias"],
                 "tsm": ct["tsm"], "rel": ct["rel"], "qrow": ct["qrow"], "bmodc": bmodc, "bgatec": bgatec, "cvec": cvec}
        m["tabsA"] = np.concatenate([parts[k] for k in ("ident", "ropec", "ropes", "dbias", "tsm", "rel", "qrow", "bmodc", "bgatec", "cvec")], axis=1)
        m["wmask"] = ct["wmask"]
        maps.append({k: np.ascontiguousarray(v, dtype=np.float32) for k, v in m.items()})
    return maps


def assemble(results):
    f = np.float32
    y_prompt = np.concatenate([results[c]["y"].reshape(4, 256, D) for c in range(4)], axis=0)
    y_sample = np.stack([results[4 + b]["y"] for b in range(4)], axis=0)

    def gath(name, tail):
        parts = []
        for c in range(4):
            a = results[c][name].reshape(DEPTH, 4, 256, *tail).transpose(1, 0, 2, *range(3, 3 + len(tail)))
            parts.append(a)
        return np.ascontiguousarray(np.concatenate(parts, axis=0).astype(f))

    ndk = gath("nk", (4, 128))
    ndv = gath("nv", (4, 128))
    nwk = gath("nwk", (2, 64))
    nwv = gath("nwv", (2, 64))
    st = []
    for c in range(4):
        a = results[c]["nst"].reshape(DEPTH, 2, 4, 2, 64, 2, 128)
        a = a.transpose(2, 0, 1, 5, 3, 4, 6).reshape(4, DEPTH, 2, 4, 64, 128)
        st.append(a)
    nst = np.ascontiguousarray(np.concatenate(st, axis=0).astype(f))
    return (y_prompt.astype(f), y_sample.astype(f), ndk, ndv, nwk, nwv, nst)


def kernel(**inputs):
    nc, _ = _get_prog(False)
    maps = make_in_maps(inputs)
    res = run_bass_kernel_spmd(nc, maps, core_ids=list(range(8)))
    return assemble(res.results)
```

```python
import contextlib
import numpy as np
import concourse.bass as bass
import concourse.mybir as mybir
from concourse.bass_utils import run_bass_kernel_spmd

F32 = mybir.dt.float32
BF16 = mybir.dt.bfloat16
AF = mybir.ActivationFunctionType
ALU = mybir.AluOpType
AX = mybir.AxisListType

PHASES = []
DEPTH = 2
T = 1024
NT = 8
D = 1024
LN_EPS = 1e-5
ALPHA = (2 * DEPTH) ** 0.25
NEG = -30000.0
NW = 3


class Atom:
    def __init__(s, name, excl=False):
        s.name = name
        s.lw = None
        s.rd = []
        s.excl = excl
        s.dcnt = 0


class Buf:
    def __init__(s, ap, atoms):
        s.ap = ap
        s.atoms = atoms

    def __getitem__(s, idx):
        return s.ap[idx]


class Rec:
    def __init__(s):
        s.calls = []

    def __getattr__(s, name):
        def f(*a, **k):
            s.calls.append((name, a, k))
            return s
        return f


class Trk:
    def __init__(s, nc, es):
        s.nc = nc
        s.es = es
        s.ops = {k: [] for k in ("pe", "act", "dve", "pool", "sp")}
        s.cnt = {k: 0 for k in s.ops}
        s.seen = {k: {} for k in s.ops}
        s.sems = {}
        s.stores = {}
        for k in ("pe", "act", "dve", "pool"):
            s.sems[k] = es.enter_context(nc.semaphore("s_" + k))

    def dsem(s, atom):
        key = "d_" + atom.name
        if key not in s.sems:
            s.sems[key] = s.es.enter_context(s.nc.semaphore(key))
        return key

    def _waits(s, eng, reads, writes):
        deps = []
        for b in reads:
            for a in b.atoms:
                if a.lw is not None:
                    deps.append(a.lw)
                if a.excl:
                    deps.extend(a.rd)
        for b in writes:
            for a in b.atoms:
                if a.lw is not None:
                    deps.append(a.lw)
                deps.extend(a.rd)
        best = {}
        for (k, v) in deps:
            if k == "pe" and eng == "pe":
                continue
            if v > best.get(k, 0):
                best[k] = v
        out = []
        for k, v in best.items():
            if s.seen[eng].get(k, 0) >= v:
                continue
            s.seen[eng][k] = v
            out.append((k, v))
        return out

    def _mark(s, tok, reads, writes):
        for b in reads:
            for a in b.atoms:
                if a.excl:
                    a.lw = tok
                    a.rd = []
                else:
                    a.rd.append(tok)
        for b in writes:
            for a in b.atoms:
                a.lw = tok
                a.rd = []

    def op(s, eng, fn, reads=(), writes=()):
        w = s._waits(eng, reads, writes)
        s.cnt[eng] += 1
        tok = (eng, s.cnt[eng])
        rec = Rec()
        fn(rec)
        s.ops[eng].append((w, rec.calls, (eng, 1)))
        s._mark(tok, reads, writes)

    def dma(s, q, out, in_, buf, load, after=()):
        a0 = buf.atoms[0]
        key = s.dsem(a0)
        w = s._waits(q, tuple(after) if load else (buf,), (buf,) if load else ())
        a0.dcnt += 1
        tok = (key, 16 * a0.dcnt)
        s.ops[q].append((w, [("dma_start", (), dict(out=out, in_=in_))], (key, 16)))
        s._mark(tok, () if load else (buf,), (buf,) if load else ())
        if not load:
            s.stores[key] = 16 * a0.dcnt

    def emit(s):
        nc = s.nc
        fin = [(k, v) for k, v in s.stores.items() if s.seen["sp"].get(k, 0) < v]
        with nc.Block() as block:
            def run(e, name, final=None):
                for (w, calls, inc) in s.ops[name]:
                    for (k, v) in w:
                        e.wait_ge(s.sems[k], v)
                    ins = None
                    for (nm, a, kw) in calls:
                        ins = getattr(e, nm)(*a, **kw)
                    ins.then_inc(s.sems[inc[0]], inc[1])
                if final:
                    for (k, v) in final:
                        e.wait_ge(s.sems[k], v)

            @block.tensor
            def _(e):
                run(e, "pe")

            @block.scalar
            def _(e):
                run(e, "act")

            @block.vector
            def _(e):
                run(e, "dve")

            @block.gpsimd
            def _(e):
                run(e, "pool")

            @block.sync
            def _(e):
                run(e, "sp", fin)


def _build(debug=False, stop=None, plan=None):
    wreqs = []
    nc = bass.Bass("TRN2", target_bir_lowering=False)
    dbg_out = {}
    with contextlib.ExitStack() as es:
        tk = Trk(nc, es)

        def din(name, shape, dt=F32):
            return nc.dram_tensor(name, list(shape), dt, kind="ExternalInput").ap()

        def dout(name, shape):
            return nc.dram_tensor(name, list(shape), F32, kind="ExternalOutput").ap()

        x_d = din("x", [T, D])
        cdk_d = din("cdk", [DEPTH, 256, 512])
        cdv_d = din("cdv", [DEPTH, 256, 512])
        cwk_d = din("cwk", [DEPTH, 256, 128])
        cwv_d = din("cwv", [DEPTH, 256, 128])
        stin_d = din("stin", [DEPTH, 2, 128, 256])
        TABS = (("ident", 128), ("ropec", 512), ("ropes", 512), ("dbias", 40), ("tsm", 32), ("rel", 512), ("qrow", 256),
                ("bmodc", DEPTH * 48), ("bgatec", DEPTH * 24), ("cvec", 8))
        NTA = sum(n for _, n in TABS)
        tabsA_d = din("tabsA", [128, NTA])
        wmask_d = din("wmask", [128, 2 * 8 * 128])
        PV = (("diff_lam", 256), ("diff_norm_g", 128), ("win_sink", 8), ("ret_decay", 8), ("ret_norm_g", 128))
        NPV = DEPTH * sum(n for _, n in PV)
        pvec_d = din("pvec", [1, NPV])
        bmod_d = din("b_mod", [DEPTH, 6 * D])
        ln_d = {n: din(n, [DEPTH, D]) for n in ("ln1_g", "ln1_b", "ln2_g", "ln2_b")}
        wmod_d = din("w_mod", [DEPTH, D, 6 * D])
        win_d = din("w_in", [DEPTH, D, 3840])
        wgate_d = din("w_gate", [DEPTH, D, 3 * D])
        wp_d = [din(n, [DEPTH, 512, D]) for n in ("w_pa", "w_pb", "w_pc")]
        wo_d = din("w_o", [DEPTH, D, D])
        wff1_d = din("w_ff1", [DEPTH, D, 4 * D])
        wff2_d = din("w_ff2", [DEPTH, 4 * D, D])
        y_d = dout("y", [T, D])
        nk_d = dout("nk", [DEPTH, T, 512])
        nv_d = dout("nv", [DEPTH, T, 512])
        nwk_d = dout("nwk", [DEPTH, T, 128])
        nwv_d = dout("nwv", [DEPTH, T, 128])
        nst_d = dout("nst", [DEPTH, 2, 4, 128, 256])

        cnt = [0]

        def sb(shape, dt, name=None, excl=False):
            cnt[0] += 1
            name = "sb_" + (name or ("b%d" % cnt[0]))
            t = es.enter_context(nc.sbuf_tensor(name, list(shape), dt))
            return Buf(t.ap(), [Atom(name, excl)])

        xb = sb([128, NT, D], F32, "x")
        xa = [Atom("x%d" % i) for i in range(NT)]
        xb.atoms = xa
        xt = [Buf(xb.ap[:, i, :], [xa[i]]) for i in range(NT)]
        hT = sb([128, 8, T], BF16, "hT")
        AR_N = 4096 + 5120 + 5184 + 5120 + 5120 + 12288
        ar_t = es.enter_context(nc.sbuf_tensor("arena", [128, AR_N], BF16))
        ar = ar_t.ap()
        offs = {}
        o = 0
        for nm, sz in (("qT", 4096), ("kT", 5120), ("vTM", 5184), ("E0", 5120), ("E1a", 2560), ("E1b", 2560), ("oT", 12288)):
            offs[nm] = (o, sz)
            o += sz
        A_at = {nm: Atom("ar_" + nm) for nm in offs}

        def arv(nm, a=0, n=None):
            o0, sz = offs[nm]
            n = sz - a if n is None else n
            return ar[:, o0 + a:o0 + a + n]

        qT = Buf(arv("qT").rearrange("p (c t) -> p c t", c=4), [A_at["qT"]])
        kT = Buf(arv("kT").rearrange("p (c t) -> p c t", c=4), [A_at["kT"]])
        vTM = Buf(arv("vTM"), [A_at["vTM"]])
        Eb = [Buf(arv("E0").rearrange("p (j q) -> p j q", j=10), [A_at["E0"]]),
              Buf(ar[:, offs["E1a"][0]:offs["E1a"][0] + 5120].rearrange("p (j q) -> p j q", j=10), [A_at["E1a"], A_at["E1b"]])]
        pslots = [Buf(arv("E1a", 0, 2048), [A_at["E1a"]]), Buf(arv("E1b", 0, 2048), [A_at["E1b"]])]
        oT = Buf(arv("oT").rearrange("p (b c t) -> p b c t", b=3, c=4), [A_at["oT"]])
        mergedT = Buf(ar[:, 0:8192].rearrange("p (c t) -> p c t", c=8), [A_at["qT"], A_at["kT"]])
        macc = Buf(ar[:, 9216:9216 + 8192].bitcast(F32).rearrange("p (c t) -> p c t", c=4),
                   [A_at["vTM"], A_at["E0"]])
        fT = Buf(ar[:, 0:32768].rearrange("p (c t) -> p c t", c=32), [A_at[n] for n in offs])
        rQxf = Buf(arv("E0", 0, 2048).rearrange("p (c t) -> p c t", c=2), [A_at["E0"]])
        rQxb = Buf(arv("E0", 2048, 2048).rearrange("p (c t) -> p c t", c=2), [A_at["E0"]])
        rSall = Buf(ar[:, offs["E1a"][0]:offs["E1a"][0] + 4096].rearrange("p (d n r e) -> p d n r e", d=2, n=8, r=2),
                    [A_at["E1a"], A_at["E1b"]])
        rKzf = Buf(arv("qT", 2048, 2048).rearrange("p (n f) -> p n f", n=8), [A_at["qT"]])
        rKzb = Buf(arv("kT", 2560, 2048).rearrange("p (n f) -> p n f", n=8), [A_at["kT"]])

        wslots = []
        for i in range(NW):
            wslots.append(sb([128, 4096], BF16, "w%d" % i))
        rowtab = sb([128, 4, D], F32, "rowtab")
        rt_at = [Atom("rt%d" % i) for i in range(4)]
        rowtab.atoms = rt_at
        rts = [Buf(rowtab.ap[:, i, :], [rt_at[i]]) for i in range(4)]
        cgs_all = Buf(rowtab.ap[:, 2:4, :].rearrange("p a d -> p (a d)").bitcast(BF16).rearrange("p (n f) -> p n f", n=8),
                      [rt_at[2], rt_at[3]])
        TAB = Atom("TAB")
        tabsA = sb([128, NTA], F32, "tabsA")
        tabsA.atoms = [TAB]
        tviews = {}
        o_ = 0
        for nm, n in TABS:
            tviews[nm] = Buf(tabsA.ap[:, o_:o_ + n], [TAB])
            o_ += n
        identf = tviews["ident"]
        ropec = Buf(tviews["ropec"].ap.rearrange("p (t f) -> p t f", t=8), [TAB])
        ropes = Buf(tviews["ropes"].ap.rearrange("p (t f) -> p t f", t=8), [TAB])
        dbias, tsm, rel, qrow, bmodc, bgatec, cvec = (tviews[k] for k in ("dbias", "tsm", "rel", "qrow", "bmodc", "bgatec", "cvec"))
        PVA = Atom("PVEC")
        pvec = sb([128, NPV], F32, "pvec")
        pvec.atoms = [PVA]
        pviews = {}
        o_ = 0
        for nm, n in PV:
            pviews[nm] = Buf(pvec.ap[:, o_:o_ + DEPTH * n], [PVA])
            o_ += DEPTH * n
        dlam, dng, wsink, rdec, rng = (pviews[k] for k in ("diff_lam", "diff_norm_g", "win_sink", "ret_decay", "ret_norm_g"))
        identb = sb([128, 128], BF16, "identb")
        wmask = sb([128, 2, 8, 128], BF16, "wmaskb")
        silb = sb([128, 8], BF16, "silb")
        silf = sb([128, 8], F32, "silf")
        silrep = sb([128, 8, 128], BF16, "silrep")
        modcol = sb([128, 2, 48], F32, "modcol")
        sm = sb([128, 64], F32, "sm")
        gA = sb([128, 128], F32, "gA")
        sinkexp = sb([128, 8], F32, "sinkexp")
        lgb = sb([128, 8], F32, "lgb")
        lgcol = sb([128, 4], F32, "lgcol")
        gccar = sb([128, 2, 8, 2], F32, "gccar")
        carcol = tsm
        Dsum = sb([128, 4, 128], F32, "Dsum")
        xif = sb([128, 2, 128], F32, "xif")
        xib = sb([128, 2, 128], F32, "xib")
        zet = sb([128, 2, 4], F32, "zet")
        Sfp = [sb([128, 2, 128], F32, "Sfp%d" % i) for i in range(3)]
        stage = [sb([128, 512], F32, "stage%d" % i) for i in range(2)]
        tmpf = [sb([128, 512], F32, "tmpf%d" % i) for i in range(3)]
        tmpb = [sb([128, 512], BF16, "tmpb%d" % i) for i in range(3)]
        smallf = [sb([128, 16], F32, "smallf%d" % i) for i in range(4)]
        otm = [sb([128, 2, 512], BF16, "otm%d" % i) for i in range(2)]
        stage4 = stage + [Buf(o.ap.rearrange("p a b -> p (a b)").bitcast(F32), o.atoms) for o in otm]
        lnstats = [sb([128, 2, 6], F32, "lnstat%d" % i) for i in range(2)]
        lnmv = sb([128, 4], F32, "lnmv")
        lnmv2 = sb([128, 4], F32, "lnmv2")
        adt = [sb([128, 4, 128], BF16, "adt%d" % i) for i in range(2)]

        banks = []
        pairs = []
        for i in range(4):
            t = es.enter_context(nc.psum_tensor("ps%d" % i, [128, 1024], F32))
            a0, a1 = Atom("ps%da" % i, excl=True), Atom("ps%db" % i, excl=True)
            banks.append(Buf(t.ap()[:, 0:512], [a0]))
            banks.append(Buf(t.ap()[:, 512:1024], [a1]))
            pairs.append(Buf(t.ap(), [a0, a1]))
        bctr = [0]

        psm = banks[7]
        bsp = banks[6]

        def nb():
            b = banks[bctr[0] % 6]
            bctr[0] += 1
            return b

        def nb2():
            if bctr[0] % 2:
                bctr[0] += 1
            i = bctr[0] % 6
            bctr[0] += 2
            return banks[i], banks[i + 1], pairs[i // 2]

        rot = {}

        def nxt(lst, key):
            i = rot.get(key, 0)
            rot[key] = i + 1
            return lst[i % len(lst)]

        def V(fn, r, w):
            tk.op("dve", fn, r, w)

        def AC(fn, r, w):
            tk.op("act", fn, r, w)

        def PE(fns, r, w):
            tk.op("pe", lambda e, fs=fns: [f(e) for f in fs][-1], r, w)

        def mm(out, lhsT, rhs, start=True, stop=True):
            return lambda e: e.matmul(out, lhsT=lhsT, rhs=rhs, start=start, stop=stop)

        def tr(out, in_, ident):
            return lambda e: e.transpose(out, in_, ident)

        def load(q, out, in_, buf, after=()):
            tk.dma(q, out, in_, buf, True, after)

        def store(out, in_, buf):
            tk.dma("sp", out, in_, buf, False)

        wctr = [0]

        wdram = {"w_mod": wmod_d, "w_in": win_d, "w_gate": wgate_d, "w_pa": wp_d[0], "w_pb": wp_d[1], "w_pc": wp_d[2],
                 "w_o": wo_d, "w_ff1": wff1_d, "w_ff2": wff2_d}
        wissued = set()

        def w_issue(j, req):
            (dk, l_, r0, kc, c0, cols) = req
            slot = wslots[j % NW]
            view = slot.ap[:, 0:kc * cols].rearrange("p (k c) -> p k c", k=kc)
            src = wdram[dk][l_][r0:r0 + kc * 128, c0:c0 + cols].rearrange("(k p) n -> p k n", p=128)
            load("pool", view, src, slot)

        def W(dk, l_, r0, kc, c0, cols, deep=True):
            i = wctr[0]
            wctr[0] += 1
            req = (dk, l_, r0, kc, c0, cols)
            wreqs.append(req)
            if plan is None:
                w_issue(i, req)
            else:
                assert plan[i] == req
                for j in ((i, i + 1, i + 2) if deep else (i, i + 1)):
                    if j < len(plan) and j not in wissued:
                        wissued.add(j)
                        w_issue(j, plan[j])
            slot = wslots[i % NW]
            return slot, slot.ap[:, 0:kc * cols].rearrange("p (k c) -> p k c", k=kc)

        def dump(name, ap, shape, buf):
            if not debug:
                return
            d = nc.dram_tensor("dbg_" + name, list(shape), ap.dtype, kind="ExternalOutput").ap()
            dbg_out[name] = shape
            store(d, ap, buf)

        load("sp", tabsA.ap, tabsA_d, tabsA)
        load("sp", pvec.ap, pvec_d.partition_broadcast(128), pvec)
        load("pool", wmask.ap.rearrange("p a n q -> p (a n q)"), wmask_d, wmask)
        V(lambda e: e.tensor_copy(out=identb.ap, in_=identf.ap), [identf], [identb])
        AC(lambda e: e.activation(out=silf.ap, in_=cvec.ap, func=AF.Silu), [cvec], [silf])
        V(lambda e: e.tensor_copy(out=silb.ap, in_=silf.ap), [silf], [silb])
        V(lambda e: e.tensor_copy(out=silrep.ap, in_=silf.ap.unsqueeze(2).broadcast_to([128, 8, 128])), [silf], [silrep])

        def bc(ap, shape):
            return ap.broadcast_to(list(shape))

        modq = []

        def mod_tiles(l, vs):
            par = l % 2
            out = []
            for v in vs:
                for half in range(2):
                    if v in (0, 1, 3, 4):
                        def f(v=v, half=half):
                            slot, wv = W("w_mod", l, 0, 8, v * D + half * 512, 512)
                            fns = []
                            for j in range(4):
                                col = v * 8 + half * 4 + j
                                for k in range(8):
                                    fns.append(mm(psm.ap[:, col:col + 1], wv[:, k, j * 128:(j + 1) * 128], silb.ap[:, k:k + 1],
                                                  start=(k == 0), stop=(k == 7)))
                            PE(fns, [slot, silb], [psm])
                            c0 = v * 8 + half * 4
                            V(lambda e: e.tensor_tensor(out=modcol.ap[:, par, c0:c0 + 4], in0=psm.ap[:, c0:c0 + 4],
                                                        in1=bmodc.ap[:, l * 48 + c0:l * 48 + c0 + 4], op=ALU.add), [psm, bmodc], [modcol])
                            if v in (1, 4):
                                V(lambda e: e.tensor_scalar(out=modcol.ap[:, par, c0:c0 + 4], in0=modcol.ap[:, par, c0:c0 + 4],
                                                            scalar1=1.0, scalar2=None, op0=ALU.add), [modcol], [modcol])
                    else:
                        def f(v=v, half=half):
                            rt = rts[0] if v == 2 else rts[1]
                            if half == 0:
                                load("sp", rt.ap, bmod_d[l:l + 1, v * D:(v + 1) * D].partition_broadcast(128), rt)
                            slot, wv = W("w_mod", l, 0, 8, v * D + half * 512, 512)
                            ps = nb()
                            PE([mm(ps.ap, silrep.ap[:, k, :], wv[:, k, :], start=(k == 0), stop=(k == 7)) for k in range(8)],
                               [slot, silrep], [ps])
                            V(lambda e: e.tensor_tensor(out=rt.ap[:, half * 512:(half + 1) * 512], in0=ps.ap,
                                                        in1=rt.ap[:, half * 512:(half + 1) * 512], op=ALU.add), [ps, rt], [rt])
                    out.append(f)
            return out

        def mod_hook():
            if modq:
                modq.pop(0)()

        def mod_drain():
            while modq:
                modq.pop(0)()

        def make_hT(par, scv, shv):
            for tg in range(2):
                for k in range(8):
                    ps = nb()
                    PE([tr(ps.ap[:, j * 128:(j + 1) * 128], xb.ap[:, tg * 4 + j, k * 128:(k + 1) * 128], identf.ap)
                        for j in range(4)], [xt[tg * 4 + j] for j in range(4)] + [identf], [ps])
                    AC(lambda e, ps=ps, k=k, tg=tg: e.activation(
                        out=hT.ap[:, k, tg * 512:(tg + 1) * 512], in_=ps.ap, func=AF.Identity,
                        scale=modcol.ap[:, par, scv * 8 + k:scv * 8 + k + 1], bias=modcol.ap[:, par, shv * 8 + k:shv * 8 + k + 1]),
                       [ps, modcol], [hT])

        def zproj(slot, wv, t, width):
            ps = nb()
            PE([mm(ps.ap[:, 0:width], hT.ap[:, k, t * 128:(t + 1) * 128], wv[:, k, 0:width], start=(k == 0), stop=(k == 7))
                for k in range(8)], [slot, hT], [ps])
            return ps

        def rope(ps, c0, nsub, t, outap, outbuf, dup=False, cp=None):
            n = nsub * 64
            t1 = nxt(tmpf, "tmpf")
            t2 = nxt(tmpf, "tmpf")
            src = ps.ap[:, c0:c0 + n]
            V(lambda e: e.tensor_tensor(out=t1.ap[:, 0:n].rearrange("p (s f) -> p s f", f=64),
                                        in0=src.rearrange("p (s f) -> p s f", f=64),
                                        in1=bc(ropec.ap[:, t:t + 1, :], [128, nsub, 64]), op=ALU.mult),
              [ps, ropec], [t1])
            s5 = src.rearrange("p (s r u i) -> p s r u i", r=2, u=2, i=16)
            o5 = t2.ap[:, 0:n].rearrange("p (s r u i) -> p s r u i", r=2, u=2, i=16)
            sn = ropes.ap[:, t, :].rearrange("p (r u i) -> p r u i", r=2, u=2)
            for u in range(2):
                V(lambda e, u=u: e.tensor_tensor(
                    out=o5[:, :, :, u, :], in0=s5[:, :, :, 1 - u, :],
                    in1=bc(sn[:, :, u, :].unsqueeze(1), [128, nsub, 2, 16]), op=ALU.mult), [ps, ropes], [t2])
            if dup:
                V(lambda e: e.tensor_tensor(
                    out=outap.rearrange("p (s d f) -> p s d f", d=2, f=64),
                    in0=bc(t1.ap[:, 0:n].rearrange("p (s f) -> p s f", f=64).unsqueeze(2), [128, nsub, 2, 64]),
                    in1=bc(t2.ap[:, 0:n].rearrange("p (s f) -> p s f", f=64).unsqueeze(2), [128, nsub, 2, 64]),
                    op=ALU.add), [t1, t2], [outbuf])
            else:
                V(lambda e: e.tensor_tensor(out=outap, in0=t1.ap[:, 0:n], in1=t2.ap[:, 0:n], op=ALU.add),
                  [t1, t2], [outbuf])

        def to_fm(src_buf, src_ap, nchunks, dst_buf, dst_ap, on_dve=False):
            ps = nb()
            pb = ps.ap.bitcast(BF16)
            PE([tr(pb[:, j * 128:(j + 1) * 128], src_ap[:, j * 128:(j + 1) * 128], identb.ap) for j in range(nchunks)],
               [src_buf, identb], [ps])
            if on_dve:
                V(lambda e: e.tensor_copy(out=dst_ap, in_=pb[:, 0:nchunks * 128].rearrange("p (c t) -> p c t", c=nchunks)), [ps], [dst_buf])
            else:
                AC(lambda e: e.activation(out=dst_ap, in_=pb[:, 0:nchunks * 128].rearrange("p (c t) -> p c t", c=nchunks),
                                          func=AF.Copy), [ps], [dst_buf])

        def rope2(ps, c0, nsub, t, dup=False):
            n = nsub * 64
            nd = n * (2 if dup else 1)
            buf = nxt(tmpf, "tmpf")
            bb = buf.ap.bitcast(BF16)
            A = bb[:, 0:nd]
            B = bb[:, 512:512 + nd]
            src = ps.ap[:, c0:c0 + n]
            s3 = src.rearrange("p (s f) -> p s f", f=64)
            s5 = src.rearrange("p (s r u i) -> p s r u i", r=2, u=2, i=16)
            sn = ropes.ap[:, t, :].rearrange("p (r u i) -> p r u i", r=2, u=2)
            for d in range(2 if dup else 1):
                if dup:
                    Ad = A.rearrange("p (s d f) -> p s d f", d=2, f=64)[:, :, d, :]
                    Bd = B.rearrange("p (s d r u i) -> p s d r u i", d=2, r=2, u=2, i=16)[:, :, d]
                else:
                    Ad = A.rearrange("p (s f) -> p s f", f=64)
                    Bd = B.rearrange("p (s r u i) -> p s r u i", r=2, u=2, i=16)
                V(lambda e, Ad=Ad: e.tensor_tensor(out=Ad, in0=s3, in1=bc(ropec.ap[:, t:t + 1, :], [128, nsub, 64]), op=ALU.mult),
                  [ps, ropec], [buf])
                for u in range(2):
                    V(lambda e, u=u, Bd=Bd: e.tensor_tensor(
                        out=Bd[:, :, :, u, :], in0=s5[:, :, :, 1 - u, :],
                        in1=bc(sn[:, :, u, :].unsqueeze(1), [128, nsub, 2, 16]), op=ALU.mult), [ps, ropes], [buf])
            return buf, A, B

        def to_fm2(buf, A, B, nchunks, dst_buf, dst_ap):
            ps = nb()
            fns = []
            for j in range(nchunks):
                o_ = ps.ap[:, j * 128:(j + 1) * 128]
                fns.append(mm(o_, A[:, j * 128:(j + 1) * 128], identb.ap, start=True, stop=False))
                fns.append(mm(o_, B[:, j * 128:(j + 1) * 128], identb.ap, start=False, stop=True))
            PE(fns, [buf, identb], [ps])
            AC(lambda e: e.activation(out=dst_ap, in_=ps.ap[:, 0:nchunks * 128].rearrange("p (c t) -> p c t", c=nchunks), func=AF.Copy),
               [ps], [dst_buf])

        deferred = []
        obuf_ctr = [0]

        def defer(fn):
            deferred.append(fn)

        def flush(keep=0):
            while len(deferred) > keep:
                deferred.pop(0)()

        def rsqrt(out_ap, in_ap, scale, rbufs, wbufs):
            AC(lambda e: e.activation(out=out_ap, in_=in_ap, func=AF.Ln, scale=scale, bias=sm.ap[:, 8:9]), list(rbufs) + [sm], wbufs)
            AC(lambda e: e.activation(out=out_ap, in_=out_ap, func=AF.Exp, scale=-0.5), wbufs, wbufs)

        def out_fp32(ps, c0, n, dram_ap):
            st = nxt(stage4, "stage4")
            AC(lambda e: e.activation(out=st.ap[:, 0:n], in_=ps.ap[:, c0:c0 + n], func=AF.Copy), [ps], [st])
            store(dram_ap, st.ap[:, 0:n], st)
            return st

        def mixer_A(l, lam_init):
            tsl = slice(None)
            lv = dlam.ap[:, l * 256:(l + 1) * 256]
            s = sm
            V(lambda e: e.tensor_tensor(out=tmpf[0].ap[:, 0:64], in0=lv[:, 0:64], in1=lv[:, 64:128], op=ALU.mult), [dlam], [tmpf[0]])
            V(lambda e: e.tensor_reduce(out=s.ap[:, 0:1], in_=tmpf[0].ap[:, 0:64], axis=AX.X, op=ALU.add), [tmpf[0]], [s])
            V(lambda e: e.tensor_tensor(out=tmpf[0].ap[:, 0:64], in0=lv[:, 128:192], in1=lv[:, 192:256], op=ALU.mult), [dlam], [tmpf[0]])
            V(lambda e: e.tensor_reduce(out=s.ap[:, 1:2], in_=tmpf[0].ap[:, 0:64], axis=AX.X, op=ALU.add), [tmpf[0]], [s])
            AC(lambda e: e.activation(out=s.ap[:, 2:4], in_=s.ap[:, 0:2], func=AF.Exp), [s], [s])
            V(lambda e: e.tensor_tensor(out=s.ap[:, 4:5], in0=s.ap[:, 3:4], in1=s.ap[:, 2:3], op=ALU.subtract), [s], [s])
            V(lambda e: e.tensor_scalar(out=s.ap[:, 5:6], in0=s.ap[:, 4:5], scalar1=-lam_init, scalar2=None, op0=ALU.add), [s], [s])
            V(lambda e: e.tensor_scalar(out=gA.ap, in0=dng.ap[:, l * 128:(l + 1) * 128], scalar1=(1.0 - lam_init), scalar2=None,
                                        op0=ALU.mult), [dng], [gA])
            v4 = vTM.ap[:, 0:5160].rearrange("p (j h e) -> p j h e", j=10, h=4)
            V(lambda e: e.memset(v4[:, :, :, 128:129], 1.0), [], [vTM])
            for j in range(2):
                st = nxt(stage, "stage")
                load("sp", st.ap, cdk_d[l, j * 128:(j + 1) * 128, :], st)
                tb = nxt(tmpb, "tmpb")
                V(lambda e, st=st, tb=tb: e.tensor_copy(out=tb.ap, in_=st.ap), [st], [tb])
                to_fm(tb, tb.ap, 4, kT, kT.ap[:, :, 1024 + j * 128:1024 + (j + 1) * 128])
                st2 = nxt(stage, "stage")
                load("sp", st2.ap, cdv_d[l, j * 128:(j + 1) * 128, :], st2)
                V(lambda e, st2=st2, j=j: e.tensor_copy(out=v4[:, 8 + j, :, 0:128],
                                                        in_=st2.ap.rearrange("p (h e) -> p h e", h=4)), [st2], [vTM])
            for gi in range(3):
                mod_hook()
                slot, wv = W("w_in", l, 0, 8, gi * 512, 512)
                for t in range(NT):
                    ps = zproj(slot, wv, t, 512)
                    if gi != 1:
                        flush()
                    if gi == 0:
                        rb, rA, rB = rope2(ps, 0, 8, t)
                        defer(lambda rb=rb, rA=rA, rB=rB, t=t: to_fm2(rb, rA, rB, 4, qT, qT.ap[:, :, t * 128:(t + 1) * 128]))
                    elif gi == 1:
                        out_fp32(ps, 0, 512, nk_d[l, t * 128:(t + 1) * 128, :])
                        rb, rA, rB = rope2(ps, 0, 8, t)
                        flush()
                        defer(lambda rb=rb, rA=rA, rB=rB, t=t: to_fm2(rb, rA, rB, 4, kT, kT.ap[:, :, t * 128:(t + 1) * 128]))
                    else:
                        out_fp32(ps, 0, 512, nv_d[l, t * 128:(t + 1) * 128, :])
                        V(lambda e, ps=ps, t=t: e.tensor_copy(out=v4[:, t, :, 0:128],
                                                              in_=ps.ap.rearrange("p (h e) -> p h e", h=4)), [ps], [vTM])
            flush()
            if debug and l == 0:
                dump("qTa", qT.ap, [128, 4, 1024], qT)
                dump("kTa", kT.ap, [128, 4, 1280], kT)
            if stop == "A0":
                return

            def qk_steps(r, h, E):
                return [lambda jp=jp: qk_step(r, h, E, jp) for jp in range(5)]

            def qk(r, h, E):
                for f in qk_steps(r, h, E):
                    f()

            def qk_step(r, h, E, jp):
                if True:
                    b0, b1, pr = nb2()
                    PE([mm((b0, b1)[m].ap[:, jj * 256:(jj + 1) * 256], kT.ap[m * 64:(m + 1) * 64, h, (2 * jp + jj) * 128:(2 * jp + jj + 1) * 128],
                           qT.ap[m * 64:(m + 1) * 64, h, r * 256:(r + 1) * 256]) for jj in range(2) for m in range(2)], [kT, qT], [b0, b1])
                    AC(lambda e, pr=pr, jp=jp: e.activation(
                        out=E.ap[:, 2 * jp:2 * jp + 2, :].rearrange("p j (m q) -> p m j q", m=2),
                        in_=pr.ap.rearrange("p (m j q) -> p m j q", m=2, j=2),
                        func=AF.Exp, scale=0.125, bias=dbias.ap[:, r * 10 + 2 * jp:r * 10 + 2 * jp + 1]), [pr, dbias], [E])

            def pv_groups(r, h, E):
                pacc = [bsp, psm]

                def grp(m, sblk):
                    PE([mm(pacc[m].ap[:, sblk * 129:(sblk + 1) * 129],
                           E.ap[:, j, m * 256 + sblk * 128:m * 256 + (sblk + 1) * 128],
                           v4[:, j, h, :], start=(j == 0), stop=(j == 9)) for j in range(10)], [E, vTM], [pacc[m]])
                return [lambda m=m, sblk=sblk: grp(m, sblk) for m in range(2) for sblk in range(2)]

            def pv(r, h, E, ot):
                pacc = [bsp, psm]
                sf = nxt(smallf, "smallf")
                pa3 = pacc[0].ap[:, 0:258].rearrange("p (s e) -> p s e", s=2)
                pb3 = pacc[1].ap[:, 0:258].rearrange("p (s e) -> p s e", s=2)
                V(lambda e: e.reciprocal(out=sf.ap[:, 0:2], in_=pa3[:, :, 128]), [pacc[0]], [sf])
                V(lambda e: e.reciprocal(out=sf.ap[:, 2:4], in_=pb3[:, :, 128]), [pacc[1]], [sf])
                V(lambda e: e.tensor_scalar(out=sf.ap[:, 2:4], in0=sf.ap[:, 2:4], scalar1=sm.ap[:, 5:6], scalar2=None, op0=ALU.mult),
                  [sf, sm], [sf])
                o1 = stage[obuf_ctr[0] % 2]
                obuf_ctr[0] += 1
                o2 = nxt(tmpf, "tmpf")
                o13 = o1.ap[:, 0:256].rearrange("p (s e) -> p s e", s=2)
                o23 = o2.ap[:, 0:256].rearrange("p (s e) -> p s e", s=2)
                V(lambda e: e.tensor_tensor(out=o13, in0=pa3[:, :, 0:128], in1=bc(sf.ap[:, 0:2].unsqueeze(2), [128, 2, 128]),
                                            op=ALU.mult), [pacc[0], sf], [o1])
                V(lambda e: e.tensor_tensor(out=o23, in0=pb3[:, :, 0:128], in1=bc(sf.ap[:, 2:4].unsqueeze(2), [128, 2, 128]),
                                            op=ALU.mult), [pacc[1], sf], [o2])
                V(lambda e: e.tensor_tensor(out=o1.ap[:, 0:256], in0=o1.ap[:, 0:256], in1=o2.ap[:, 0:256], op=ALU.add), [o1, o2], [o1])
                V(lambda e: e.tensor_tensor(out=o2.ap[:, 0:256], in0=o1.ap[:, 0:256], in1=o1.ap[:, 0:256], op=ALU.mult), [o1], [o2])
                V(lambda e: e.tensor_reduce(out=sf.ap[:, 4:6], in_=o23, axis=AX.X, op=ALU.add), [o2], [sf])

                def tail():
                    rsqrt(sf.ap[:, 8:10], sf.ap[:, 4:6], 1.0 / 128.0, [sf], [sf])
                    V(lambda e: e.tensor_tensor(out=o13, in0=o13, in1=bc(sf.ap[:, 8:10].unsqueeze(2), [128, 2, 128]), op=ALU.mult),
                      [o1, sf], [o1])
                    V(lambda e: e.tensor_tensor(out=ot.ap[:, :, h * 128:(h + 1) * 128], in0=o13,
                                                in1=bc(gA.ap.unsqueeze(1), [128, 2, 128]), op=ALU.mult), [o1, gA], [ot])
                    if h == 3:
                        for sblk in range(2):
                            qb = r * 2 + sblk
                            defer(lambda sblk=sblk, qb=qb: to_fm(ot, ot.ap[:, sblk, :], 4, oT, oT.ap[:, 0, :, qb * 128:(qb + 1) * 128], on_dve=True))
                return tail

            seq = [(r, h) for r in range(4) for h in range(4)]
            qk(seq[0][0], seq[0][1], Eb[0])
            pend_tail = None
            for i, (r, h) in enumerate(seq):
                qs = qk_steps(seq[i + 1][0], seq[i + 1][1], Eb[(i + 1) % 2]) if i + 1 < len(seq) else []
                pgs = pv_groups(r, h, Eb[i % 2])
                for k_ in range(5):
                    if k_ < len(qs):
                        qs[k_]()
                    if k_ == 2 and pend_tail is not None:
                        pend_tail()
                        pend_tail = None
                    if k_ < 4:
                        pgs[k_]()
                if pend_tail is not None:
                    pend_tail()
                ot = otm[r % 2]
                pend_tail = pv(r, h, Eb[i % 2], ot)
                flush()
            pend_tail()
            flush()
            flush()

        def mixer_B(l):
            v3 = vTM.ap[:, 0:1300].rearrange("p (j g e) -> p j g e", j=10, g=2)
            V(lambda e: e.memset(v3[:, :, :, 64:65], 1.0), [], [vTM])
            AC(lambda e: e.activation(out=sinkexp.ap, in_=wsink.ap[:, l * 8:(l + 1) * 8], func=AF.Exp), [wsink], [sinkexp])
            for j in range(2):
                st = nxt(stage, "stage")
                load("sp", st.ap[:, 0:128], cwk_d[l, j * 128:(j + 1) * 128, :], st)
                tb = nxt(tmpb, "tmpb")
                V(lambda e, st=st, tb=tb: e.tensor_copy(
                    out=tb.ap[:, 0:256].rearrange("p (s d f) -> p s d f", d=2, f=64),
                    in_=bc(st.ap[:, 0:128].rearrange("p (s f) -> p s f", f=64).unsqueeze(2), [128, 2, 2, 64])), [st], [tb])
                to_fm(tb, tb.ap, 2, kT, kT.ap[:, 0:2, 1024 + j * 128:1024 + (j + 1) * 128])
                st2 = nxt(stage, "stage")
                load("sp", st2.ap[:, 0:128], cwv_d[l, j * 128:(j + 1) * 128, :], st2)
                V(lambda e, st2=st2, j=j: e.tensor_copy(out=v3[:, 8 + j, :, 0:64],
                                                        in_=st2.ap[:, 0:128].rearrange("p (g e) -> p g e", g=2)), [st2], [vTM])
            mod_hook()
            slot, wv = W("w_in", l, 0, 8, 1536, 512)
            for t in range(NT):
                ps = zproj(slot, wv, t, 512)
                flush()
                rb, rA, rB = rope2(ps, 0, 8, t)
                defer(lambda rb=rb, rA=rA, rB=rB, t=t: to_fm2(rb, rA, rB, 4, qT, qT.ap[:, :, t * 128:(t + 1) * 128]))
            mod_hook()
            slot, wv = W("w_in", l, 0, 8, 2048, 256)
            for t in range(NT):
                ps = zproj(slot, wv, t, 256)
                out_fp32(ps, 0, 128, nwk_d[l, t * 128:(t + 1) * 128, :])
                out_fp32(ps, 128, 128, nwv_d[l, t * 128:(t + 1) * 128, :])
                rb, rA, rB = rope2(ps, 0, 2, t, dup=True)
                flush()
                defer(lambda rb=rb, rA=rA, rB=rB, t=t: to_fm2(rb, rA, rB, 2, kT, kT.ap[:, 0:2, t * 128:(t + 1) * 128]))
                V(lambda e, ps=ps, t=t: e.tensor_copy(out=v3[:, t, :, 0:64],
                                                      in_=ps.ap[:, 128:256].rearrange("p (g e) -> p g e", g=2)), [ps], [vTM])
            flush()
            if debug and l == 0:
                dump("qTb", qT.ap, [128, 4, 1024], qT)
                dump("kTb", kT.ap, [128, 4, 1280], kT)

            def tiles_of(n):
                tl = []
                if n >= 1:
                    tl.append((n - 1, 0))
                tl.append((n, None))
                if n <= 6:
                    tl.append((n + 1, 1))
                tl.append((8, "c"))
                tl.append((9, "c"))
                return tl

            def groups_of(tl):
                own = [i for i, (ch, kind) in enumerate(tl) if kind != "c"]
                ctx = [i for i, (ch, kind) in enumerate(tl) if kind == "c"]
                return [own[i:i + 2] for i in range(0, len(own), 2)] + [ctx]

            def ecol(tl, ti, hh):
                for grp in groups_of(tl):
                    if ti in grp:
                        ng = len(grp)
                        return grp[0] * 512 + (hh % 2) * ng * 256 + grp.index(ti) * 256 + (hh // 2) * 128
                raise AssertionError

            def qk_steps(n, g, E):
                tl = tiles_of(n)
                Ef = E.ap.rearrange("p j q -> p (j q)")
                own = [i for i, (ch, kind) in enumerate(tl) if kind != "c"]
                ctx = [i for i, (ch, kind) in enumerate(tl) if kind == "c"]
                groups = [own[i:i + 2] for i in range(0, len(own), 2)] + [ctx]
                return [lambda grp=grp: qk_group(n, g, E, tl, Ef, grp) for grp in groups]

            def qk(n, g, E):
                for f in qk_steps(n, g, E):
                    f()

            def qk_group(n, g, E, tl, Ef, grp):
                if True:
                    isctx = tl[grp[0]][1] == "c"
                    ng = len(grp)
                    b0, b1, pr = nb2()
                    fns = []
                    for gi2, ti in enumerate(grp):
                        ch = tl[ti][0]
                        for i2 in range(2):
                            for half in range(2):
                                head = 4 * g + 2 * i2 + half
                                c = head // 2
                                fns.append(mm((b0, b1)[half].ap[:, (gi2 * 2 + i2) * 128:(gi2 * 2 + i2 + 1) * 128],
                                              kT.ap[half * 64:(half + 1) * 64, g, ch * 128:(ch + 1) * 128],
                                              qT.ap[half * 64:(half + 1) * 64, c, n * 128:(n + 1) * 128]))
                    PE(fns, [kT, qT], [b0, b1])
                    eo = Ef[:, grp[0] * 512:grp[0] * 512 + ng * 512].rearrange("p (h x) -> p h x", h=2)
                    pin = pr.ap.rearrange("p (h x) -> p h x", h=2)[:, :, 0:ng * 256]
                    if isctx:
                        AC(lambda e, pin=pin, eo=eo: e.activation(out=eo, in_=pin, func=AF.Exp, scale=0.125, bias=tsm.ap[:, 0:1]),
                           [pr, tsm], [E])
                    else:
                        AC(lambda e, pin=pin, eo=eo: e.activation(out=eo, in_=pin, func=AF.Exp, scale=0.125), [pr], [E])
                    for gi2, ti in enumerate(grp):
                        kind = tl[ti][1]
                        if kind is not None and kind != "c":
                            blk = Ef[:, grp[0] * 512:grp[0] * 512 + ng * 512].rearrange("p (h x) -> p h x", h=2)[
                                :, :, gi2 * 256:(gi2 + 1) * 256].rearrange("p h (i q) -> p h i q", i=2)
                            V(lambda e, blk=blk, kind=kind: e.tensor_tensor(
                                out=blk, in0=blk, in1=bc(wmask.ap[:, kind, n:n + 1, :].unsqueeze(1), [128, 2, 2, 128]), op=ALU.mult),
                              [E, wmask], [E])

            def pv_groups(n, g, E, pacc):
                tl = tiles_of(n)

                def grp(h2):
                    fns = []
                    for hh in (2 * h2, 2 * h2 + 1):
                        for ti, (ch, kind) in enumerate(tl):
                            c0 = ecol(tl, ti, hh)
                            fns.append(mm(pacc.ap[:, hh * 65:(hh + 1) * 65], E.ap.rearrange("p j q -> p (j q)")[:, c0:c0 + 128],
                                          v3[:, ch, g, :], start=(ti == 0), stop=(ti == len(tl) - 1)))
                    PE(fns, [E, vTM], [pacc])
                return [lambda h2=h2: grp(h2) for h2 in range(2)]

            def pv(n, g, E, ot, pacc):
                tl = tiles_of(n)
                sf = nxt(smallf, "smallf")
                p3 = pacc.ap[:, 0:260].rearrange("p (h e) -> p h e", h=4)
                V(lambda e: e.tensor_tensor(out=sf.ap[:, 0:4], in0=p3[:, :, 64], in1=sinkexp.ap[:, 4 * g:4 * g + 4], op=ALU.add),
                  [pacc, sinkexp], [sf])
                V(lambda e: e.reciprocal(out=sf.ap[:, 4:8], in_=sf.ap[:, 0:4]), [sf], [sf])
                V(lambda e: e.tensor_tensor(out=ot.ap[:, 0, g * 256:(g + 1) * 256].rearrange("p (h e) -> p h e", h=4),
                                            in0=p3[:, :, 0:64], in1=bc(sf.ap[:, 4:8].unsqueeze(2), [128, 4, 64]), op=ALU.mult),
                  [pacc, sf], [ot])

            seq = [(n, g) for n in range(8) for g in range(2)]
            qk(0, 0, Eb[0])
            for i, (n, g) in enumerate(seq):
                qs = qk_steps(seq[i + 1][0], seq[i + 1][1], Eb[(i + 1) % 2]) if i + 1 < len(seq) else []
                pacc_i = (bsp, psm)[i % 2]
                pgs = pv_groups(n, g, Eb[i % 2], pacc_i)
                for k_ in range(max(len(qs), 2)):
                    if k_ < len(qs):
                        qs[k_]()
                    if k_ < 2:
                        pgs[k_]()
                ot = otm[n % 2]
                pv(n, g, Eb[i % 2], ot, pacc_i)
                flush()
                if g == 1:
                    defer(lambda ot=ot, n=n: to_fm(ot, ot.ap[:, 0, :], 4, oT, oT.ap[:, 1, :, n * 128:(n + 1) * 128], on_dve=True))
            flush()

        def mixer_C(l):
            rd = rdec.ap[:, l * 8:(l + 1) * 8]
            AC(lambda e: e.activation(out=lgb.ap, in_=rd, func=AF.Exp, scale=-1.0), [rdec], [lgb])
            AC(lambda e: e.activation(out=lgb.ap, in_=lgb.ap, func=AF.Ln, bias=sm.ap[:, 9:10]), [lgb, sm], [lgb])
            V(lambda e: e.tensor_scalar(out=lgb.ap, in0=lgb.ap, scalar1=-1.0, scalar2=None, op0=ALU.mult), [lgb], [lgb])
            lg4 = lgb.ap.rearrange("p (d r m) -> p d r m", d=2, r=2)
            lc3 = lgcol.ap.rearrange("p (d r) -> p d r", d=2)
            V(lambda e: e.tensor_copy(out=lc3[0:64, :, :], in_=lg4[0:64, :, :, 0]), [lgb], [lgcol])
            V(lambda e: e.tensor_copy(out=lc3[64:128, :, :], in_=lg4[64:128, :, :, 1]), [lgb], [lgcol])
            relf = rel.ap[:, 0:128]
            relb = rel.ap[:, 128:256]
            mf = rel.ap[:, 256:384]
            mb = rel.ap[:, 384:512]
            dtb = nxt(tmpf, "tmpf")
            dtv = dtb.ap.rearrange("p (h q) -> p h q", h=4)
            for h in range(4):
                AC(lambda e, h=h: e.activation(out=Dsum.ap[:, h, :], in_=relf, func=AF.Exp, scale=lgb.ap[:, h:h + 1]), [rel, lgb], [Dsum])
                AC(lambda e, h=h: e.activation(out=dtv[:, h, :], in_=relb, func=AF.Exp, scale=lgb.ap[:, 4 + h:5 + h]), [rel, lgb], [dtb])
            V(lambda e: e.tensor_tensor(out=Dsum.ap, in0=Dsum.ap, in1=bc(mf.unsqueeze(1), [128, 4, 128]), op=ALU.mult), [Dsum, rel], [Dsum])
            V(lambda e: e.tensor_tensor(out=dtv, in0=dtv, in1=bc(mb.unsqueeze(1), [128, 4, 128]), op=ALU.mult), [dtb, rel], [dtb])
            V(lambda e: e.tensor_tensor(out=Dsum.ap, in0=Dsum.ap, in1=dtv, op=ALU.add), [Dsum, dtb], [Dsum])
            for r in range(2):
                AC(lambda e, r=r: e.activation(out=xif.ap[:, r, :], in_=qrow.ap[:, 0:128], func=AF.Exp, scale=lgcol.ap[:, r:r + 1]),
                   [qrow, lgcol], [xif])
                AC(lambda e, r=r: e.activation(out=xib.ap[:, r, :], in_=qrow.ap[:, 128:256], func=AF.Exp, scale=lgcol.ap[:, 2 + r:3 + r]),
                   [qrow, lgcol], [xib])
            AC(lambda e: e.activation(out=zet.ap[:, 0, :], in_=lgb.ap[:, 0:4], func=AF.Exp, scale=tsm.ap[:, 17:18]), [lgb, tsm], [zet])
            AC(lambda e: e.activation(out=zet.ap[:, 1, :], in_=lgb.ap[:, 4:8], func=AF.Exp, scale=tsm.ap[:, 18:19]), [lgb, tsm], [zet])
            AC(lambda e: e.activation(out=sm.ap[:, 12:16], in_=lgcol.ap, func=AF.Exp, scale=128.0), [lgcol], [sm])
            car3 = tsm.ap[:, 1:17].rearrange("p (d n) -> p d n", d=2)
            V(lambda e: e.tensor_tensor(out=gccar.ap, in0=bc(sm.ap[:, 12:16].rearrange("p (d r) -> p d r", d=2).unsqueeze(2), [128, 2, 8, 2]),
                                        in1=bc(car3.unsqueeze(3), [128, 2, 8, 2]), op=ALU.mult), [sm, tsm], [gccar])
            mod_hook()
            slot, wv = W("w_in", l, 0, 8, 2304, 512)
            kz = {0: rKzf, 1: rKzb}
            for t in range(NT):
                ps = zproj(slot, wv, t, 512)
                tb = nxt(tmpb, "tmpb")
                AC(lambda e, ps=ps, tb=tb: e.activation(out=tb.ap[:, 0:256], in_=ps.ap[:, 0:256], func=AF.Copy), [ps], [tb])
                AC(lambda e, ps=ps, tb=tb: e.activation(out=tb.ap[:, 256:512], in_=ps.ap[:, 256:512], func=AF.Copy, scale=0.125), [ps], [tb])
                flush()
                defer(lambda tb=tb, t=t: to_fm(tb, tb.ap[:, 0:256], 2, qT, qT.ap[:, 0:2, t * 128:(t + 1) * 128]))
                defer(lambda tb=tb, t=t: to_fm(tb, tb.ap[:, 256:512], 2, kT, kT.ap[:, 0:2, t * 128:(t + 1) * 128]))
                for d in range(2):
                    V(lambda e, tb=tb, t=t, d=d: e.tensor_tensor(
                        out=kz[d].ap[:, t, :].rearrange("p (h f) -> p h f", h=4),
                        in0=tb.ap[:, 256:512].rearrange("p (h f) -> p h f", h=4),
                        in1=bc(zet.ap[:, d, :].unsqueeze(2), [128, 4, 64]), op=ALU.mult), [tb, zet], [kz[d]])
            slot, wv = W("w_in", l, 0, 8, 2816, 512)
            vr = vTM.ap[:, 0:4096].rearrange("p (n f) -> p n f", n=8)
            for t in range(NT):
                ps = zproj(slot, wv, t, 512)
                AC(lambda e, ps=ps, t=t: e.activation(out=vr[:, t, :], in_=ps.ap, func=AF.Copy), [ps], [vTM])
            flush()
            slot_g, wv_g = W("w_in", l, 0, 8, 3328, 512)
            cgq = []
            for t in range(NT):
                def cgf(t=t):
                    psg = zproj(slot_g, wv_g, t, 512)
                    AC(lambda e: e.activation(out=cgs_all.ap[:, t, :], in_=psg.ap, func=AF.Silu), [psg], [cgs_all])
                    V(lambda e: e.tensor_tensor(out=cgs_all.ap[:, t, :].rearrange("p (h e) -> p h e", h=4),
                                                in0=cgs_all.ap[:, t, :].rearrange("p (h e) -> p h e", h=4),
                                                in1=bc(rng.ap[:, l * 128:(l + 1) * 128].unsqueeze(1), [128, 4, 128]), op=ALU.mult),
                      [cgs_all, rng], [cgs_all])
                cgq.append(cgf)
            for (dst, xi) in ((rQxf, xif), (rQxb, xib)):
                for r in range(2):
                    V(lambda e, dst=dst, xi=xi, r=r: e.tensor_tensor(
                        out=dst.ap[:, r, :].rearrange("p (n q) -> p n q", n=8),
                        in0=qT.ap[:, r, :].rearrange("p (n q) -> p n q", n=8),
                        in1=bc(xi.ap[:, r:r + 1, :], [128, 8, 128]), op=ALU.mult), [qT, xi], [dst])
            for d in range(2):
                order = list(range(8)) if d == 0 else list(range(7, -1, -1))
                sprev = nxt(Sfp, "Sfp")
                load("sp", sprev.ap.rearrange("p r e -> p (r e)"), stin_d[l, d], sprev)
                for n in order:
                    V(lambda e, sprev=sprev, d=d, n=n: e.tensor_scalar(
                        out=rSall.ap[:, d, n], in0=sprev.ap, scalar1=tsm.ap[:, 1 + d * 8 + n:2 + d * 8 + n], scalar2=None,
                        op0=ALU.mult), [sprev, tsm], [rSall])
                    ps = nb()
                    fns = []
                    for r in range(2):
                        for m in range(2):
                            hd = 2 * r + m
                            fns.append(mm(ps.ap[m * 64:(m + 1) * 64, r * 128:(r + 1) * 128],
                                          kz[d].ap[:, n, hd * 64:(hd + 1) * 64], vr[:, n, hd * 128:(hd + 1) * 128]))
                    PE(fns, [kz[d], vTM], [ps])
                    snew = nxt(Sfp, "Sfp")
                    V(lambda e, sprev=sprev, snew=snew, d=d, n=n: e.tensor_tensor(
                        out=snew.ap, in0=sprev.ap, in1=bc(gccar.ap[:, d, n, :].unsqueeze(2), [128, 2, 128]), op=ALU.mult),
                      [sprev, gccar], [snew])
                    V(lambda e, snew=snew, ps=ps: e.tensor_tensor(
                        out=snew.ap, in0=snew.ap, in1=ps.ap[:, 0:256].rearrange("p (r e) -> p r e", r=2), op=ALU.add),
                      [snew, ps], [snew])
                    if (d == 0 and n % 2 == 1) or (d == 1 and n % 2 == 0):
                        store(nst_d[l, d, n // 2], snew.ap.rearrange("p r e -> p (r e)"), snew)
                    sprev = snew
                    if n % 2 == 1 and cgq:
                        cgq.pop(0)()
            while cgq:
                cgq.pop(0)()
            def qk_c(n):
                ad = adt[n % 2]
                for m in range(2):
                    psa = nb()
                    PE([mm(psa.ap[:, r * 128:(r + 1) * 128], kT.ap[m * 64:(m + 1) * 64, r, n * 128:(n + 1) * 128],
                           qT.ap[m * 64:(m + 1) * 64, r, n * 128:(n + 1) * 128]) for r in range(2)], [kT, qT], [psa])
                    V(lambda e, psa=psa, m=m: e.tensor_tensor(
                        out=ad.ap.rearrange("p (r t) q -> p r t q", t=2)[:, :, m, :],
                        in0=psa.ap[:, 0:256].rearrange("p (r q) -> p r q", r=2),
                        in1=Dsum.ap.rearrange("p (r t) q -> p r t q", t=2)[:, :, m, :], op=ALU.mult), [psa, Dsum], [ad])

            def pv_c(n):
                ad = adt[n % 2]
                psy = bsp
                fns = []
                for hd in range(4):
                    r, m = hd // 2, hd % 2
                    o_ = psy.ap[:, hd * 128:(hd + 1) * 128]
                    fns.append(mm(o_, ad.ap[:, hd, :], vr[:, n, hd * 128:(hd + 1) * 128], start=True, stop=False))
                    fns.append(mm(o_, rQxf.ap[m * 64:(m + 1) * 64, r, n * 128:(n + 1) * 128], rSall.ap[m * 64:(m + 1) * 64, 0, n, r, :],
                                  start=False, stop=False))
                    fns.append(mm(o_, rQxb.ap[m * 64:(m + 1) * 64, r, n * 128:(n + 1) * 128], rSall.ap[m * 64:(m + 1) * 64, 1, n, r, :],
                                  start=False, stop=True))
                PE(fns, [ad, vTM, rQxf, rQxb, rSall], [psy])
                return psy

            def ln_c(n, psy):
                y3 = psy.ap.rearrange("p (h e) -> p h e", h=4)
                sf = nxt(smallf, "smallf")
                sq = nxt(tmpf, "tmpf")
                t1 = nxt(tmpf, "tmpf")
                V(lambda e: e.tensor_reduce(out=sf.ap[:, 0:4], in_=y3, axis=AX.X, op=ALU.add), [psy], [sf])
                AC(lambda e: e.activation(out=sq.ap, in_=psy.ap, func=AF.Square), [psy], [sq])
                t13 = t1.ap.rearrange("p (h e) -> p h e", h=4)
                V(lambda e: e.tensor_scalar(out=sf.ap[:, 0:4], in0=sf.ap[:, 0:4], scalar1=1.0 / 128.0, scalar2=None, op0=ALU.mult), [sf], [sf])
                V(lambda e: e.tensor_tensor(out=t13, in0=y3, in1=bc(sf.ap[:, 0:4].unsqueeze(2), [128, 4, 128]), op=ALU.subtract),
                  [psy, sf], [t1])
                flush()
                V(lambda e: e.tensor_reduce(out=sf.ap[:, 4:8], in_=sq.ap.rearrange("p (h e) -> p h e", h=4), axis=AX.X, op=ALU.add),
                  [sq], [sf])
                V(lambda e: e.tensor_scalar(out=sf.ap[:, 4:8], in0=sf.ap[:, 4:8], scalar1=1.0 / 128.0, scalar2=None, op0=ALU.mult), [sf], [sf])
                V(lambda e: e.tensor_tensor(out=sf.ap[:, 8:12], in0=sf.ap[:, 0:4], in1=sf.ap[:, 0:4], op=ALU.mult), [sf], [sf])
                V(lambda e: e.tensor_tensor(out=sf.ap[:, 4:8], in0=sf.ap[:, 4:8], in1=sf.ap[:, 8:12], op=ALU.subtract), [sf], [sf])
                rsqrt(sf.ap[:, 12:16], sf.ap[:, 4:8], 1.0, [sf], [sf])
                V(lambda e: e.tensor_tensor(out=t13, in0=t13, in1=bc(sf.ap[:, 12:16].unsqueeze(2), [128, 4, 128]), op=ALU.mult),
                  [t1, sf], [t1])
                ot = otm[n % 2]
                V(lambda e: e.tensor_tensor(out=ot.ap[:, 0, :], in0=t1.ap, in1=cgs_all.ap[:, n, :], op=ALU.mult), [t1, cgs_all], [ot])
                defer(lambda: to_fm(ot, ot.ap[:, 0, :], 4, oT, oT.ap[:, 2, :, n * 128:(n + 1) * 128]))

            mod_drain()
            if l + 1 < DEPTH:
                modq.extend(mod_tiles(l + 1, (0, 1, 3, 4)))
            qk_c(0)
            for n in range(8):
                mod_hook()
                psy = pv_c(n)
                if n + 1 < 8:
                    qk_c(n + 1)
                flush()
                ln_c(n, psy)
            flush()

        lnm = [lnmv, lnmv2]

        def ln_load(l, gname, bname):
            load("sp", rts[2].ap, ln_d[gname][l:l + 1, :].partition_broadcast(128), rts[2])
            load("sp", rts[3].ap, ln_d[bname][l:l + 1, :].partition_broadcast(128), rts[3])

        def ln_stats(t):
            mv = lnm[t % 2]
            st = lnstats[t % 2]
            for hf in range(2):
                V(lambda e, hf=hf: e.bn_stats(out=st.ap[:, hf, :], in_=xb.ap[:, t, hf * 512:(hf + 1) * 512]), [xt[t]], [st])
            V(lambda e: e.bn_aggr(out=mv.ap[:, 0:2], in_=st.ap.rearrange("p a b -> p (a b)")), [st], [mv])
            rsqrt(mv.ap[:, 2:3], mv.ap[:, 1:2], 1.0, [mv], [mv])
            V(lambda e: e.scalar_tensor_tensor(out=mv.ap[:, 3:4], in0=mv.ap[:, 0:1], scalar=-1.0, in1=mv.ap[:, 2:3],
                                               op0=ALU.mult, op1=ALU.mult), [mv], [mv])
            AC(lambda e: e.activation(out=xb.ap[:, t, :], in_=xb.ap[:, t, :], func=AF.Identity, scale=mv.ap[:, 2:3],
                                      bias=mv.ap[:, 3:4]), [xt[t], mv], [xt[t]])

        def ln_affine(t):
            V(lambda e: e.tensor_tensor(out=xb.ap[:, t, :], in0=xb.ap[:, t, :], in1=rts[2].ap, op=ALU.mult), [xt[t], rts[2]], [xt[t]])
            V(lambda e: e.tensor_tensor(out=xb.ap[:, t, :], in0=xb.ap[:, t, :], in1=rts[3].ap, op=ALU.add), [xt[t], rts[3]], [xt[t]])

        def ln_step(t):
            ln_stats(t)
            if t >= 1:
                ln_affine(t - 1)
            if t == NT - 1:
                ln_affine(t)

        def merge_phase(l):
            ln_load(l, "ln1_g", "ln1_b")
            mod_drain()
            modq.extend(mod_tiles(l, (5,)))
            units = [(G, b) for G in range(2) for b in range(3)]

            def p_issue(i):
                G_, b_ = units[i]
                sl = pslots[i % 2]
                load("pool", sl.ap.rearrange("p (k c) -> p k c", k=4),
                     wp_d[b_][l][0:512, G_ * 512:(G_ + 1) * 512].rearrange("(k p) n -> p k n", p=128), sl)

            p_issue(0)
            for ui, (G, b) in enumerate(units):
                if True:
                    mod_hook()
                    gslot, gwv = W("w_gate", l, 0, 8, b * D + G * 512, 512)
                    if ui + 1 < len(units):
                        p_issue(ui + 1)
                    pslot = pslots[ui % 2]
                    pwv = pslot.ap.rearrange("p (k c) -> p k c", k=4)
                    for j in range(4):
                        col = l * 24 + b * 8 + G * 4 + j
                        for hf in range(2):
                            tsel = slice(hf * 512, (hf + 1) * 512)
                            psg = nb()
                            PE([mm(psg.ap, gwv[:, k, j * 128:(j + 1) * 128], hT.ap[:, k, tsel], start=(k == 0), stop=(k == 7))
                                for k in range(8)], [gslot, hT], [psg])
                            gt = nxt(tmpb, "tmpb")
                            AC(lambda e, psg=psg, gt=gt, col=col: e.activation(out=gt.ap, in_=psg.ap, func=AF.Sigmoid,
                                                                               bias=bgatec.ap[:, col:col + 1]), [psg, bgatec], [gt])
                            psp = nb()
                            PE([mm(psp.ap, pwv[:, k, j * 128:(j + 1) * 128], oT.ap[:, b, k, tsel], start=(k == 0), stop=(k == 3))
                                for k in range(4)], [pslot, oT], [psp])
                            if b == 0:
                                V(lambda e, psp=psp, gt=gt, j=j, tsel=tsel: e.tensor_tensor(out=macc.ap[:, j, tsel], in0=psp.ap, in1=gt.ap,
                                                                                           op=ALU.mult), [psp, gt], [macc])
                            else:
                                tmq = nxt(tmpf, "tmpf")
                                V(lambda e, psp=psp, gt=gt, tmq=tmq: e.tensor_tensor(out=tmq.ap, in0=psp.ap, in1=gt.ap, op=ALU.mult),
                                  [psp, gt], [tmq])
                                if b == 1:
                                    V(lambda e, tmq=tmq, j=j, tsel=tsel: e.tensor_tensor(out=macc.ap[:, j, tsel], in0=macc.ap[:, j, tsel],
                                                                                        in1=tmq.ap, op=ALU.add), [macc, tmq], [macc])
                                else:
                                    V(lambda e, tmq=tmq, j=j, tsel=tsel, G=G: e.tensor_tensor(
                                        out=mergedT.ap[:, G * 4 + j, tsel], in0=macc.ap[:, j, tsel], in1=tmq.ap, op=ALU.add),
                                      [macc, tmq], [mergedT])
            if debug and l == 0:
                dump("mergedT", mergedT.ap, [128, 8, 1024], mergedT)
            mod_drain()
            wo_t = [W("w_o", l, 0, 8, cg * 512, 512, deep=(cg == 0)) for cg in range(2)]
            for half in range(2):
                for cg in range(2):
                    slot, wv = wo_t[cg]
                    for t in range(half * 4, half * 4 + 4):
                        ps = nb()
                        PE([mm(ps.ap, mergedT.ap[:, k, t * 128:(t + 1) * 128], wv[:, k, :], start=(k == 0), stop=(k == 7)) for k in range(8)],
                           [slot, mergedT], [ps])
                        tmq = nxt(tmpf, "tmpf")
                        csl = slice(cg * 512, (cg + 1) * 512)
                        V(lambda e, ps=ps, tmq=tmq, csl=csl: e.tensor_tensor(out=tmq.ap, in0=ps.ap, in1=rts[0].ap[:, csl], op=ALU.mult),
                          [ps, rts[0]], [tmq])
                        V(lambda e, tmq=tmq, t=t, csl=csl: e.scalar_tensor_tensor(out=xb.ap[:, t, csl], in0=xb.ap[:, t, csl], scalar=ALPHA,
                                                                                  in1=tmq.ap, op0=ALU.mult, op1=ALU.add), [xt[t], tmq], [xt[t]])
                        if cg == 1:
                            ln_step(t)

        def mlp_phase(l):
            make_hT(l % 2, 4, 3)
            ln_load(l, "ln2_g", "ln2_b")
            if l + 1 < DEPTH:
                modq.extend(mod_tiles(l + 1, (2,)))
            for cgp in range(8):
                slot, wv = W("w_ff1", l, 0, 8, cgp * 512, 512)
                for j in range(4):
                    for hf in range(2):
                        ps = nb()
                        PE([mm(ps.ap, wv[:, k, j * 128:(j + 1) * 128], hT.ap[:, k, hf * 512:(hf + 1) * 512], start=(k == 0), stop=(k == 7))
                            for k in range(8)], [slot, hT], [ps])
                        tmq = nxt(tmpf, "tmpf")
                        AC(lambda e, ps=ps, tmq=tmq: e.activation(out=tmq.ap, in_=ps.ap, func=AF.Relu), [ps], [tmq])
                        V(lambda e, tmq=tmq, c=cgp * 4 + j, hf=hf: e.tensor_tensor(out=fT.ap[:, c, hf * 512:(hf + 1) * 512], in0=tmq.ap,
                                                                                  in1=tmq.ap, op=ALU.mult), [tmq], [fT])
                mod_hook()
            def ff2_pass(slot, wv, cg, hg, tiles, last):
                csl = slice(cg * 512, (cg + 1) * 512)
                for t in tiles:
                    ps = nb()
                    PE([mm(ps.ap, fT.ap[:, hg * 8 + k, t * 128:(t + 1) * 128], wv[:, k, :], start=(k == 0), stop=(k == 7)) for k in range(8)],
                       [slot, fT], [ps])
                    tmq = nxt(tmpf, "tmpf")
                    V(lambda e, ps=ps, tmq=tmq: e.tensor_tensor(out=tmq.ap, in0=ps.ap, in1=rts[1].ap[:, csl], op=ALU.mult),
                      [ps, rts[1]], [tmq])
                    if hg == 0:
                        V(lambda e, tmq=tmq, t=t: e.scalar_tensor_tensor(out=xb.ap[:, t, csl], in0=xb.ap[:, t, csl], scalar=ALPHA,
                                                                         in1=tmq.ap, op0=ALU.mult, op1=ALU.add), [xt[t], tmq], [xt[t]])
                    else:
                        V(lambda e, tmq=tmq, t=t: e.tensor_tensor(out=xb.ap[:, t, csl], in0=xb.ap[:, t, csl], in1=tmq.ap, op=ALU.add),
                          [xt[t], tmq], [xt[t]])
                    if last:
                        ln_step(t)

            for cg in range(2):
                for hg in range(3):
                    mod_hook()
                    slot, wv = W("w_ff2", l, hg * 1024, 8, cg * 512, 512)
                    ff2_pass(slot, wv, cg, hg, range(NT), False)
            mod_hook()
            mod_hook()
            last_t = [W("w_ff2", l, 3 * 1024, 8, cg * 512, 512, deep=(cg == 0)) for cg in range(2)]
            for half in range(2):
                for cg in range(2):
                    ff2_pass(last_t[cg][0], last_t[cg][1], cg, 3, range(half * 4, half * 4 + 4), cg == 1)
            mod_drain()

        V(lambda e: e.memset(modcol.ap, 0.0), [], [modcol])
        V(lambda e: e.memset(sm.ap, 0.0), [], [sm])
        V(lambda e: e.memset(sm.ap[:, 8:9], LN_EPS), [sm], [sm])
        V(lambda e: e.memset(sm.ap[:, 9:10], 1.0), [sm], [sm])
        import math
        def mark(name):
            PHASES.append((name, sum(len(c) for (_, c, _) in tk.ops["pe"])))

        def run_layer(l):
            lam_init = 0.8 - 0.6 * math.exp(-0.3 * l)
            mark("L%d mod" % l)
            if l == 0:
                for fi, f in enumerate(mod_tiles(0, (0, 1))):
                    f()
                    if fi == 0:
                        for i in range(NT):
                            load("act", xt[i].ap, x_d[i * 128:(i + 1) * 128, :], xt[i], after=[wslots[0]])
                modq.extend(mod_tiles(0, (3, 4, 2)))
            if stop == "mod":
                mod_drain()
                dump("modcol", modcol.ap[:, 0, :], [128, 48], modcol)
                dump("g1", rts[0].ap, [128, 1024], rts[0])
                return False
            mark("L%d hT" % l)
            make_hT(l % 2, 1, 0)
            if debug and l == 0:
                dump("hT", hT.ap, [128, 8, 1024], hT)
            if stop == "hT":
                return False
            mark("L%d A" % l)
            mixer_A(l, lam_init)
            if stop == "A0":
                return False
            if stop == "A":
                dump("oTa", oT.ap[:, 0], [128, 4, 1024], oT)
                return False
            mark("L%d B" % l)
            mixer_B(l)
            if stop == "B":
                dump("oTb", oT.ap[:, 1], [128, 4, 1024], oT)
                return False
            mark("L%d C" % l)
            mixer_C(l)
            if stop == "C":
                dump("oT", oT.ap, [128, 3, 4, 1024], oT)
                return False
            mark("L%d merge" % l)
            merge_phase(l)
            if debug and l == 0:
                dump("x1", xb.ap, [128, 8, 1024], xb)
            if stop == "merge":
                return False
            mark("L%d mlp" % l)
            mlp_phase(l)
            if debug and l == 0:
                dump("x2", xb.ap, [128, 8, 1024], xb)
            return True

        for l in range(DEPTH):
            if not run_layer(l):
                break
        mark("end")
        for i in range(NT):
            store(y_d[i * 128:(i + 1) * 128, :], xt[i].ap, xt[i])
        if plan is not None:
            tk.emit()
    return nc, dbg_out, wreqs


def build_program(debug=False, stop=None):
    _, _, reqs = _build(debug, stop, None)
    PHASES.clear()
    nc, dbg, _ = _build(debug, stop, reqs)
    return nc, dbg


def _const_tables(role):
    f = np.float32
    tabs = {}
    tabs["ident"] = np.eye(128, dtype=f)
    p = np.arange(128)
    tt = (np.arange(8)[None, :] * 128 + p[:, None])
    if role == "sample":
        row = (tt // 64).astype(f)
        col = (tt % 64).astype(f)
        inv = (np.float32(10000.0) ** (-(np.arange(16, dtype=f)) / np.float32(16))).astype(f)
        ang_r = (row[..., None] * inv).astype(f)
        ang_c = (col[..., None] * inv).astype(f)
        cos = np.zeros((128, 8, 64), f)
        sin = np.zeros((128, 8, 64), f)
        for rc, ang in enumerate((ang_r, ang_c)):
            for u in range(2):
                sl = slice(rc * 32 + u * 16, rc * 32 + u * 16 + 16)
                cos[:, :, sl] = np.cos(ang)
                sin[:, :, sl] = (-np.sin(ang)) if u == 0 else np.sin(ang)
        tabs["ropec"], tabs["ropes"] = cos, sin
    else:
        tabs["ropec"] = np.ones((128, 8, 64), f)
        tabs["ropes"] = np.zeros((128, 8, 64), f)
    db = np.zeros((128, 40), f)
    if role == "prompt":
        for r in range(4):
            for j in range(10):
                if j not in (2 * r, 2 * r + 1):
                    db[:, r * 10 + j] = NEG
    tabs["dbias"] = db
    wm = np.zeros((128, 2, 8, 128), f)
    kk = p[:, None]
    qq = p[None, :]
    for n in range(8):
        if role == "sample":
            if n >= 1:
                wm[:, 0, n, :] = (kk >= qq)
            if n <= 6:
                wm[:, 1, n, :] = (kk <= qq)
        else:
            if n % 2 == 1:
                wm[:, 0, n, :] = 1.0
            else:
                wm[:, 1, n, :] = 1.0
    tabs["wmask"] = wm.reshape(128, -1)
    tsm = np.zeros((128, 32), f)
    tsm[:, 0] = 0.0 if role == "sample" else NEG
    for n in range(8):
        if role == "sample":
            tsm[:, 1 + n] = 1.0
            tsm[:, 9 + n] = 1.0
        else:
            tsm[:, 1 + n] = 1.0 if n % 2 == 1 else 0.0
            tsm[:, 9 + n] = 1.0 if n % 2 == 0 else 0.0
    tsm[:, 17] = 127.0 - p
    tsm[:, 18] = p
    tabs["tsm"] = tsm
    rel = np.zeros((128, 4, 128), f)
    rel[:, 0] = np.maximum(qq - kk, 0)
    rel[:, 1] = np.maximum(kk - qq, 0)
    rel[:, 2] = (qq >= kk)
    rel[:, 3] = (kk >= qq)
    tabs["rel"] = rel.reshape(128, -1)
    qr = np.zeros((128, 2, 128), f)
    qr[:, 0] = (qq + 1)
    qr[:, 1] = (128 - qq)
    tabs["qrow"] = qr.reshape(128, -1)
    return tabs


_PROG = {}


def _get_prog(debug=False):
    if debug not in _PROG:
        _PROG[debug] = build_program(debug)
    return _PROG[debug]


def make_in_maps(inputs):
    f = np.float32
    g = {k: np.ascontiguousarray(np.asarray(v, dtype=f)) for k, v in inputs.items()}
    shared = {}
    for n in ("w_mod", "w_in", "w_gate", "w_pa", "w_pb", "w_pc", "w_o", "w_ff1", "w_ff2", "b_mod",
              "ln1_g", "ln1_b", "ln2_g", "ln2_b"):
        shared[n] = g[n]
    bmodc = np.ascontiguousarray(g["b_mod"].reshape(DEPTH, 48, 128).transpose(2, 0, 1).reshape(128, DEPTH * 48))
    bgatec = np.ascontiguousarray(g["b_gate"].reshape(DEPTH, 24, 128).transpose(2, 0, 1).reshape(128, DEPTH * 24))
    shared["pvec"] = np.concatenate([g[n].reshape(-1) for n in ("diff_lam", "diff_norm_g", "win_sink", "ret_decay", "ret_norm_g")]).reshape(1, -1)
    ctab = {r: _const_tables(r) for r in ("prompt", "sample")}
    maps = []
    for core in range(8):
        m = dict(shared)
        if core < 4:
            m["x"] = g["x_prompt"][4 * core:4 * core + 4].reshape(T, D)
            cv = g["c_ctx"]
            m["cdk"] = np.zeros((DEPTH, 256, 512), f)
            m["cdv"] = np.zeros((DEPTH, 256, 512), f)
            m["cwk"] = np.zeros((DEPTH, 256, 128), f)
            m["cwv"] = np.zeros((DEPTH, 256, 128), f)
            m["stin"] = np.zeros((DEPTH, 2, 128, 256), f)
        else:
            b = core - 4
            m["x"] = g["x_sample"][b]
            cv = g["c"][b]
            m["cdk"] = g["cache_diff_k"][b].reshape(DEPTH, 256, 512)
            m["cdv"] = g["cache_diff_v"][b].reshape(DEPTH, 256, 512)
            m["cwk"] = g["cache_win_k"][b].reshape(DEPTH, 256, 128)
            m["cwv"] = g["cache_win_v"][b].reshape(DEPTH, 256, 128)
            s = g["state_ret"][b].reshape(DEPTH, 2, 2, 2, 64, 128).transpose(0, 1, 3, 4, 2, 5)
            m["stin"] = np.ascontiguousarray(s.reshape(DEPTH, 2, 128, 256))
        cvec = np.ascontiguousarray(cv.reshape(8, 128).T)
        ct = ctab["prompt" if core < 4 else "sample"]
        parts = {"ident": ct["ident"], "ropec": ct["ropec"].reshape(128, -1), "ropes": ct["ropes"].reshape(128, -1), "dbias": ct["dbias"],
                 "tsm": ct["tsm"], "rel": ct["rel"], "qrow": ct["qrow"], "bmodc": bmodc, "bgatec": bgatec, "cvec": cvec}
        m["tabsA"] = np.concatenate([parts[k] for k in ("ident", "ropec", "ropes", "dbias", "tsm", "rel", "qrow", "bmodc", "bgatec", "cvec")], axis=1)
        m["wmask"] = ct["wmask"]
        maps.append({k: np.ascontiguousarray(v, dtype=np.float32) for k, v in m.items()})
    return maps


def assemble(results):
    f = np.float32
    y_prompt = np.concatenate([results[c]["y"].reshape(4, 256, D) for c in range(4)], axis=0)
    y_sample = np.stack([results[4 + b]["y"] for b in range(4)], axis=0)

    def gath(name, tail):
        parts = []
        for c in range(4):
            a = results[c][name].reshape(DEPTH, 4, 256, *tail).transpose(1, 0, 2, *range(3, 3 + len(tail)))
            parts.append(a)
        return np.ascontiguousarray(np.concatenate(parts, axis=0).astype(f))

    ndk = gath("nk", (4, 128))
    ndv = gath("nv", (4, 128))
    nwk = gath("nwk", (2, 64))
    nwv = gath("nwv", (2, 64))
    st = []
    for c in range(4):
        a = results[c]["nst"].reshape(DEPTH, 2, 4, 2, 64, 2, 128)
        a = a.transpose(2, 0, 1, 5, 3, 4, 6).reshape(4, DEPTH, 2, 4, 64, 128)
        st.append(a)
    nst = np.ascontiguousarray(np.concatenate(st, axis=0).astype(f))
    return (y_prompt.astype(f), y_sample.astype(f), ndk, ndv, nwk, nwv, nst)


def kernel(**inputs):
    nc, _ = _get_prog(False)
    maps = make_in_maps(inputs)
    res = run_bass_kernel_spmd(nc, maps, core_ids=list(range(8)))
    return assemble(res.results)
```

```python
import contextlib
import numpy as np
import concourse.bass as bass
import concourse.mybir as mybir
from concourse.bass_utils import run_bass_kernel_spmd

F32 = mybir.dt.float32
BF16 = mybir.dt.bfloat16
AF = mybir.ActivationFunctionType
ALU = mybir.AluOpType
AX = mybir.AxisListType

PHASES = []
DEPTH = 2
T = 1024
NT = 8
D = 1024
LN_EPS = 1e-5
ALPHA = (2 * DEPTH) ** 0.25
NEG = -30000.0
NW = 3


class Atom:
    def __init__(s, name, excl=False):
        s.name = name
        s.lw = None
        s.rd = []
        s.excl = excl
        s.dcnt = 0


class Buf:
    def __init__(s, ap, atoms):
        s.ap = ap
        s.atoms = atoms

    def __getitem__(s, idx):
        return s.ap[idx]


class Rec:
    def __init__(s):
        s.calls = []

    def __getattr__(s, name):
        def f(*a, **k):
            s.calls.append((name, a, k))
            return s
        return f


class Trk:
    def __init__(s, nc, es):
        s.nc = nc
        s.es = es
        s.ops = {k: [] for k in ("pe", "act", "dve", "pool", "sp")}
        s.cnt = {k: 0 for k in s.ops}
        s.seen = {k: {} for k in s.ops}
        s.sems = {}
        s.stores = {}
        for k in ("pe", "act", "dve", "pool"):
            s.sems[k] = es.enter_context(nc.semaphore("s_" + k))

    def dsem(s, atom):
        key = "d_" + atom.name
        if key not in s.sems:
            s.sems[key] = s.es.enter_context(s.nc.semaphore(key))
        return key

    def _waits(s, eng, reads, writes):
        deps = []
        for b in reads:
            for a in b.atoms:
                if a.lw is not None:
                    deps.append(a.lw)
                if a.excl:
                    deps.extend(a.rd)
        for b in writes:
            for a in b.atoms:
                if a.lw is not None:
                    deps.append(a.lw)
                deps.extend(a.rd)
        best = {}
        for (k, v) in deps:
            if k == "pe" and eng == "pe":
                continue
            if v > best.get(k, 0):
                best[k] = v
        out = []
        for k, v in best.items():
            if s.seen[eng].get(k, 0) >= v:
                continue
            s.seen[eng][k] = v
            out.append((k, v))
        return out

    def _mark(s, tok, reads, writes):
        for b in reads:
            for a in b.atoms:
                if a.excl:
                    a.lw = tok
                    a.rd = []
                else:
                    a.rd.append(tok)
        for b in writes:
            for a in b.atoms:
                a.lw = tok
                a.rd = []

    def op(s, eng, fn, reads=(), writes=()):
        w = s._waits(eng, reads, writes)
        s.cnt[eng] += 1
        tok = (eng, s.cnt[eng])
        rec = Rec()
        fn(rec)
        s.ops[eng].append((w, rec.calls, (eng, 1)))
        s._mark(tok, reads, writes)

    def dma(s, q, out, in_, buf, load, after=()):
        a0 = buf.atoms[0]
        key = s.dsem(a0)
        w = s._waits(q, tuple(after) if load else (buf,), (buf,) if load else ())
        a0.dcnt += 1
        tok = (key, 16 * a0.dcnt)
        s.ops[q].append((w, [("dma_start", (), dict(out=out, in_=in_))], (key, 16)))
        s._mark(tok, () if load else (buf,), (buf,) if load else ())
        if not load:
            s.stores[key] = 16 * a0.dcnt

    def emit(s):
        nc = s.nc
        fin = [(k, v) for k, v in s.stores.items() if s.seen["sp"].get(k, 0) < v]
        with nc.Block() as block:
            def run(e, name, final=None):
                for (w, calls, inc) in s.ops[name]:
                    for (k, v) in w:
                        e.wait_ge(s.sems[k], v)
                    ins = None
                    for (nm, a, kw) in calls:
                        ins = getattr(e, nm)(*a, **kw)
                    ins.then_inc(s.sems[inc[0]], inc[1])
                if final:
                    for (k, v) in final:
                        e.wait_ge(s.sems[k], v)

            @block.tensor
            def _(e):
                run(e, "pe")

            @block.scalar
            def _(e):
                run(e, "act")

            @block.vector
            def _(e):
                run(e, "dve")

            @block.gpsimd
            def _(e):
                run(e, "pool")

            @block.sync
            def _(e):
                run(e, "sp", fin)


def _build(debug=False, stop=None, plan=None):
    wreqs = []
    nc = bass.Bass("TRN2", target_bir_lowering=False)
    dbg_out = {}
    with contextlib.ExitStack() as es:
        tk = Trk(nc, es)

        def din(name, shape, dt=F32):
            return nc.dram_tensor(name, list(shape), dt, kind="ExternalInput").ap()

        def dout(name, shape):
            return nc.dram_tensor(name, list(shape), F32, kind="ExternalOutput").ap()

        x_d = din("x", [T, D])
        cdk_d = din("cdk", [DEPTH, 256, 512])
        cdv_d = din("cdv", [DEPTH, 256, 512])
        cwk_d = din("cwk", [DEPTH, 256, 128])
        cwv_d = din("cwv", [DEPTH, 256, 128])
        stin_d = din("stin", [DEPTH, 2, 128, 256])
        TABS = (("ident", 128), ("ropec", 512), ("ropes", 512), ("dbias", 40), ("tsm", 32), ("rel", 512), ("qrow", 256),
                ("bmodc", DEPTH * 48), ("bgatec", DEPTH * 24), ("cvec", 8))
        NTA = sum(n for _, n in TABS)
        tabsA_d = din("tabsA", [128, NTA])
        wmask_d = din("wmask", [128, 2 * 8 * 128])
        PV = (("diff_lam", 256), ("diff_norm_g", 128), ("win_sink", 8), ("ret_decay", 8), ("ret_norm_g", 128))
        NPV = DEPTH * sum(n for _, n in PV)
        pvec_d = din("pvec", [1, NPV])
        bmod_d = din("b_mod", [DEPTH, 6 * D])
        ln_d = {n: din(n, [DEPTH, D]) for n in ("ln1_g", "ln1_b", "ln2_g", "ln2_b")}
        wmod_d = din("w_mod", [DEPTH, D, 6 * D])
        win_d = din("w_in", [DEPTH, D, 3840])
        wgate_d = din("w_gate", [DEPTH, D, 3 * D])
        wp_d = [din(n, [DEPTH, 512, D]) for n in ("w_pa", "w_pb", "w_pc")]
        wo_d = din("w_o", [DEPTH, D, D])
        wff1_d = din("w_ff1", [DEPTH, D, 4 * D])
        wff2_d = din("w_ff2", [DEPTH, 4 * D, D])
        y_d = dout("y", [T, D])
        nk_d = dout("nk", [DEPTH, T, 512])
        nv_d = dout("nv", [DEPTH, T, 512])
        nwk_d = dout("nwk", [DEPTH, T, 128])
        nwv_d = dout("nwv", [DEPTH, T, 128])
        nst_d = dout("nst", [DEPTH, 2, 4, 128, 256])

        cnt = [0]

        def sb(shape, dt, name=None, excl=False):
            cnt[0] += 1
            name = "sb_" + (name or ("b%d" % cnt[0]))
            t = es.enter_context(nc.sbuf_tensor(name, list(shape), dt))
            return Buf(t.ap(), [Atom(name, excl)])

        xb = sb([128, NT, D], F32, "x")
        xa = [Atom("x%d" % i) for i in range(NT)]
        xb.atoms = xa
        xt = [Buf(xb.ap[:, i, :], [xa[i]]) for i in range(NT)]
        hT = sb([128, 8, T], BF16, "hT")
        AR_N = 4096 + 5120 + 5184 + 5120 + 5120 + 12288
        ar_t = es.enter_context(nc.sbuf_tensor("arena", [128, AR_N], BF16))
        ar = ar_t.ap()
        offs = {}
        o = 0
        for nm, sz in (("qT", 4096), ("kT", 5120), ("vTM", 5184), ("E0", 5120), ("E1a", 2560), ("E1b", 2560), ("oT", 12288)):
            offs[nm] = (o, sz)
            o += sz
        A_at = {nm: Atom("ar_" + nm) for nm in offs}

        def arv(nm, a=0, n=None):
            o0, sz = offs[nm]
            n = sz - a if n is None else n
            return ar[:, o0 + a:o0 + a + n]

        qT = Buf(arv("qT").rearrange("p (c t) -> p c t", c=4), [A_at["qT"]])
        kT = Buf(arv("kT").rearrange("p (c t) -> p c t", c=4), [A_at["kT"]])
        vTM = Buf(arv("vTM"), [A_at["vTM"]])
        Eb = [Buf(arv("E0").rearrange("p (j q) -> p j q", j=10), [A_at["E0"]]),
              Buf(ar[:, offs["E1a"][0]:offs["E1a"][0] + 5120].rearrange("p (j q) -> p j q", j=10), [A_at["E1a"], A_at["E1b"]])]
        pslots = [Buf(arv("E1a", 0, 2048), [A_at["E1a"]]), Buf(arv("E1b", 0, 2048), [A_at["E1b"]])]
        oT = Buf(arv("oT").rearrange("p (b c t) -> p b c t", b=3, c=4), [A_at["oT"]])
        mergedT = Buf(ar[:, 0:8192].rearrange("p (c t) -> p c t", c=8), [A_at["qT"], A_at["kT"]])
        macc = Buf(ar[:, 9216:9216 + 8192].bitcast(F32).rearrange("p (c t) -> p c t", c=4),
                   [A_at["vTM"], A_at["E0"]])
        fT = Buf(ar[:, 0:32768].rearrange("p (c t) -> p c t", c=32), [A_at[n] for n in offs])
        rQxf = Buf(arv("E0", 0, 2048).rearrange("p (c t) -> p c t", c=2), [A_at["E0"]])
        rQxb = Buf(arv("E0", 2048, 2048).rearrange("p (c t) -> p c t", c=2), [A_at["E0"]])
        rSall = Buf(ar[:, offs["E1a"][0]:offs["E1a"][0] + 4096].rearrange("p (d n r e) -> p d n r e", d=2, n=8, r=2),
                    [A_at["E1a"], A_at["E1b"]])
        rKzf = Buf(arv("qT", 2048, 2048).rearrange("p (n f) -> p n f", n=8), [A_at["qT"]])
        rKzb = Buf(arv("kT", 2560, 2048).rearrange("p (n f) -> p n f", n=8), [A_at["kT"]])

        wslots = []
        for i in range(NW):
            wslots.append(sb([128, 4096], BF16, "w%d" % i))
        rowtab = sb([128, 4, D], F32, "rowtab")
        rt_at = [Atom("rt%d" % i) for i in range(4)]
        rowtab.atoms = rt_at
        rts = [Buf(rowtab.ap[:, i, :], [rt_at[i]]) for i in range(4)]
        cgs_all = Buf(rowtab.ap[:, 2:4, :].rearrange("p a d -> p (a d)").bitcast(BF16).rearrange("p (n f) -> p n f", n=8),
                      [rt_at[2], rt_at[3]])
        TAB = Atom("TAB")
        tabsA = sb([128, NTA], F32, "tabsA")
        tabsA.atoms = [TAB]
        tviews = {}
        o_ = 0
        for nm, n in TABS:
            tviews[nm] = Buf(tabsA.ap[:, o_:o_ + n], [TAB])
            o_ += n
        identf = tviews["ident"]
        ropec = Buf(tviews["ropec"].ap.rearrange("p (t f) -> p t f", t=8), [TAB])
        ropes = Buf(tviews["ropes"].ap.rearrange("p (t f) -> p t f", t=8), [TAB])
        dbias, tsm, rel, qrow, bmodc, bgatec, cvec = (tviews[k] for k in ("dbias", "tsm", "rel", "qrow", "bmodc", "bgatec", "cvec"))
        PVA = Atom("PVEC")
        pvec = sb([128, NPV], F32, "pvec")
        pvec.atoms = [PVA]
        pviews = {}
        o_ = 0
        for nm, n in PV:
            pviews[nm] = Buf(pvec.ap[:, o_:o_ + DEPTH * n], [PVA])
            o_ += DEPTH * n
        dlam, dng, wsink, rdec, rng = (pviews[k] for k in ("diff_lam", "diff_norm_g", "win_sink", "ret_decay", "ret_norm_g"))
        identb = sb([128, 128], BF16, "identb")
        wmask = sb([128, 2, 8, 128], BF16, "wmaskb")
        silb = sb([128, 8], BF16, "silb")
        silf = sb([128, 8], F32, "silf")
        silrep = sb([128, 8, 128], BF16, "silrep")
        modcol = sb([128, 2, 48], F32, "modcol")
        sm = sb([128, 64], F32, "sm")
        gA = sb([128, 128], F32, "gA")
        sinkexp = sb([128, 8], F32, "sinkexp")
        lgb = sb([128, 8], F32, "lgb")
        lgcol = sb([128, 4], F32, "lgcol")
        gccar = sb([128, 2, 8, 2], F32, "gccar")
        carcol = tsm
        Dsum = sb([128, 4, 128], F32, "Dsum")
        xif = sb([128, 2, 128], F32, "xif")
        xib = sb([128, 2, 128], F32, "xib")
        zet = sb([128, 2, 4], F32, "zet")
        Sfp = [sb([128, 2, 128], F32, "Sfp%d" % i) for i in range(3)]
        stage = [sb([128, 512], F32, "stage%d" % i) for i in range(2)]
        tmpf = [sb([128, 512], F32, "tmpf%d" % i) for i in range(3)]
        tmpb = [sb([128, 512], BF16, "tmpb%d" % i) for i in range(3)]
        smallf = [sb([128, 16], F32, "smallf%d" % i) for i in range(4)]
        otm = [sb([128, 2, 512], BF16, "otm%d" % i) for i in range(2)]
        stage4 = stage + [Buf(o.ap.rearrange("p a b -> p (a b)").bitcast(F32), o.atoms) for o in otm]
        lnstats = [sb([128, 2, 6], F32, "lnstat%d" % i) for i in range(2)]
        lnmv = sb([128, 4], F32, "lnmv")
        lnmv2 = sb([128, 4], F32, "lnmv2")
        adt = [sb([128, 4, 128], BF16, "adt%d" % i) for i in range(2)]

        banks = []
        pairs = []
        for i in range(4):
            t = es.enter_context(nc.psum_tensor("ps%d" % i, [128, 1024], F32))
            a0, a1 = Atom("ps%da" % i, excl=True), Atom("ps%db" % i, excl=True)
            banks.append(Buf(t.ap()[:, 0:512], [a0]))
            banks.append(Buf(t.ap()[:, 512:1024], [a1]))
            pairs.append(Buf(t.ap(), [a0, a1]))
        bctr = [0]

        psm = banks[7]
        bsp = banks[6]

        def nb():
            b = banks[bctr[0] % 6]
            bctr[0] += 1
            return b

        def nb2():
            if bctr[0] % 2:
                bctr[0] += 1
            i = bctr[0] % 6
            bctr[0] += 2
            return banks[i], banks[i + 1], pairs[i // 2]

        rot = {}

        def nxt(lst, key):
            i = rot.get(key, 0)
            rot[key] = i + 1
            return lst[i % len(lst)]

        def V(fn, r, w):
            tk.op("dve", fn, r, w)

        def AC(fn, r, w):
            tk.op("act", fn, r, w)

        def PE(fns, r, w):
            tk.op("pe", lambda e, fs=fns: [f(e) for f in fs][-1], r, w)

        def mm(out, lhsT, rhs, start=True, stop=True):
            return lambda e: e.matmul(out, lhsT=lhsT, rhs=rhs, start=start, stop=stop)

        def tr(out, in_, ident):
            return lambda e: e.transpose(out, in_, ident)

        def load(q, out, in_, buf, after=()):
            tk.dma(q, out, in_, buf, True, after)

        def store(out, in_, buf):
            tk.dma("sp", out, in_, buf, False)

        wctr = [0]

        wdram = {"w_mod": wmod_d, "w_in": win_d, "w_gate": wgate_d, "w_pa": wp_d[0], "w_pb": wp_d[1], "w_pc": wp_d[2],
                 "w_o": wo_d, "w_ff1": wff1_d, "w_ff2": wff2_d}
        wissued = set()

        def w_issue(j, req):
            (dk, l_, r0, kc, c0, cols) = req
            slot = wslots[j % NW]
            view = slot.ap[:, 0:kc * cols].rearrange("p (k c) -> p k c", k=kc)
            src = wdram[dk][l_][r0:r0 + kc * 128, c0:c0 + cols].rearrange("(k p) n -> p k n", p=128)
            load("pool", view, src, slot)

        def W(dk, l_, r0, kc, c0, cols, deep=True):
            i = wctr[0]
            wctr[0] += 1
            req = (dk, l_, r0, kc, c0, cols)
            wreqs.append(req)
            if plan is None:
                w_issue(i, req)
            else:
                assert plan[i] == req
                for j in ((i, i + 1, i + 2) if deep else (i, i + 1)):
                    if j < len(plan) and j not in wissued:
                        wissued.add(j)
                        w_issue(j, plan[j])
            slot = wslots[i % NW]
            return slot, slot.ap[:, 0:kc * cols].rearrange("p (k c) -> p k c", k=kc)

        def dump(name, ap, shape, buf):
            if not debug:
                return
            d = nc.dram_tensor("dbg_" + name, list(shape), ap.dtype, kind="ExternalOutput").ap()
            dbg_out[name] = shape
            store(d, ap, buf)

        load("sp", tabsA.ap, tabsA_d, tabsA)
        load("sp", pvec.ap, pvec_d.partition_broadcast(128), pvec)
        load("pool", wmask.ap.rearrange("p a n q -> p (a n q)"), wmask_d, wmask)
        V(lambda e: e.tensor_copy(out=identb.ap, in_=identf.ap), [identf], [identb])
        AC(lambda e: e.activation(out=silf.ap, in_=cvec.ap, func=AF.Silu), [cvec], [silf])
        V(lambda e: e.tensor_copy(out=silb.ap, in_=silf.ap), [silf], [silb])
        V(lambda e: e.tensor_copy(out=silrep.ap, in_=silf.ap.unsqueeze(2).broadcast_to([128, 8, 128])), [silf], [silrep])

        def bc(ap, shape):
            return ap.broadcast_to(list(shape))

        modq = []

        def mod_tiles(l, vs):
            par = l % 2
            out = []
            for v in vs:
                for half in range(2):
                    if v in (0, 1, 3, 4):
                        def f(v=v, half=half):
                            slot, wv = W("w_mod", l, 0, 8, v * D + half * 512, 512)
                            fns = []
                            for j in range(4):
                                col = v * 8 + half * 4 + j
                                for k in range(8):
                                    fns.append(mm(psm.ap[:, col:col + 1], wv[:, k, j * 128:(j + 1) * 128], silb.ap[:, k:k + 1],
                                                  start=(k == 0), stop=(k == 7)))
                            PE(fns, [slot, silb], [psm])
                            c0 = v * 8 + half * 4
                            V(lambda e: e.tensor_tensor(out=modcol.ap[:, par, c0:c0 + 4], in0=psm.ap[:, c0:c0 + 4],
                                                        in1=bmodc.ap[:, l * 48 + c0:l * 48 + c0 + 4], op=ALU.add), [psm, bmodc], [modcol])
                            if v in (1, 4):
                                V(lambda e: e.tensor_scalar(out=modcol.ap[:, par, c0:c0 + 4], in0=modcol.ap[:, par, c0:c0 + 4],
                                                            scalar1=1.0, scalar2=None, op0=ALU.add), [modcol], [modcol])
                    else:
                        def f(v=v, half=half):
                            rt = rts[0] if v == 2 else rts[1]
                            if half == 0:
                                load("sp", rt.ap, bmod_d[l:l + 1, v * D:(v + 1) * D].partition_broadcast(128), rt)
                            slot, wv = W("w_mod", l, 0, 8, v * D + half * 512, 512)
                            ps = nb()
                            PE([mm(ps.ap, silrep.ap[:, k, :], wv[:, k, :], start=(k == 0), stop=(k == 7)) for k in range(8)],
                               [slot, silrep], [ps])
                            V(lambda e: e.tensor_tensor(out=rt.ap[:, half * 512:(half + 1) * 512], in0=ps.ap,
                                                        in1=rt.ap[:, half * 512:(half + 1) * 512], op=ALU.add), [ps, rt], [rt])
                    out.append(f)
            return out

        def mod_hook():
            if modq:
                modq.pop(0)()

        def mod_drain():
            while modq:
                modq.pop(0)()

        def make_hT(par, scv, shv):
            for tg in range(2):
                for k in range(8):
                    ps = nb()
                    PE([tr(ps.ap[:, j * 128:(j + 1) * 128], xb.ap[:, tg * 4 + j, k * 128:(k + 1) * 128], identf.ap)
                        for j in range(4)], [xt[tg * 4 + j] for j in range(4)] + [identf], [ps])
                    AC(lambda e, ps=ps, k=k, tg=tg: e.activation(
                        out=hT.ap[:, k, tg * 512:(tg + 1) * 512], in_=ps.ap, func=AF.Identity,
                        scale=modcol.ap[:, par, scv * 8 + k:scv * 8 + k + 1], bias=modcol.ap[:, par, shv * 8 + k:shv * 8 + k + 1]),
                       [ps, modcol], [hT])

        def zproj(slot, wv, t, width):
            ps = nb()
            PE([mm(ps.ap[:, 0:width], hT.ap[:, k, t * 128:(t + 1) * 128], wv[:, k, 0:width], start=(k == 0), stop=(k == 7))
                for k in range(8)], [slot, hT], [ps])
            return ps

        def rope(ps, c0, nsub, t, outap, outbuf, dup=False, cp=None):
            n = nsub * 64
            t1 = nxt(tmpf, "tmpf")
            t2 = nxt(tmpf, "tmpf")
            src = ps.ap[:, c0:c0 + n]
            V(lambda e: e.tensor_tensor(out=t1.ap[:, 0:n].rearrange("p (s f) -> p s f", f=64),
                                        in0=src.rearrange("p (s f) -> p s f", f=64),
                                        in1=bc(ropec.ap[:, t:t + 1, :], [128, nsub, 64]), op=ALU.mult),
              [ps, ropec], [t1])
            s5 = src.rearrange("p (s r u i) -> p s r u i", r=2, u=2, i=16)
            o5 = t2.ap[:, 0:n].rearrange("p (s r u i) -> p s r u i", r=2, u=2, i=16)
            sn = ropes.ap[:, t, :].rearrange("p (r u i) -> p r u i", r=2, u=2)
            for u in range(2):
                V(lambda e, u=u: e.tensor_tensor(
                    out=o5[:, :, :, u, :], in0=s5[:, :, :, 1 - u, :],
                    in1=bc(sn[:, :, u, :].unsqueeze(1), [128, nsub, 2, 16]), op=ALU.mult), [ps, ropes], [t2])
            if dup:
                V(lambda e: e.tensor_tensor(
                    out=outap.rearrange("p (s d f) -> p s d f", d=2, f=64),
                    in0=bc(t1.ap[:, 0:n].rearrange("p (s f) -> p s f", f=64).unsqueeze(2), [128, nsub, 2, 64]),
                    in1=bc(t2.ap[:, 0:n].rearrange("p (s f) -> p s f", f=64).unsqueeze(2), [128, nsub, 2, 64]),
                    op=ALU.add), [t1, t2], [outbuf])
            else:
                V(lambda e: e.tensor_tensor(out=outap, in0=t1.ap[:, 0:n], in1=t2.ap[:, 0:n], op=ALU.add),
                  [t1, t2], [outbuf])

        def to_fm(src_buf, src_ap, nchunks, dst_buf, dst_ap, on_dve=False):
            ps = nb()
            pb = ps.ap.bitcast(BF16)
            PE([tr(pb[:, j * 128:(j + 1) * 128], src_ap[:, j * 128:(j + 1) * 128], identb.ap) for j in range(nchunks)],
               [src_buf, identb], [ps])
            if on_dve:
                V(lambda e: e.tensor_copy(out=dst_ap, in_=pb[:, 0:nchunks * 128].rearrange("p (c t) -> p c t", c=nchunks)), [ps], [dst_buf])
            else:
                AC(lambda e: e.activation(out=dst_ap, in_=pb[:, 0:nchunks * 128].rearrange("p (c t) -> p c t", c=nchunks),
                                          func=AF.Copy), [ps], [dst_buf])

        def rope2(ps, c0, nsub, t, dup=False):
            n = nsub * 64
            nd = n * (2 if dup else 1)
            buf = nxt(tmpf, "tmpf")
            bb = buf.ap.bitcast(BF16)
            A = bb[:, 0:nd]
            B = bb[:, 512:512 + nd]
            src = ps.ap[:, c0:c0 + n]
            s3 = src.rearrange("p (s f) -> p s f", f=64)
            s5 = src.rearrange("p (s r u i) -> p s r u i", r=2, u=2, i=16)
            sn = ropes.ap[:, t, :].rearrange("p (r u i) -> p r u i", r=2, u=2)
            for d in range(2 if dup else 1):
                if dup:
                    Ad = A.rearrange("p (s d f) -> p s d f", d=2, f=64)[:, :, d, :]
                    Bd = B.rearrange("p (s d r u i) -> p s d r u i", d=2, r=2, u=2, i=16)[:, :, d]
                else:
                    Ad = A.rearrange("p (s f) -> p s f", f=64)
                    Bd = B.rearrange("p (s r u i) -> p s r u i", r=2, u=2, i=16)
                V(lambda e, Ad=Ad: e.tensor_tensor(out=Ad, in0=s3, in1=bc(ropec.ap[:, t:t + 1, :], [128, nsub, 64]), op=ALU.mult),
                  [ps, ropec], [buf])
                for u in range(2):
                    V(lambda e, u=u, Bd=Bd: e.tensor_tensor(
                        out=Bd[:, :, :, u, :], in0=s5[:, :, :, 1 - u, :],
                        in1=bc(sn[:, :, u, :].unsqueeze(1), [128, nsub, 2, 16]), op=ALU.mult), [ps, ropes], [buf])
            return buf, A, B

        def to_fm2(buf, A, B, nchunks, dst_buf, dst_ap):
            ps = nb()
            fns = []
            for j in range(nchunks):
                o_ = ps.ap[:, j * 128:(j + 1) * 128]
                fns.append(mm(o_, A[:, j * 128:(j + 1) * 128], identb.ap, start=True, stop=False))
                fns.append(mm(o_, B[:, j * 128:(j + 1) * 128], identb.ap, start=False, stop=True))
            PE(fns, [buf, identb], [ps])
            AC(lambda e: e.activation(out=dst_ap, in_=ps.ap[:, 0:nchunks * 128].rearrange("p (c t) -> p c t", c=nchunks), func=AF.Copy),
               [ps], [dst_buf])

        def to_fm2_dup(buf, A, B, dst_buf, dst_ap):
            ps = nb()
            fns = []
            for g_ in range(2):
                for d_ in range(2):
                    o_ = ps.ap[d_ * 64:(d_ + 1) * 64, g_ * 128:(g_ + 1) * 128]
                    fns.append(mm(o_, A[:, g_ * 64:(g_ + 1) * 64], identb.ap, start=True, stop=False))
                    fns.append(mm(o_, B[:, g_ * 64:(g_ + 1) * 64], identb.ap, start=False, stop=True))
            PE(fns, [buf, identb], [ps])
            AC(lambda e: e.activation(out=dst_ap, in_=ps.ap[:, 0:256].rearrange("p (c t) -> p c t", c=2), func=AF.Copy), [ps], [dst_buf])

        deferred = []
        obuf_ctr = [0]

        def defer(fn):
            deferred.append(fn)

        def flush(keep=0):
            while len(deferred) > keep:
                deferred.pop(0)()

        def rsqrt(out_ap, in_ap, scale, rbufs, wbufs):
            AC(lambda e: e.activation(out=out_ap, in_=in_ap, func=AF.Ln, scale=scale, bias=sm.ap[:, 8:9]), list(rbufs) + [sm], wbufs)
            AC(lambda e: e.activation(out=out_ap, in_=out_ap, func=AF.Exp, scale=-0.5), wbufs, wbufs)

        def out_fp32(ps, c0, n, dram_ap):
            st = nxt(stage4, "stage4")
            AC(lambda e: e.activation(out=st.ap[:, 0:n], in_=ps.ap[:, c0:c0 + n], func=AF.Copy), [ps], [st])
            store(dram_ap, st.ap[:, 0:n], st)
            return st

        def mixer_A(l, lam_init):
            tsl = slice(None)
            lv = dlam.ap[:, l * 256:(l + 1) * 256]
            s = sm
            V(lambda e: e.tensor_tensor(out=tmpf[0].ap[:, 0:64], in0=lv[:, 0:64], in1=lv[:, 64:128], op=ALU.mult), [dlam], [tmpf[0]])
            V(lambda e: e.tensor_reduce(out=s.ap[:, 0:1], in_=tmpf[0].ap[:, 0:64], axis=AX.X, op=ALU.add), [tmpf[0]], [s])
            V(lambda e: e.tensor_tensor(out=tmpf[0].ap[:, 0:64], in0=lv[:, 128:192], in1=lv[:, 192:256], op=ALU.mult), [dlam], [tmpf[0]])
            V(lambda e: e.tensor_reduce(out=s.ap[:, 1:2], in_=tmpf[0].ap[:, 0:64], axis=AX.X, op=ALU.add), [tmpf[0]], [s])
            AC(lambda e: e.activation(out=s.ap[:, 2:4], in_=s.ap[:, 0:2], func=AF.Exp), [s], [s])
            V(lambda e: e.tensor_tensor(out=s.ap[:, 4:5], in0=s.ap[:, 3:4], in1=s.ap[:, 2:3], op=ALU.subtract), [s], [s])
            V(lambda e: e.tensor_scalar(out=s.ap[:, 5:6], in0=s.ap[:, 4:5], scalar1=-lam_init, scalar2=None, op0=ALU.add), [s], [s])
            V(lambda e: e.tensor_scalar(out=gA.ap, in0=dng.ap[:, l * 128:(l + 1) * 128], scalar1=(1.0 - lam_init), scalar2=None,
                                        op0=ALU.mult), [dng], [gA])
            v4 = vTM.ap[:, 0:5160].rearrange("p (j h e) -> p j h e", j=10, h=4)
            V(lambda e: e.memset(v4[:, :, :, 128:129], 1.0), [], [vTM])
            for j in range(2):
                st = nxt(stage, "stage")
                load("sp", st.ap, cdk_d[l, j * 128:(j + 1) * 128, :], st)
                tb = nxt(tmpb, "tmpb")
                V(lambda e, st=st, tb=tb: e.tensor_copy(out=tb.ap, in_=st.ap), [st], [tb])
                to_fm(tb, tb.ap, 4, kT, kT.ap[:, :, 1024 + j * 128:1024 + (j + 1) * 128])
                st2 = nxt(stage, "stage")
                load("sp", st2.ap, cdv_d[l, j * 128:(j + 1) * 128, :], st2)
                V(lambda e, st2=st2, j=j: e.tensor_copy(out=v4[:, 8 + j, :, 0:128],
                                                        in_=st2.ap.rearrange("p (h e) -> p h e", h=4)), [st2], [vTM])
            for gi in range(3):
                mod_hook()
                slot, wv = W("w_in", l, 0, 8, gi * 512, 512)
                for t in range(NT):
                    ps = zproj(slot, wv, t, 512)
                    if gi != 1:
                        flush()
                    if gi == 0:
                        rb, rA, rB = rope2(ps, 0, 8, t)
                        defer(lambda rb=rb, rA=rA, rB=rB, t=t: to_fm2(rb, rA, rB, 4, qT, qT.ap[:, :, t * 128:(t + 1) * 128]))
                    elif gi == 1:
                        out_fp32(ps, 0, 512, nk_d[l, t * 128:(t + 1) * 128, :])
                        rb, rA, rB = rope2(ps, 0, 8, t)
                        flush()
                        defer(lambda rb=rb, rA=rA, rB=rB, t=t: to_fm2(rb, rA, rB, 4, kT, kT.ap[:, :, t * 128:(t + 1) * 128]))
                    else:
                        out_fp32(ps, 0, 512, nv_d[l, t * 128:(t + 1) * 128, :])
                        V(lambda e, ps=ps, t=t: e.tensor_copy(out=v4[:, t, :, 0:128],
                                                              in_=ps.ap.rearrange("p (h e) -> p h e", h=4)), [ps], [vTM])
            flush()
            if debug and l == 0:
                dump("qTa", qT.ap, [128, 4, 1024], qT)
                dump("kTa", kT.ap, [128, 4, 1280], kT)
            if stop == "A0":
                return

            def qk_steps(r, h, E):
                return [lambda jp=jp: qk_step(r, h, E, jp) for jp in range(5)]

            def qk(r, h, E):
                for f in qk_steps(r, h, E):
                    f()

            def qk_step(r, h, E, jp):
                if True:
                    b0, b1, pr = nb2()
                    PE([mm((b0, b1)[m].ap[:, jj * 256:(jj + 1) * 256], kT.ap[m * 64:(m + 1) * 64, h, (2 * jp + jj) * 128:(2 * jp + jj + 1) * 128],
                           qT.ap[m * 64:(m + 1) * 64, h, r * 256:(r + 1) * 256]) for jj in range(2) for m in range(2)], [kT, qT], [b0, b1])
                    AC(lambda e, pr=pr, jp=jp: e.activation(
                        out=E.ap[:, 2 * jp:2 * jp + 2, :].rearrange("p j (m q) -> p m j q", m=2),
                        in_=pr.ap.rearrange("p (m j q) -> p m j q", m=2, j=2),
                        func=AF.Exp, scale=0.125, bias=dbias.ap[:, r * 10 + 2 * jp:r * 10 + 2 * jp + 1]), [pr, dbias], [E])

            def pv_groups(r, h, E):
                pacc = [bsp, psm]

                def grp(m, sblk):
                    PE([mm(pacc[m].ap[:, sblk * 129:(sblk + 1) * 129],
                           E.ap[:, j, m * 256 + sblk * 128:m * 256 + (sblk + 1) * 128],
                           v4[:, j, h, :], start=(j == 0), stop=(j == 9)) for j in range(10)], [E, vTM], [pacc[m]])
                return [lambda m=m, sblk=sblk: grp(m, sblk) for m in range(2) for sblk in range(2)]

            def pv(r, h, E, ot):
                pacc = [bsp, psm]
                sf = nxt(smallf, "smallf")
                pa3 = pacc[0].ap[:, 0:258].rearrange("p (s e) -> p s e", s=2)
                pb3 = pacc[1].ap[:, 0:258].rearrange("p (s e) -> p s e", s=2)
                V(lambda e: e.reciprocal(out=sf.ap[:, 0:2], in_=pa3[:, :, 128]), [pacc[0]], [sf])
                V(lambda e: e.reciprocal(out=sf.ap[:, 2:4], in_=pb3[:, :, 128]), [pacc[1]], [sf])
                V(lambda e: e.tensor_scalar(out=sf.ap[:, 2:4], in0=sf.ap[:, 2:4], scalar1=sm.ap[:, 5:6], scalar2=None, op0=ALU.mult),
                  [sf, sm], [sf])
                o1 = stage[obuf_ctr[0] % 2]
                obuf_ctr[0] += 1
                o2 = nxt(tmpf, "tmpf")
                o13 = o1.ap[:, 0:256].rearrange("p (s e) -> p s e", s=2)
                o23 = o2.ap[:, 0:256].rearrange("p (s e) -> p s e", s=2)
                V(lambda e: e.tensor_tensor(out=o13, in0=pa3[:, :, 0:128], in1=bc(sf.ap[:, 0:2].unsqueeze(2), [128, 2, 128]),
                                            op=ALU.mult), [pacc[0], sf], [o1])
                V(lambda e: e.tensor_tensor(out=o23, in0=pb3[:, :, 0:128], in1=bc(sf.ap[:, 2:4].unsqueeze(2), [128, 2, 128]),
                                            op=ALU.mult), [pacc[1], sf], [o2])
                V(lambda e: e.tensor_tensor(out=o1.ap[:, 0:256], in0=o1.ap[:, 0:256], in1=o2.ap[:, 0:256], op=ALU.add), [o1, o2], [o1])
                V(lambda e: e.tensor_tensor(out=o2.ap[:, 0:256], in0=o1.ap[:, 0:256], in1=o1.ap[:, 0:256], op=ALU.mult), [o1], [o2])
                V(lambda e: e.tensor_reduce(out=sf.ap[:, 4:6], in_=o23, axis=AX.X, op=ALU.add), [o2], [sf])

                def tail():
                    rsqrt(sf.ap[:, 8:10], sf.ap[:, 4:6], 1.0 / 128.0, [sf], [sf])
                    V(lambda e: e.tensor_tensor(out=o13, in0=o13, in1=bc(sf.ap[:, 8:10].unsqueeze(2), [128, 2, 128]), op=ALU.mult),
                      [o1, sf], [o1])
                    V(lambda e: e.tensor_tensor(out=ot.ap[:, :, h * 128:(h + 1) * 128], in0=o13,
                                                in1=bc(gA.ap.unsqueeze(1), [128, 2, 128]), op=ALU.mult), [o1, gA], [ot])
                    if h == 3:
                        for sblk in range(2):
                            qb = r * 2 + sblk
                            defer(lambda sblk=sblk, qb=qb: to_fm(ot, ot.ap[:, sblk, :], 4, oT, oT.ap[:, 0, :, qb * 128:(qb + 1) * 128], on_dve=True))
                return tail

            seq = [(r, h) for r in range(4) for h in range(4)]
            qk(seq[0][0], seq[0][1], Eb[0])
            pend_tail = None
            for i, (r, h) in enumerate(seq):
                qs = qk_steps(seq[i + 1][0], seq[i + 1][1], Eb[(i + 1) % 2]) if i + 1 < len(seq) else []
                pgs = pv_groups(r, h, Eb[i % 2])
                for k_ in range(5):
                    if k_ < len(qs):
                        qs[k_]()
                    if k_ == 2 and pend_tail is not None:
                        pend_tail()
                        pend_tail = None
                    if k_ < 4:
                        pgs[k_]()
                if pend_tail is not None:
                    pend_tail()
                ot = otm[r % 2]
                pend_tail = pv(r, h, Eb[i % 2], ot)
                flush()
            pend_tail()
            flush()
            flush()

        def mixer_B(l):
            v3 = vTM.ap[:, 0:1300].rearrange("p (j g e) -> p j g e", j=10, g=2)
            V(lambda e: e.memset(v3[:, :, :, 64:65], 1.0), [], [vTM])
            AC(lambda e: e.activation(out=sinkexp.ap, in_=wsink.ap[:, l * 8:(l + 1) * 8], func=AF.Exp), [wsink], [sinkexp])
            for j in range(2):
                st = nxt(stage, "stage")
                load("sp", st.ap[:, 0:128], cwk_d[l, j * 128:(j + 1) * 128, :], st)
                tb = nxt(tmpb, "tmpb")
                V(lambda e, st=st, tb=tb: e.tensor_copy(
                    out=tb.ap[:, 0:256].rearrange("p (s d f) -> p s d f", d=2, f=64),
                    in_=bc(st.ap[:, 0:128].rearrange("p (s f) -> p s f", f=64).unsqueeze(2), [128, 2, 2, 64])), [st], [tb])
                to_fm(tb, tb.ap, 2, kT, kT.ap[:, 0:2, 1024 + j * 128:1024 + (j + 1) * 128])
                st2 = nxt(stage, "stage")
                load("sp", st2.ap[:, 0:128], cwv_d[l, j * 128:(j + 1) * 128, :], st2)
                V(lambda e, st2=st2, j=j: e.tensor_copy(out=v3[:, 8 + j, :, 0:64],
                                                        in_=st2.ap[:, 0:128].rearrange("p (g e) -> p g e", g=2)), [st2], [vTM])
            mod_hook()
            slot, wv = W("w_in", l, 0, 8, 1536, 512)
            for t in range(NT):
                ps = zproj(slot, wv, t, 512)
                flush()
                rb, rA, rB = rope2(ps, 0, 8, t)
                defer(lambda rb=rb, rA=rA, rB=rB, t=t: to_fm2(rb, rA, rB, 4, qT, qT.ap[:, :, t * 128:(t + 1) * 128]))
            mod_hook()
            slot, wv = W("w_in", l, 0, 8, 2048, 256)
            for t in range(NT):
                ps = zproj(slot, wv, t, 256)
                out_fp32(ps, 0, 128, nwk_d[l, t * 128:(t + 1) * 128, :])
                out_fp32(ps, 128, 128, nwv_d[l, t * 128:(t + 1) * 128, :])
                rb, rA, rB = rope2(ps, 0, 2, t)
                flush()
                defer(lambda rb=rb, rA=rA, rB=rB, t=t: to_fm2_dup(rb, rA, rB, kT, kT.ap[:, 0:2, t * 128:(t + 1) * 128]))
                V(lambda e, ps=ps, t=t: e.tensor_copy(out=v3[:, t, :, 0:64],
                                                      in_=ps.ap[:, 128:256].rearrange("p (g e) -> p g e", g=2)), [ps], [vTM])
            flush()
            if debug and l == 0:
                dump("qTb", qT.ap, [128, 4, 1024], qT)
                dump("kTb", kT.ap, [128, 4, 1280], kT)

            def tiles_of(n):
                tl = []
                if n >= 1:
                    tl.append((n - 1, 0))
                tl.append((n, None))
                if n <= 6:
                    tl.append((n + 1, 1))
                tl.append((8, "c"))
                tl.append((9, "c"))
                return tl

            def groups_of(tl):
                own = [i for i, (ch, kind) in enumerate(tl) if kind != "c"]
                ctx = [i for i, (ch, kind) in enumerate(tl) if kind == "c"]
                return [own[i:i + 2] for i in range(0, len(own), 2)] + [ctx]

            def ecol(tl, ti, hh):
                for grp in groups_of(tl):
                    if ti in grp:
                        ng = len(grp)
                        return grp[0] * 512 + (hh % 2) * ng * 256 + grp.index(ti) * 256 + (hh // 2) * 128
                raise AssertionError

            def qk_steps(n, g, E):
                tl = tiles_of(n)
                Ef = E.ap.rearrange("p j q -> p (j q)")
                own = [i for i, (ch, kind) in enumerate(tl) if kind != "c"]
                ctx = [i for i, (ch, kind) in enumerate(tl) if kind == "c"]
                groups = [own[i:i + 2] for i in range(0, len(own), 2)] + [ctx]
                return [lambda grp=grp: qk_group(n, g, E, tl, Ef, grp) for grp in groups]

            def qk(n, g, E):
                for f in qk_steps(n, g, E):
                    f()

            def qk_group(n, g, E, tl, Ef, grp):
                if True:
                    isctx = tl[grp[0]][1] == "c"
                    ng = len(grp)
                    b0, b1, pr = nb2()
                    fns = []
                    for gi2, ti in enumerate(grp):
                        ch = tl[ti][0]
                        for i2 in range(2):
                            for half in range(2):
                                head = 4 * g + 2 * i2 + half
                                c = head // 2
                                fns.append(mm((b0, b1)[half].ap[:, (gi2 * 2 + i2) * 128:(gi2 * 2 + i2 + 1) * 128],
                                              kT.ap[half * 64:(half + 1) * 64, g, ch * 128:(ch + 1) * 128],
                                              qT.ap[half * 64:(half + 1) * 64, c, n * 128:(n + 1) * 128]))
                    PE(fns, [kT, qT], [b0, b1])
                    eo = Ef[:, grp[0] * 512:grp[0] * 512 + ng * 512].rearrange("p (h x) -> p h x", h=2)
                    pin = pr.ap.rearrange("p (h x) -> p h x", h=2)[:, :, 0:ng * 256]
                    if isctx:
                        AC(lambda e, pin=pin, eo=eo: e.activation(out=eo, in_=pin, func=AF.Exp, scale=0.125, bias=tsm.ap[:, 0:1]),
                           [pr, tsm], [E])
                    else:
                        AC(lambda e, pin=pin, eo=eo: e.activation(out=eo, in_=pin, func=AF.Exp, scale=0.125), [pr], [E])
                    for gi2, ti in enumerate(grp):
                        kind = tl[ti][1]
                        if kind is not None and kind != "c":
                            blk = Ef[:, grp[0] * 512:grp[0] * 512 + ng * 512].rearrange("p (h x) -> p h x", h=2)[
                                :, :, gi2 * 256:(gi2 + 1) * 256].rearrange("p h (i q) -> p h i q", i=2)
                            V(lambda e, blk=blk, kind=kind: e.tensor_tensor(
                                out=blk, in0=blk, in1=bc(wmask.ap[:, kind, n:n + 1, :].unsqueeze(1), [128, 2, 2, 128]), op=ALU.mult),
                              [E, wmask], [E])

            def pv_groups(n, g, E, pacc):
                tl = tiles_of(n)

                def grp(h2):
                    fns = []
                    for hh in (2 * h2, 2 * h2 + 1):
                        for ti, (ch, kind) in enumerate(tl):
                            c0 = ecol(tl, ti, hh)
                            fns.append(mm(pacc.ap[:, hh * 65:(hh + 1) * 65], E.ap.rearrange("p j q -> p (j q)")[:, c0:c0 + 128],
                                          v3[:, ch, g, :], start=(ti == 0), stop=(ti == len(tl) - 1)))
                    PE(fns, [E, vTM], [pacc])
                return [lambda h2=h2: grp(h2) for h2 in range(2)]

            def pv(n, g, E, ot, pacc):
                tl = tiles_of(n)
                sf = nxt(smallf, "smallf")
                p3 = pacc.ap[:, 0:260].rearrange("p (h e) -> p h e", h=4)
                V(lambda e: e.tensor_tensor(out=sf.ap[:, 0:4], in0=p3[:, :, 64], in1=sinkexp.ap[:, 4 * g:4 * g + 4], op=ALU.add),
                  [pacc, sinkexp], [sf])
                V(lambda e: e.reciprocal(out=sf.ap[:, 4:8], in_=sf.ap[:, 0:4]), [sf], [sf])
                V(lambda e: e.tensor_tensor(out=ot.ap[:, 0, g * 256:(g + 1) * 256].rearrange("p (h e) -> p h e", h=4),
                                            in0=p3[:, :, 0:64], in1=bc(sf.ap[:, 4:8].unsqueeze(2), [128, 4, 64]), op=ALU.mult),
                  [pacc, sf], [ot])

            seq = [(n, g) for n in range(8) for g in range(2)]
            qk(0, 0, Eb[0])
            for i, (n, g) in enumerate(seq):
                qs = qk_steps(seq[i + 1][0], seq[i + 1][1], Eb[(i + 1) % 2]) if i + 1 < len(seq) else []
                pacc_i = (bsp, psm)[i % 2]
                pgs = pv_groups(n, g, Eb[i % 2], pacc_i)
                for k_ in range(max(len(qs), 2)):
                    if k_ < len(qs):
                        qs[k_]()
                    if k_ < 2:
                        pgs[k_]()
                ot = otm[n % 2]
                pv(n, g, Eb[i % 2], ot, pacc_i)
                flush()
                if g == 1:
                    defer(lambda ot=ot, n=n: to_fm(ot, ot.ap[:, 0, :], 4, oT, oT.ap[:, 1, :, n * 128:(n + 1) * 128], on_dve=True))
            flush()

        def mixer_C(l):
            rd = rdec.ap[:, l * 8:(l + 1) * 8]
            AC(lambda e: e.activation(out=lgb.ap, in_=rd, func=AF.Exp, scale=-1.0), [rdec], [lgb])
            AC(lambda e: e.activation(out=lgb.ap, in_=lgb.ap, func=AF.Ln, bias=sm.ap[:, 9:10]), [lgb, sm], [lgb])
            V(lambda e: e.tensor_scalar(out=lgb.ap, in0=lgb.ap, scalar1=-1.0, scalar2=None, op0=ALU.mult), [lgb], [lgb])
            lg4 = lgb.ap.rearrange("p (d r m) -> p d r m", d=2, r=2)
            lc3 = lgcol.ap.rearrange("p (d r) -> p d r", d=2)
            V(lambda e: e.tensor_copy(out=lc3[0:64, :, :], in_=lg4[0:64, :, :, 0]), [lgb], [lgcol])
            V(lambda e: e.tensor_copy(out=lc3[64:128, :, :], in_=lg4[64:128, :, :, 1]), [lgb], [lgcol])
            relf = rel.ap[:, 0:128]
            relb = rel.ap[:, 128:256]
            mf = rel.ap[:, 256:384]
            mb = rel.ap[:, 384:512]
            dtb = nxt(tmpf, "tmpf")
            dtv = dtb.ap.rearrange("p (h q) -> p h q", h=4)
            for h in range(4):
                AC(lambda e, h=h: e.activation(out=Dsum.ap[:, h, :], in_=relf, func=AF.Exp, scale=lgb.ap[:, h:h + 1]), [rel, lgb], [Dsum])
                AC(lambda e, h=h: e.activation(out=dtv[:, h, :], in_=relb, func=AF.Exp, scale=lgb.ap[:, 4 + h:5 + h]), [rel, lgb], [dtb])
            V(lambda e: e.tensor_tensor(out=Dsum.ap, in0=Dsum.ap, in1=bc(mf.unsqueeze(1), [128, 4, 128]), op=ALU.mult), [Dsum, rel], [Dsum])
            V(lambda e: e.tensor_tensor(out=dtv, in0=dtv, in1=bc(mb.unsqueeze(1), [128, 4, 128]), op=ALU.mult), [dtb, rel], [dtb])
            V(lambda e: e.tensor_tensor(out=Dsum.ap, in0=Dsum.ap, in1=dtv, op=ALU.add), [Dsum, dtb], [Dsum])
            for r in range(2):
                AC(lambda e, r=r: e.activation(out=xif.ap[:, r, :], in_=qrow.ap[:, 0:128], func=AF.Exp, scale=lgcol.ap[:, r:r + 1]),
                   [qrow, lgcol], [xif])
                AC(lambda e, r=r: e.activation(out=xib.ap[:, r, :], in_=qrow.ap[:, 128:256], func=AF.Exp, scale=lgcol.ap[:, 2 + r:3 + r]),
                   [qrow, lgcol], [xib])
            AC(lambda e: e.activation(out=zet.ap[:, 0, :], in_=lgb.ap[:, 0:4], func=AF.Exp, scale=tsm.ap[:, 17:18]), [lgb, tsm], [zet])
            AC(lambda e: e.activation(out=zet.ap[:, 1, :], in_=lgb.ap[:, 4:8], func=AF.Exp, scale=tsm.ap[:, 18:19]), [lgb, tsm], [zet])
            AC(lambda e: e.activation(out=sm.ap[:, 12:16], in_=lgcol.ap, func=AF.Exp, scale=128.0), [lgcol], [sm])
            car3 = tsm.ap[:, 1:17].rearrange("p (d n) -> p d n", d=2)
            V(lambda e: e.tensor_tensor(out=gccar.ap, in0=bc(sm.ap[:, 12:16].rearrange("p (d r) -> p d r", d=2).unsqueeze(2), [128, 2, 8, 2]),
                                        in1=bc(car3.unsqueeze(3), [128, 2, 8, 2]), op=ALU.mult), [sm, tsm], [gccar])
            mod_hook()
            slot, wv = W("w_in", l, 0, 8, 2304, 512)
            kz = {0: rKzf, 1: rKzb}
            for t in range(NT):
                ps = zproj(slot, wv, t, 512)
                tb = nxt(tmpb, "tmpb")
                AC(lambda e, ps=ps, tb=tb: e.activation(out=tb.ap[:, 0:256], in_=ps.ap[:, 0:256], func=AF.Copy), [ps], [tb])
                AC(lambda e, ps=ps, tb=tb: e.activation(out=tb.ap[:, 256:512], in_=ps.ap[:, 256:512], func=AF.Copy, scale=0.125), [ps], [tb])
                flush()
                defer(lambda tb=tb, t=t: to_fm(tb, tb.ap[:, 0:256], 2, qT, qT.ap[:, 0:2, t * 128:(t + 1) * 128]))
                defer(lambda tb=tb, t=t: to_fm(tb, tb.ap[:, 256:512], 2, kT, kT.ap[:, 0:2, t * 128:(t + 1) * 128]))
                for d in range(2):
                    V(lambda e, tb=tb, t=t, d=d: e.tensor_tensor(
                        out=kz[d].ap[:, t, :].rearrange("p (h f) -> p h f", h=4),
                        in0=tb.ap[:, 256:512].rearrange("p (h f) -> p h f", h=4),
                        in1=bc(zet.ap[:, d, :].unsqueeze(2), [128, 4, 64]), op=ALU.mult), [tb, zet], [kz[d]])
            slot, wv = W("w_in", l, 0, 8, 2816, 512)
            vr = vTM.ap[:, 0:4096].rearrange("p (n f) -> p n f", n=8)
            for t in range(NT):
                ps = zproj(slot, wv, t, 512)
                AC(lambda e, ps=ps, t=t: e.activation(out=vr[:, t, :], in_=ps.ap, func=AF.Copy), [ps], [vTM])
            flush()
            slot_g, wv_g = W("w_in", l, 0, 8, 3328, 512)
            cgq = []
            for t in range(NT):
                def cgf(t=t):
                    psg = zproj(slot_g, wv_g, t, 512)
                    AC(lambda e: e.activation(out=cgs_all.ap[:, t, :], in_=psg.ap, func=AF.Silu), [psg], [cgs_all])
                    V(lambda e: e.tensor_tensor(out=cgs_all.ap[:, t, :].rearrange("p (h e) -> p h e", h=4),
                                                in0=cgs_all.ap[:, t, :].rearrange("p (h e) -> p h e", h=4),
                                                in1=bc(rng.ap[:, l * 128:(l + 1) * 128].unsqueeze(1), [128, 4, 128]), op=ALU.mult),
                      [cgs_all, rng], [cgs_all])
                cgq.append(cgf)
            for (dst, xi) in ((rQxf, xif), (rQxb, xib)):
                for r in range(2):
                    V(lambda e, dst=dst, xi=xi, r=r: e.tensor_tensor(
                        out=dst.ap[:, r, :].rearrange("p (n q) -> p n q", n=8),
                        in0=qT.ap[:, r, :].rearrange("p (n q) -> p n q", n=8),
                        in1=bc(xi.ap[:, r:r + 1, :], [128, 8, 128]), op=ALU.mult), [qT, xi], [dst])
            for d in range(2):
                order = list(range(8)) if d == 0 else list(range(7, -1, -1))
                sprev = nxt(Sfp, "Sfp")
                load("sp", sprev.ap.rearrange("p r e -> p (r e)"), stin_d[l, d], sprev)
                for n in order:
                    V(lambda e, sprev=sprev, d=d, n=n: e.tensor_scalar(
                        out=rSall.ap[:, d, n], in0=sprev.ap, scalar1=tsm.ap[:, 1 + d * 8 + n:2 + d * 8 + n], scalar2=None,
                        op0=ALU.mult), [sprev, tsm], [rSall])
                    ps = nb()
                    fns = []
                    for r in range(2):
                        for m in range(2):
                            hd = 2 * r + m
                            fns.append(mm(ps.ap[m * 64:(m + 1) * 64, r * 128:(r + 1) * 128],
                                          kz[d].ap[:, n, hd * 64:(hd + 1) * 64], vr[:, n, hd * 128:(hd + 1) * 128]))
                    PE(fns, [kz[d], vTM], [ps])
                    snew = nxt(Sfp, "Sfp")
                    V(lambda e, sprev=sprev, snew=snew, d=d, n=n: e.tensor_tensor(
                        out=snew.ap, in0=sprev.ap, in1=bc(gccar.ap[:, d, n, :].unsqueeze(2), [128, 2, 128]), op=ALU.mult),
                      [sprev, gccar], [snew])
                    V(lambda e, snew=snew, ps=ps: e.tensor_tensor(
                        out=snew.ap, in0=snew.ap, in1=ps.ap[:, 0:256].rearrange("p (r e) -> p r e", r=2), op=ALU.add),
                      [snew, ps], [snew])
                    if (d == 0 and n % 2 == 1) or (d == 1 and n % 2 == 0):
                        store(nst_d[l, d, n // 2], snew.ap.rearrange("p r e -> p (r e)"), snew)
                    sprev = snew
                    if n % 2 == 1 and cgq:
                        cgq.pop(0)()
            while cgq:
                cgq.pop(0)()
            def qk_c(n):
                ad = adt[n % 2]
                for m in range(2):
                    psa = nb()
                    PE([mm(psa.ap[:, r * 128:(r + 1) * 128], kT.ap[m * 64:(m + 1) * 64, r, n * 128:(n + 1) * 128],
                           qT.ap[m * 64:(m + 1) * 64, r, n * 128:(n + 1) * 128]) for r in range(2)], [kT, qT], [psa])
                    V(lambda e, psa=psa, m=m: e.tensor_tensor(
                        out=ad.ap.rearrange("p (r t) q -> p r t q", t=2)[:, :, m, :],
                        in0=psa.ap[:, 0:256].rearrange("p (r q) -> p r q", r=2),
                        in1=Dsum.ap.rearrange("p (r t) q -> p r t q", t=2)[:, :, m, :], op=ALU.mult), [psa, Dsum], [ad])

            def pv_c(n):
                ad = adt[n % 2]
                psy = bsp
                fns = []
                for hd in range(4):
                    r, m = hd // 2, hd % 2
                    o_ = psy.ap[:, hd * 128:(hd + 1) * 128]
                    fns.append(mm(o_, ad.ap[:, hd, :], vr[:, n, hd * 128:(hd + 1) * 128], start=True, stop=False))
                    fns.append(mm(o_, rQxf.ap[m * 64:(m + 1) * 64, r, n * 128:(n + 1) * 128], rSall.ap[m * 64:(m + 1) * 64, 0, n, r, :],
                                  start=False, stop=False))
                    fns.append(mm(o_, rQxb.ap[m * 64:(m + 1) * 64, r, n * 128:(n + 1) * 128], rSall.ap[m * 64:(m + 1) * 64, 1, n, r, :],
                                  start=False, stop=True))
                PE(fns, [ad, vTM, rQxf, rQxb, rSall], [psy])
                return psy

            def ln_c(n, psy):
                y3 = psy.ap.rearrange("p (h e) -> p h e", h=4)
                sf = nxt(smallf, "smallf")
                sq = nxt(tmpf, "tmpf")
                t1 = nxt(tmpf, "tmpf")
                V(lambda e: e.tensor_reduce(out=sf.ap[:, 0:4], in_=y3, axis=AX.X, op=ALU.add), [psy], [sf])
                AC(lambda e: e.activation(out=sq.ap, in_=psy.ap, func=AF.Square), [psy], [sq])
                t13 = t1.ap.rearrange("p (h e) -> p h e", h=4)
                V(lambda e: e.tensor_scalar(out=sf.ap[:, 0:4], in0=sf.ap[:, 0:4], scalar1=1.0 / 128.0, scalar2=None, op0=ALU.mult), [sf], [sf])
                V(lambda e: e.tensor_tensor(out=t13, in0=y3, in1=bc(sf.ap[:, 0:4].unsqueeze(2), [128, 4, 128]), op=ALU.subtract),
                  [psy, sf], [t1])
                flush()
                V(lambda e: e.tensor_reduce(out=sf.ap[:, 4:8], in_=sq.ap.rearrange("p (h e) -> p h e", h=4), axis=AX.X, op=ALU.add),
                  [sq], [sf])
                V(lambda e: e.tensor_scalar(out=sf.ap[:, 4:8], in0=sf.ap[:, 4:8], scalar1=1.0 / 128.0, scalar2=None, op0=ALU.mult), [sf], [sf])
                V(lambda e: e.tensor_tensor(out=sf.ap[:, 8:12], in0=sf.ap[:, 0:4], in1=sf.ap[:, 0:4], op=ALU.mult), [sf], [sf])
                V(lambda e: e.tensor_tensor(out=sf.ap[:, 4:8], in0=sf.ap[:, 4:8], in1=sf.ap[:, 8:12], op=ALU.subtract), [sf], [sf])
                rsqrt(sf.ap[:, 12:16], sf.ap[:, 4:8], 1.0, [sf], [sf])
                V(lambda e: e.tensor_tensor(out=t13, in0=t13, in1=bc(sf.ap[:, 12:16].unsqueeze(2), [128, 4, 128]), op=ALU.mult),
                  [t1, sf], [t1])
                ot = otm[n % 2]
                V(lambda e: e.tensor_tensor(out=ot.ap[:, 0, :], in0=t1.ap, in1=cgs_all.ap[:, n, :], op=ALU.mult), [t1, cgs_all], [ot])
                defer(lambda: to_fm(ot, ot.ap[:, 0, :], 4, oT, oT.ap[:, 2, :, n * 128:(n + 1) * 128]))

            mod_drain()
            if l + 1 < DEPTH:
                modq.extend(mod_tiles(l + 1, (0, 1, 3, 4)))
            qk_c(0)
            for n in range(8):
                mod_hook()
                psy = pv_c(n)
                if n + 1 < 8:
                    qk_c(n + 1)
                flush()
                ln_c(n, psy)
            flush()

        lnm = [lnmv, lnmv2]

        def ln_load(l, gname, bname):
            load("sp", rts[2].ap, ln_d[gname][l:l + 1, :].partition_broadcast(128), rts[2])
            load("sp", rts[3].ap, ln_d[bname][l:l + 1, :].partition_broadcast(128), rts[3])

        def ln_stats(t):
            mv = lnm[t % 2]
            st = lnstats[t % 2]
            for hf in range(2):
                V(lambda e, hf=hf: e.bn_stats(out=st.ap[:, hf, :], in_=xb.ap[:, t, hf * 512:(hf + 1) * 512]), [xt[t]], [st])
            V(lambda e: e.bn_aggr(out=mv.ap[:, 0:2], in_=st.ap.rearrange("p a b -> p (a b)")), [st], [mv])
            rsqrt(mv.ap[:, 2:3], mv.ap[:, 1:2], 1.0, [mv], [mv])
            V(lambda e: e.scalar_tensor_tensor(out=mv.ap[:, 3:4], in0=mv.ap[:, 0:1], scalar=-1.0, in1=mv.ap[:, 2:3],
                                               op0=ALU.mult, op1=ALU.mult), [mv], [mv])
            AC(lambda e: e.activation(out=xb.ap[:, t, :], in_=xb.ap[:, t, :], func=AF.Identity, scale=mv.ap[:, 2:3],
                                      bias=mv.ap[:, 3:4]), [xt[t], mv], [xt[t]])

        def ln_affine(t):
            V(lambda e: e.tensor_tensor(out=xb.ap[:, t, :], in0=xb.ap[:, t, :], in1=rts[2].ap, op=ALU.mult), [xt[t], rts[2]], [xt[t]])
            V(lambda e: e.tensor_tensor(out=xb.ap[:, t, :], in0=xb.ap[:, t, :], in1=rts[3].ap, op=ALU.add), [xt[t], rts[3]], [xt[t]])

        def ln_step(t):
            ln_stats(t)
            if t >= 1:
                ln_affine(t - 1)
            if t == NT - 1:
                ln_affine(t)

        def merge_phase(l):
            ln_load(l, "ln1_g", "ln1_b")
            mod_drain()
            modq.extend(mod_tiles(l, (5,)))
            units = [(G, b) for G in range(2) for b in range(3)]

            def p_issue(i):
                G_, b_ = units[i]
                sl = pslots[i % 2]
                load("pool", sl.ap.rearrange("p (k c) -> p k c", k=4),
                     wp_d[b_][l][0:512, G_ * 512:(G_ + 1) * 512].rearrange("(k p) n -> p k n", p=128), sl)

            p_issue(0)
            for ui, (G, b) in enumerate(units):
                if True:
                    mod_hook()
                    gslot, gwv = W("w_gate", l, 0, 8, b * D + G * 512, 512)
                    if ui + 1 < len(units):
                        p_issue(ui + 1)
                    pslot = pslots[ui % 2]
                    pwv = pslot.ap.rearrange("p (k c) -> p k c", k=4)
                    for j in range(4):
                        col = l * 24 + b * 8 + G * 4 + j
                        for hf in range(2):
                            tsel = slice(hf * 512, (hf + 1) * 512)
                            psg = nb()
                            PE([mm(psg.ap, gwv[:, k, j * 128:(j + 1) * 128], hT.ap[:, k, tsel], start=(k == 0), stop=(k == 7))
                                for k in range(8)], [gslot, hT], [psg])
                            gt = nxt(tmpb, "tmpb")
                            AC(lambda e, psg=psg, gt=gt, col=col: e.activation(out=gt.ap, in_=psg.ap, func=AF.Sigmoid,
                                                                               bias=bgatec.ap[:, col:col + 1]), [psg, bgatec], [gt])
                            psp = nb()
                            PE([mm(psp.ap, pwv[:, k, j * 128:(j + 1) * 128], oT.ap[:, b, k, tsel], start=(k == 0), stop=(k == 3))
                                for k in range(4)], [pslot, oT], [psp])
                            if b == 0:
                                V(lambda e, psp=psp, gt=gt, j=j, tsel=tsel: e.tensor_tensor(out=macc.ap[:, j, tsel], in0=psp.ap, in1=gt.ap,
                                                                                           op=ALU.mult), [psp, gt], [macc])
                            else:
                                tmq = nxt(tmpf, "tmpf")
                                V(lambda e, psp=psp, gt=gt, tmq=tmq: e.tensor_tensor(out=tmq.ap, in0=psp.ap, in1=gt.ap, op=ALU.mult),
                                  [psp, gt], [tmq])
                                if b == 1:
                                    V(lambda e, tmq=tmq, j=j, tsel=tsel: e.tensor_tensor(out=macc.ap[:, j, tsel], in0=macc.ap[:, j, tsel],
                                                                                        in1=tmq.ap, op=ALU.add), [macc, tmq], [macc])
                                else:
                                    V(lambda e, tmq=tmq, j=j, tsel=tsel, G=G: e.tensor_tensor(
                                        out=mergedT.ap[:, G * 4 + j, tsel], in0=macc.ap[:, j, tsel], in1=tmq.ap, op=ALU.add),
                                      [macc, tmq], [mergedT])
            if debug and l == 0:
                dump("mergedT", mergedT.ap, [128, 8, 1024], mergedT)
            mod_drain()
            wo_t = [W("w_o", l, 0, 8, cg * 512, 512, deep=(cg == 0)) for cg in range(2)]
            for half in range(2):
                for cg in range(2):
                    slot, wv = wo_t[cg]
                    for t in range(half * 4, half * 4 + 4):
                        ps = nb()
                        PE([mm(ps.ap, mergedT.ap[:, k, t * 128:(t + 1) * 128], wv[:, k, :], start=(k == 0), stop=(k == 7)) for k in range(8)],
                           [slot, mergedT], [ps])
                        tmq = nxt(tmpf, "tmpf")
                        csl = slice(cg * 512, (cg + 1) * 512)
                        V(lambda e, ps=ps, tmq=tmq, csl=csl: e.tensor_tensor(out=tmq.ap, in0=ps.ap, in1=rts[0].ap[:, csl], op=ALU.mult),
                          [ps, rts[0]], [tmq])
                        V(lambda e, tmq=tmq, t=t, csl=csl: e.scalar_tensor_tensor(out=xb.ap[:, t, csl], in0=xb.ap[:, t, csl], scalar=ALPHA,
                                                                                  in1=tmq.ap, op0=ALU.mult, op1=ALU.add), [xt[t], tmq], [xt[t]])
                        if cg == 1:
                            ln_step(t)

        def mlp_phase(l):
            make_hT(l % 2, 4, 3)
            ln_load(l, "ln2_g", "ln2_b")
            if l + 1 < DEPTH:
                modq.extend(mod_tiles(l + 1, (2,)))
            for cgp in range(8):
                slot, wv = W("w_ff1", l, 0, 8, cgp * 512, 512)
                for j in range(4):
                    for hf in range(2):
                        ps = nb()
                        PE([mm(ps.ap, wv[:, k, j * 128:(j + 1) * 128], hT.ap[:, k, hf * 512:(hf + 1) * 512], start=(k == 0), stop=(k == 7))
                            for k in range(8)], [slot, hT], [ps])
                        tmq = nxt(tmpf, "tmpf")
                        AC(lambda e, ps=ps, tmq=tmq: e.activation(out=tmq.ap, in_=ps.ap, func=AF.Relu), [ps], [tmq])
                        V(lambda e, tmq=tmq, c=cgp * 4 + j, hf=hf: e.tensor_tensor(out=fT.ap[:, c, hf * 512:(hf + 1) * 512], in0=tmq.ap,
                                                                                  in1=tmq.ap, op=ALU.mult), [tmq], [fT])
                mod_hook()
            def ff2_pass(slot, wv, cg, hg, tiles, last):
                csl = slice(cg * 512, (cg + 1) * 512)
                for t in tiles:
                    ps = nb()
                    PE([mm(ps.ap, fT.ap[:, hg * 8 + k, t * 128:(t + 1) * 128], wv[:, k, :], start=(k == 0), stop=(k == 7)) for k in range(8)],
                       [slot, fT], [ps])
                    tmq = nxt(tmpf, "tmpf")
                    V(lambda e, ps=ps, tmq=tmq: e.tensor_tensor(out=tmq.ap, in0=ps.ap, in1=rts[1].ap[:, csl], op=ALU.mult),
                      [ps, rts[1]], [tmq])
                    if hg == 0:
                        V(lambda e, tmq=tmq, t=t: e.scalar_tensor_tensor(out=xb.ap[:, t, csl], in0=xb.ap[:, t, csl], scalar=ALPHA,
                                                                         in1=tmq.ap, op0=ALU.mult, op1=ALU.add), [xt[t], tmq], [xt[t]])
                    else:
                        V(lambda e, tmq=tmq, t=t: e.tensor_tensor(out=xb.ap[:, t, csl], in0=xb.ap[:, t, csl], in1=tmq.ap, op=ALU.add),
                          [xt[t], tmq], [xt[t]])
                    if last:
                        ln_step(t)

            for cg in range(2):
                for hg in range(3):
                    mod_hook()
                    slot, wv = W("w_ff2", l, hg * 1024, 8, cg * 512, 512)
                    ff2_pass(slot, wv, cg, hg, range(NT), False)
            mod_hook()
            mod_hook()
            last_t = [W("w_ff2", l, 3 * 1024, 8, cg * 512, 512, deep=(cg == 0)) for cg in range(2)]
            for half in range(2):
                for cg in range(2):
                    ff2_pass(last_t[cg][0], last_t[cg][1], cg, 3, range(half * 4, half * 4 + 4), cg == 1)
            mod_drain()

        V(lambda e: e.memset(modcol.ap, 0.0), [], [modcol])
        V(lambda e: e.memset(sm.ap, 0.0), [], [sm])
        V(lambda e: e.memset(sm.ap[:, 8:9], LN_EPS), [sm], [sm])
        V(lambda e: e.memset(sm.ap[:, 9:10], 1.0), [sm], [sm])
        import math
        def mark(name):
            PHASES.append((name, sum(len(c) for (_, c, _) in tk.ops["pe"])))

        def run_layer(l):
            lam_init = 0.8 - 0.6 * math.exp(-0.3 * l)
            mark("L%d mod" % l)
            if l == 0:
                for fi, f in enumerate(mod_tiles(0, (0, 1))):
                    f()
                    if fi == 0:
                        for i in range(NT):
                            load("act", xt[i].ap, x_d[i * 128:(i + 1) * 128, :], xt[i], after=[wslots[0]])
                modq.extend(mod_tiles(0, (3, 4, 2)))
            if stop == "mod":
                mod_drain()
                dump("modcol", modcol.ap[:, 0, :], [128, 48], modcol)
                dump("g1", rts[0].ap, [128, 1024], rts[0])
                return False
            mark("L%d hT" % l)
            make_hT(l % 2, 1, 0)
            if debug and l == 0:
                dump("hT", hT.ap, [128, 8, 1024], hT)
            if stop == "hT":
                return False
            mark("L%d A" % l)
            mixer_A(l, lam_init)
            if stop == "A0":
                return False
            if stop == "A":
                dump("oTa", oT.ap[:, 0], [128, 4, 1024], oT)
                return False
            mark("L%d B" % l)
            mixer_B(l)
            if stop == "B":
                dump("oTb", oT.ap[:, 1], [128, 4, 1024], oT)
                return False
            mark("L%d C" % l)
            mixer_C(l)
            if stop == "C":
                dump("oT", oT.ap, [128, 3, 4, 1024], oT)
                return False
            mark("L%d merge" % l)
            merge_phase(l)
            if debug and l == 0:
                dump("x1", xb.ap, [128, 8, 1024], xb)
            if stop == "merge":
                return False
            mark("L%d mlp" % l)
            mlp_phase(l)
            if debug and l == 0:
                dump("x2", xb.ap, [128, 8, 1024], xb)
            return True

        for l in range(DEPTH):
            if not run_layer(l):
                break
        mark("end")
        for i in range(NT):
            store(y_d[i * 128:(i + 1) * 128, :], xt[i].ap, xt[i])
        if plan is not None:
            tk.emit()
    return nc, dbg_out, wreqs


def build_program(debug=False, stop=None):
    _, _, reqs = _build(debug, stop, None)
    PHASES.clear()
    nc, dbg, _ = _build(debug, stop, reqs)
    return nc, dbg


def _const_tables(role):
    f = np.float32
    tabs = {}
    tabs["ident"] = np.eye(128, dtype=f)
    p = np.arange(128)
    tt = (np.arange(8)[None, :] * 128 + p[:, None])
    if role == "sample":
        row = (tt // 64).astype(f)
        col = (tt % 64).astype(f)
        inv = (np.float32(10000.0) ** (-(np.arange(16, dtype=f)) / np.float32(16))).astype(f)
        ang_r = (row[..., None] * inv).astype(f)
        ang_c = (col[..., None] * inv).astype(f)
        cos = np.zeros((128, 8, 64), f)
        sin = np.zeros((128, 8, 64), f)
        for rc, ang in enumerate((ang_r, ang_c)):
            for u in range(2):
                sl = slice(rc * 32 + u * 16, rc * 32 + u * 16 + 16)
                cos[:, :, sl] = np.cos(ang)
                sin[:, :, sl] = (-np.sin(ang)) if u == 0 else np.sin(ang)
        tabs["ropec"], tabs["ropes"] = cos, sin
    else:
        tabs["ropec"] = np.ones((128, 8, 64), f)
        tabs["ropes"] = np.zeros((128, 8, 64), f)
    db = np.zeros((128, 40), f)
    if role == "prompt":
        for r in range(4):
            for j in range(10):
                if j not in (2 * r, 2 * r + 1):
                    db[:, r * 10 + j] = NEG
    tabs["dbias"] = db
    wm = np.zeros((128, 2, 8, 128), f)
    kk = p[:, None]
    qq = p[None, :]
    for n in range(8):
        if role == "sample":
            if n >= 1:
                wm[:, 0, n, :] = (kk >= qq)
            if n <= 6:
                wm[:, 1, n, :] = (kk <= qq)
        else:
            if n % 2 == 1:
                wm[:, 0, n, :] = 1.0
            else:
                wm[:, 1, n, :] = 1.0
    tabs["wmask"] = wm.reshape(128, -1)
    tsm = np.zeros((128, 32), f)
    tsm[:, 0] = 0.0 if role == "sample" else NEG
    for n in range(8):
        if role == "sample":
            tsm[:, 1 + n] = 1.0
            tsm[:, 9 + n] = 1.0
        else:
            tsm[:, 1 + n] = 1.0 if n % 2 == 1 else 0.0
            tsm[:, 9 + n] = 1.0 if n % 2 == 0 else 0.0
    tsm[:, 17] = 127.0 - p
    tsm[:, 18] = p
    tabs["tsm"] = tsm
    rel = np.zeros((128, 4, 128), f)
    rel[:, 0] = np.maximum(qq - kk, 0)
    rel[:, 1] = np.maximum(kk - qq, 0)
    rel[:, 2] = (qq >= kk)
    rel[:, 3] = (kk >= qq)
    tabs["rel"] = rel.reshape(128, -1)
    qr = np.zeros((128, 2, 128), f)
    qr[:, 0] = (qq + 1)
    qr[:, 1] = (128 - qq)
    tabs["qrow"] = qr.reshape(128, -1)
    return tabs


_PROG = {}


def _get_prog(debug=False):
    if debug not in _PROG:
        _PROG[debug] = build_program(debug)
    return _PROG[debug]


def make_in_maps(inputs):
    f = np.float32
    g = {k: np.ascontiguousarray(np.asarray(v, dtype=f)) for k, v in inputs.items()}
    shared = {}
    for n in ("w_mod", "w_in", "w_gate", "w_pa", "w_pb", "w_pc", "w_o", "w_ff1", "w_ff2", "b_mod",
              "ln1_g", "ln1_b", "ln2_g", "ln2_b"):
        shared[n] = g[n]
    bmodc = np.ascontiguousarray(g["b_mod"].reshape(DEPTH, 48, 128).transpose(2, 0, 1).reshape(128, DEPTH * 48))
    bgatec = np.ascontiguousarray(g["b_gate"].reshape(DEPTH, 24, 128).transpose(2, 0, 1).reshape(128, DEPTH * 24))
    shared["pvec"] = np.concatenate([g[n].reshape(-1) for n in ("diff_lam", "diff_norm_g", "win_sink", "ret_decay", "ret_norm_g")]).reshape(1, -1)
    ctab = {r: _const_tables(r) for r in ("prompt", "sample")}
    maps = []
    for core in range(8):
        m = dict(shared)
        if core < 4:
            m["x"] = g["x_prompt"][4 * core:4 * core + 4].reshape(T, D)
            cv = g["c_ctx"]
            m["cdk"] = np.zeros((DEPTH, 256, 512), f)
            m["cdv"] = np.zeros((DEPTH, 256, 512), f)
            m["cwk"] = np.zeros((DEPTH, 256, 128), f)
            m["cwv"] = np.zeros((DEPTH, 256, 128), f)
            m["stin"] = np.zeros((DEPTH, 2, 128, 256), f)
        else:
            b = core - 4
            m["x"] = g["x_sample"][b]
            cv = g["c"][b]
            m["cdk"] = g["cache_diff_k"][b].reshape(DEPTH, 256, 512)
            m["cdv"] = g["cache_diff_v"][b].reshape(DEPTH, 256, 512)
            m["cwk"] = g["cache_win_k"][b].reshape(DEPTH, 256, 128)
            m["cwv"] = g["cache_win_v"][b].reshape(DEPTH, 256, 128)
            s = g["state_ret"][b].reshape(DEPTH, 2, 2, 2, 64, 128).transpose(0, 1, 3, 4, 2, 5)
            m["stin"] = np.ascontiguousarray(s.reshape(DEPTH, 2, 128, 256))
        cvec = np.ascontiguousarray(cv.reshape(8, 128).T)
        ct = ctab["prompt" if core < 4 else "sample"]
        parts = {"ident": ct["ident"], "ropec": ct["ropec"].reshape(128, -1), "ropes": ct["ropes"].reshape(128, -1), "dbias": ct["dbias"],
                 "tsm": ct["tsm"], "rel": ct["rel"], "qrow": ct["qrow"], "bmodc": bmodc, "bgatec": bgatec, "cvec": cvec}
        m["tabsA"] = np.concatenate([parts[k] for k in ("ident", "ropec", "ropes", "dbias", "tsm", "rel", "qrow", "bmodc", "bgatec", "cvec")], axis=1)
        m["wmask"] = ct["wmask"]
        maps.append({k: np.ascontiguousarray(v, dtype=np.float32) for k, v in m.items()})
    return maps


def assemble(results):
    f = np.float32
    y_prompt = np.concatenate([results[c]["y"].reshape(4, 256, D) for c in range(4)], axis=0)
    y_sample = np.stack([results[4 + b]["y"] for b in range(4)], axis=0)

    def gath(name, tail):
        parts = []
        for c in range(4):
            a = results[c][name].reshape(DEPTH, 4, 256, *tail).transpose(1, 0, 2, *range(3, 3 + len(tail)))
            parts.append(a)
        return np.ascontiguousarray(np.concatenate(parts, axis=0).astype(f))

    ndk = gath("nk", (4, 128))
    ndv = gath("nv", (4, 128))
    nwk = gath("nwk", (2, 64))
    nwv = gath("nwv", (2, 64))
    st = []
    for c in range(4):
        a = results[c]["nst"].reshape(DEPTH, 2, 4, 2, 64, 2, 128)
        a = a.transpose(2, 0, 1, 5, 3, 4, 6).reshape(4, DEPTH, 2, 4, 64, 128)
        st.append(a)
    nst = np.ascontiguousarray(np.concatenate(st, axis=0).astype(f))
    return (y_prompt.astype(f), y_sample.astype(f), ndk, ndv, nwk, nwv, nst)


def kernel(**inputs):
    nc, _ = _get_prog(False)
    maps = make_in_maps(inputs)
    res = run_bass_kernel_spmd(nc, maps, core_ids=list(range(8)))
    return assemble(res.results)
```

```python
import contextlib
import numpy as np
import concourse.bass as bass
import concourse.mybir as mybir
from concourse.bass_utils import run_bass_kernel_spmd

F32 = mybir.dt.float32
BF16 = mybir.dt.bfloat16
AF = mybir.ActivationFunctionType
ALU = mybir.AluOpType
AX = mybir.AxisListType

PHASES = []
DEPTH = 2
T = 1024
NT = 8
D = 1024
LN_EPS = 1e-5
ALPHA = (2 * DEPTH) ** 0.25
NEG = -30000.0
NW = 3


class Atom:
    def __init__(s, name, excl=False):
        s.name = name
        s.lw = None
        s.rd = []
        s.excl = excl
        s.dcnt = 0


class Buf:
    def __init__(s, ap, atoms):
        s.ap = ap
        s.atoms = atoms

    def __getitem__(s, idx):
        return s.ap[idx]


class Rec:
    def __init__(s):
        s.calls = []

    def __getattr__(s, name):
        def f(*a, **k):
            s.calls.append((name, a, k))
            return s
        return f


class Trk:
    def __init__(s, nc, es):
        s.nc = nc
        s.es = es
        s.ops = {k: [] for k in ("pe", "act", "dve", "pool", "sp")}
        s.cnt = {k: 0 for k in s.ops}
        s.seen = {k: {} for k in s.ops}
        s.sems = {}
        s.stores = {}
        for k in ("pe", "act", "dve", "pool"):
            s.sems[k] = es.enter_context(nc.semaphore("s_" + k))

    def dsem(s, atom):
        key = "d_" + atom.name
        if key not in s.sems:
            s.sems[key] = s.es.enter_context(s.nc.semaphore(key))
        return key

    def _waits(s, eng, reads, writes):
        deps = []
        for b in reads:
            for a in b.atoms:
                if a.lw is not None:
                    deps.append(a.lw)
                if a.excl:
                    deps.extend(a.rd)
        for b in writes:
            for a in b.atoms:
                if a.lw is not None:
                    deps.append(a.lw)
                deps.extend(a.rd)
        best = {}
        for (k, v) in deps:
            if k == "pe" and eng == "pe":
                continue
            if v > best.get(k, 0):
                best[k] = v
        out = []
        for k, v in best.items():
            if s.seen[eng].get(k, 0) >= v:
                continue
            s.seen[eng][k] = v
            out.append((k, v))
        return out

    def _mark(s, tok, reads, writes):
        for b in reads:
            for a in b.atoms:
                if a.excl:
                    a.lw = tok
                    a.rd = []
                else:
                    a.rd.append(tok)
        for b in writes:
            for a in b.atoms:
                a.lw = tok
                a.rd = []

    def op(s, eng, fn, reads=(), writes=()):
        w = s._waits(eng, reads, writes)
        s.cnt[eng] += 1
        tok = (eng, s.cnt[eng])
        rec = Rec()
        fn(rec)
        s.ops[eng].append((w, rec.calls, (eng, 1)))
        s._mark(tok, reads, writes)

    def dma(s, q, out, in_, buf, load, after=()):
        a0 = buf.atoms[0]
        key = s.dsem(a0)
        w = s._waits(q, tuple(after) if load else (buf,), (buf,) if load else ())
        a0.dcnt += 1
        tok = (key, 16 * a0.dcnt)
        s.ops[q].append((w, [("dma_start", (), dict(out=out, in_=in_))], (key, 16)))
        s._mark(tok, () if load else (buf,), (buf,) if load else ())
        if not load:
            s.stores[key] = 16 * a0.dcnt

    def emit(s):
        nc = s.nc
        fin = [(k, v) for k, v in s.stores.items() if s.seen["sp"].get(k, 0) < v]
        with nc.Block() as block:
            def run(e, name, final=None):
                for (w, calls, inc) in s.ops[name]:
                    for (k, v) in w:
                        e.wait_ge(s.sems[k], v)
                    ins = None
                    for (nm, a, kw) in calls:
                        ins = getattr(e, nm)(*a, **kw)
                    ins.then_inc(s.sems[inc[0]], inc[1])
                if final:
                    for (k, v) in final:
                        e.wait_ge(s.sems[k], v)

            @block.tensor
            def _(e):
                run(e, "pe")

            @block.scalar
            def _(e):
                run(e, "act")

            @block.vector
            def _(e):
                run(e, "dve")

            @block.gpsimd
            def _(e):
                run(e, "pool")

            @block.sync
            def _(e):
                run(e, "sp", fin)


def _build(debug=False, stop=None, plan=None):
    wreqs = []
    nc = bass.Bass("TRN2", target_bir_lowering=False)
    dbg_out = {}
    with contextlib.ExitStack() as es:
        tk = Trk(nc, es)

        def din(name, shape, dt=F32):
            return nc.dram_tensor(name, list(shape), dt, kind="ExternalInput").ap()

        def dout(name, shape):
            return nc.dram_tensor(name, list(shape), F32, kind="ExternalOutput").ap()

        x_d = din("x", [T, D])
        cdk_d = din("cdk", [DEPTH, 256, 512])
        cdv_d = din("cdv", [DEPTH, 256, 512])
        cwk_d = din("cwk", [DEPTH, 256, 128])
        cwv_d = din("cwv", [DEPTH, 256, 128])
        stin_d = din("stin", [DEPTH, 2, 128, 256])
        TABS = (("ident", 128), ("ropec", 512), ("ropes", 512), ("dbias", 40), ("tsm", 32), ("rel", 512), ("qrow", 256),
                ("bmodc", DEPTH * 48), ("bgatec", DEPTH * 24), ("cvec", 8))
        NTA = sum(n for _, n in TABS)
        tabsA_d = din("tabsA", [128, NTA])
        wmask_d = din("wmask", [128, 2 * 8 * 128])
        PV = (("diff_lam", 256), ("diff_norm_g", 128), ("win_sink", 8), ("ret_decay", 8), ("ret_norm_g", 128))
        NPV = DEPTH * sum(n for _, n in PV)
        pvec_d = din("pvec", [1, NPV])
        bmod_d = din("b_mod", [DEPTH, 6 * D])
        ln_d = {n: din(n, [DEPTH, D]) for n in ("ln1_g", "ln1_b", "ln2_g", "ln2_b")}
        wmod_d = din("w_mod", [DEPTH, D, 6 * D])
        win_d = din("w_in", [DEPTH, D, 3840])
        wgate_d = din("w_gate", [DEPTH, D, 3 * D])
        wp_d = [din(n, [DEPTH, 512, D]) for n in ("w_pa", "w_pb", "w_pc")]
        wo_d = din("w_o", [DEPTH, D, D])
        wff1_d = din("w_ff1", [DEPTH, D, 4 * D])
        wff2_d = din("w_ff2", [DEPTH, 4 * D, D])
        y_d = dout("y", [T, D])
        nk_d = dout("nk", [DEPTH, T, 512])
        nv_d = dout("nv", [DEPTH, T, 512])
        nwk_d = dout("nwk", [DEPTH, T, 128])
        nwv_d = dout("nwv", [DEPTH, T, 128])
        nst_d = dout("nst", [DEPTH, 2, 4, 128, 256])

        cnt = [0]

        def sb(shape, dt, name=None, excl=False):
            cnt[0] += 1
            name = "sb_" + (name or ("b%d" % cnt[0]))
            t = es.enter_context(nc.sbuf_tensor(name, list(shape), dt))
            return Buf(t.ap(), [Atom(name, excl)])

        xb = sb([128, NT, D], F32, "x")
        xa = [Atom("x%d" % i) for i in range(NT)]
        xb.atoms = xa
        xt = [Buf(xb.ap[:, i, :], [xa[i]]) for i in range(NT)]
        hT = sb([128, 8, T], BF16, "hT")
        AR_N = 4096 + 5120 + 5184 + 5120 + 5120 + 12288
        ar_t = es.enter_context(nc.sbuf_tensor("arena", [128, AR_N], BF16))
        ar = ar_t.ap()
        offs = {}
        o = 0
        for nm, sz in (("qT", 4096), ("kT", 5120), ("vTM", 5184), ("E0", 5120), ("E1a", 2560), ("E1b", 2560), ("oT", 12288)):
            offs[nm] = (o, sz)
            o += sz
        A_at = {nm: Atom("ar_" + nm) for nm in offs}

        def arv(nm, a=0, n=None):
            o0, sz = offs[nm]
            n = sz - a if n is None else n
            return ar[:, o0 + a:o0 + a + n]

        qT = Buf(arv("qT").rearrange("p (c t) -> p c t", c=4), [A_at["qT"]])
        kT = Buf(arv("kT").rearrange("p (c t) -> p c t", c=4), [A_at["kT"]])
        vTM = Buf(arv("vTM"), [A_at["vTM"]])
        Eb = [Buf(arv("E0").rearrange("p (j q) -> p j q", j=10), [A_at["E0"]]),
              Buf(ar[:, offs["E1a"][0]:offs["E1a"][0] + 5120].rearrange("p (j q) -> p j q", j=10), [A_at["E1a"], A_at["E1b"]])]
        pslots = [Buf(arv("E1a", 0, 2048), [A_at["E1a"]]), Buf(arv("E1b", 0, 2048), [A_at["E1b"]])]
        oT = Buf(arv("oT").rearrange("p (b c t) -> p b c t", b=3, c=4), [A_at["oT"]])
        mergedT = Buf(ar[:, 0:8192].rearrange("p (c t) -> p c t", c=8), [A_at["qT"], A_at["kT"]])
        macc = Buf(ar[:, 9216:9216 + 8192].bitcast(F32).rearrange("p (c t) -> p c t", c=4),
                   [A_at["vTM"], A_at["E0"]])
        fT = Buf(ar[:, 0:32768].rearrange("p (c t) -> p c t", c=32), [A_at[n] for n in offs])
        rQxf = Buf(arv("E0", 0, 2048).rearrange("p (c t) -> p c t", c=2), [A_at["E0"]])
        rQxb = Buf(arv("E0", 2048, 2048).rearrange("p (c t) -> p c t", c=2), [A_at["E0"]])
        rSall = Buf(ar[:, offs["E1a"][0]:offs["E1a"][0] + 4096].rearrange("p (d n r e) -> p d n r e", d=2, n=8, r=2),
                    [A_at["E1a"], A_at["E1b"]])
        rKzf = Buf(arv("qT", 2048, 2048).rearrange("p (n f) -> p n f", n=8), [A_at["qT"]])
        rKzb = Buf(arv("kT", 2560, 2048).rearrange("p (n f) -> p n f", n=8), [A_at["kT"]])

        wslots = []
        for i in range(NW):
            wslots.append(sb([128, 4096], BF16, "w%d" % i))
        rowtab = sb([128, 4, D], F32, "rowtab")
        rt_at = [Atom("rt%d" % i) for i in range(4)]
        rowtab.atoms = rt_at
        rts = [Buf(rowtab.ap[:, i, :], [rt_at[i]]) for i in range(4)]
        cgs_all = Buf(rowtab.ap[:, 2:4, :].rearrange("p a d -> p (a d)").bitcast(BF16).rearrange("p (n f) -> p n f", n=8),
                      [rt_at[2], rt_at[3]])
        TAB = Atom("TAB")
        tabsA = sb([128, NTA], F32, "tabsA")
        tabsA.atoms = [TAB]
        tviews = {}
        o_ = 0
        for nm, n in TABS:
            tviews[nm] = Buf(tabsA.ap[:, o_:o_ + n], [TAB])
            o_ += n
        identf = tviews["ident"]
        ropec = Buf(tviews["ropec"].ap.rearrange("p (t f) -> p t f", t=8), [TAB])
        ropes = Buf(tviews["ropes"].ap.rearrange("p (t f) -> p t f", t=8), [TAB])
        dbias, tsm, rel, qrow, bmodc, bgatec, cvec = (tviews[k] for k in ("dbias", "tsm", "rel", "qrow", "bmodc", "bgatec", "cvec"))
        PVA = Atom("PVEC")
        pvec = sb([128, NPV], F32, "pvec")
        pvec.atoms = [PVA]
        pviews = {}
        o_ = 0
        for nm, n in PV:
            pviews[nm] = Buf(pvec.ap[:, o_:o_ + DEPTH * n], [PVA])
            o_ += DEPTH * n
        dlam, dng, wsink, rdec, rng = (pviews[k] for k in ("diff_lam", "diff_norm_g", "win_sink", "ret_decay", "ret_norm_g"))
        identb = sb([128, 128], BF16, "identb")
        wmask = sb([128, 2, 8, 128], BF16, "wmaskb")
        silb = sb([128, 8], BF16, "silb")
        silf = sb([128, 8], F32, "silf")
        silrep = sb([128, 8, 128], BF16, "silrep")
        modcol = sb([128, 2, 48], F32, "modcol")
        sm = sb([128, 64], F32, "sm")
        gA = sb([128, 128], F32, "gA")
        sinkexp = sb([128, 8], F32, "sinkexp")
        lgb = sb([128, 8], F32, "lgb")
        lgcol = sb([128, 4], F32, "lgcol")
        gccar = sb([128, 2, 8, 2], F32, "gccar")
        carcol = tsm
        Dsum = sb([128, 4, 128], F32, "Dsum")
        xif = sb([128, 2, 128], F32, "xif")
        xib = sb([128, 2, 128], F32, "xib")
        zet = sb([128, 2, 4], F32, "zet")
        Sfp = [sb([128, 2, 128], F32, "Sfp%d" % i) for i in range(3)]
        stage = [sb([128, 512], F32, "stage%d" % i) for i in range(2)]
        tmpf = [sb([128, 512], F32, "tmpf%d" % i) for i in range(3)]
        tmpb = [sb([128, 512], BF16, "tmpb%d" % i) for i in range(3)]
        smallf = [sb([128, 16], F32, "smallf%d" % i) for i in range(4)]
        otm = [sb([128, 2, 512], BF16, "otm%d" % i) for i in range(2)]
        stage4 = stage + [Buf(o.ap.rearrange("p a b -> p (a b)").bitcast(F32), o.atoms) for o in otm]
        lnstats = [sb([128, 2, 6], F32, "lnstat%d" % i) for i in range(2)]
        lnmv = sb([128, 4], F32, "lnmv")
        lnmv2 = sb([128, 4], F32, "lnmv2")
        adt = [sb([128, 4, 128], BF16, "adt%d" % i) for i in range(2)]

        banks = []
        pairs = []
        for i in range(4):
            t = es.enter_context(nc.psum_tensor("ps%d" % i, [128, 1024], F32))
            a0, a1 = Atom("ps%da" % i, excl=True), Atom("ps%db" % i, excl=True)
            banks.append(Buf(t.ap()[:, 0:512], [a0]))
            banks.append(Buf(t.ap()[:, 512:1024], [a1]))
            pairs.append(Buf(t.ap(), [a0, a1]))
        bctr = [0]

        psm = banks[7]
        bsp = banks[6]

        def nb():
            b = banks[bctr[0] % 6]
            bctr[0] += 1
            return b

        def nb2():
            if bctr[0] % 2:
                bctr[0] += 1
            i = bctr[0] % 6
            bctr[0] += 2
            return banks[i], banks[i + 1], pairs[i // 2]

        rot = {}

        def nxt(lst, key):
            i = rot.get(key, 0)
            rot[key] = i + 1
            return lst[i % len(lst)]

        def V(fn, r, w):
            tk.op("dve", fn, r, w)

        def AC(fn, r, w):
            tk.op("act", fn, r, w)

        def PE(fns, r, w):
            tk.op("pe", lambda e, fs=fns: [f(e) for f in fs][-1], r, w)

        def mm(out, lhsT, rhs, start=True, stop=True):
            return lambda e: e.matmul(out, lhsT=lhsT, rhs=rhs, start=start, stop=stop)

        def tr(out, in_, ident):
            return lambda e: e.transpose(out, in_, ident)

        def load(q, out, in_, buf, after=()):
            tk.dma(q, out, in_, buf, True, after)

        def store(out, in_, buf):
            tk.dma("sp", out, in_, buf, False)

        wctr = [0]

        wdram = {"w_mod": wmod_d, "w_in": win_d, "w_gate": wgate_d, "w_pa": wp_d[0], "w_pb": wp_d[1], "w_pc": wp_d[2],
                 "w_o": wo_d, "w_ff1": wff1_d, "w_ff2": wff2_d}
        wissued = set()

        def w_issue(j, req):
            (dk, l_, r0, kc, c0, cols) = req
            slot = wslots[j % NW]
            view = slot.ap[:, 0:kc * cols].rearrange("p (k c) -> p k c", k=kc)
            src = wdram[dk][l_][r0:r0 + kc * 128, c0:c0 + cols].rearrange("(k p) n -> p k n", p=128)
            load("pool", view, src, slot)

        def W(dk, l_, r0, kc, c0, cols, deep=True):
            i = wctr[0]
            wctr[0] += 1
            req = (dk, l_, r0, kc, c0, cols)
            wreqs.append(req)
            if plan is None:
                w_issue(i, req)
            else:
                assert plan[i] == req
                for j in ((i, i + 1, i + 2) if deep else (i, i + 1)):
                    if j < len(plan) and j not in wissued:
                        wissued.add(j)
                        w_issue(j, plan[j])
            slot = wslots[i % NW]
            return slot, slot.ap[:, 0:kc * cols].rearrange("p (k c) -> p k c", k=kc)

        def dump(name, ap, shape, buf):
            if not debug:
                return
            d = nc.dram_tensor("dbg_" + name, list(shape), ap.dtype, kind="ExternalOutput").ap()
            dbg_out[name] = shape
            store(d, ap, buf)

        load("sp", tabsA.ap, tabsA_d, tabsA)
        load("sp", pvec.ap, pvec_d.partition_broadcast(128), pvec)
        load("pool", wmask.ap.rearrange("p a n q -> p (a n q)"), wmask_d, wmask)
        V(lambda e: e.tensor_copy(out=identb.ap, in_=identf.ap), [identf], [identb])
        AC(lambda e: e.activation(out=silf.ap, in_=cvec.ap, func=AF.Silu), [cvec], [silf])
        V(lambda e: e.tensor_copy(out=silb.ap, in_=silf.ap), [silf], [silb])
        V(lambda e: e.tensor_copy(out=silrep.ap, in_=silf.ap.unsqueeze(2).broadcast_to([128, 8, 128])), [silf], [silrep])

        def bc(ap, shape):
            return ap.broadcast_to(list(shape))

        modq = []

        def mod_tiles(l, vs):
            par = l % 2
            out = []
            for v in vs:
                for half in range(2):
                    if v in (0, 1, 3, 4):
                        def f(v=v, half=half):
                            slot, wv = W("w_mod", l, 0, 8, v * D + half * 512, 512)
                            fns = []
                            for j in range(4):
                                col = v * 8 + half * 4 + j
                                for k in range(8):
                                    fns.append(mm(psm.ap[:, col:col + 1], wv[:, k, j * 128:(j + 1) * 128], silb.ap[:, k:k + 1],
                                                  start=(k == 0), stop=(k == 7)))
                            PE(fns, [slot, silb], [psm])
                            c0 = v * 8 + half * 4
                            V(lambda e: e.tensor_tensor(out=modcol.ap[:, par, c0:c0 + 4], in0=psm.ap[:, c0:c0 + 4],
                                                        in1=bmodc.ap[:, l * 48 + c0:l * 48 + c0 + 4], op=ALU.add), [psm, bmodc], [modcol])
                            if v in (1, 4):
                                V(lambda e: e.tensor_scalar(out=modcol.ap[:, par, c0:c0 + 4], in0=modcol.ap[:, par, c0:c0 + 4],
                                                            scalar1=1.0, scalar2=None, op0=ALU.add), [modcol], [modcol])
                    else:
                        def f(v=v, half=half):
                            rt = rts[0] if v == 2 else rts[1]
                            if half == 0:
                                load("sp", rt.ap, bmod_d[l:l + 1, v * D:(v + 1) * D].partition_broadcast(128), rt)
                            slot, wv = W("w_mod", l, 0, 8, v * D + half * 512, 512)
                            ps = nb()
                            PE([mm(ps.ap, silrep.ap[:, k, :], wv[:, k, :], start=(k == 0), stop=(k == 7)) for k in range(8)],
                               [slot, silrep], [ps])
                            V(lambda e: e.tensor_tensor(out=rt.ap[:, half * 512:(half + 1) * 512], in0=ps.ap,
                                                        in1=rt.ap[:, half * 512:(half + 1) * 512], op=ALU.add), [ps, rt], [rt])
                    out.append(f)
            return out

        def mod_hook():
            if modq:
                modq.pop(0)()

        def mod_drain():
            while modq:
                modq.pop(0)()

        def make_hT(par, scv, shv):
            for tg in range(2):
                for k in range(8):
                    ps = nb()
                    PE([tr(ps.ap[:, j * 128:(j + 1) * 128], xb.ap[:, tg * 4 + j, k * 128:(k + 1) * 128], identf.ap)
                        for j in range(4)], [xt[tg * 4 + j] for j in range(4)] + [identf], [ps])
                    AC(lambda e, ps=ps, k=k, tg=tg: e.activation(
                        out=hT.ap[:, k, tg * 512:(tg + 1) * 512], in_=ps.ap, func=AF.Identity,
                        scale=modcol.ap[:, par, scv * 8 + k:scv * 8 + k + 1], bias=modcol.ap[:, par, shv * 8 + k:shv * 8 + k + 1]),
                       [ps, modcol], [hT])

        def zproj(slot, wv, t, width):
            ps = nb()
            PE([mm(ps.ap[:, 0:width], hT.ap[:, k, t * 128:(t + 1) * 128], wv[:, k, 0:width], start=(k == 0), stop=(k == 7))
                for k in range(8)], [slot, hT], [ps])
            return ps

        def rope(ps, c0, nsub, t, outap, outbuf, dup=False, cp=None):
            n = nsub * 64
            t1 = nxt(tmpf, "tmpf")
            t2 = nxt(tmpf, "tmpf")
            src = ps.ap[:, c0:c0 + n]
            V(lambda e: e.tensor_tensor(out=t1.ap[:, 0:n].rearrange("p (s f) -> p s f", f=64),
                                        in0=src.rearrange("p (s f) -> p s f", f=64),
                                        in1=bc(ropec.ap[:, t:t + 1, :], [128, nsub, 64]), op=ALU.mult),
              [ps, ropec], [t1])
            s5 = src.rearrange("p (s r u i) -> p s r u i", r=2, u=2, i=16)
            o5 = t2.ap[:, 0:n].rearrange("p (s r u i) -> p s r u i", r=2, u=2, i=16)
            sn = ropes.ap[:, t, :].rearrange("p (r u i) -> p r u i", r=2, u=2)
            for u in range(2):
                V(lambda e, u=u: e.tensor_tensor(
                    out=o5[:, :, :, u, :], in0=s5[:, :, :, 1 - u, :],
                    in1=bc(sn[:, :, u, :].unsqueeze(1), [128, nsub, 2, 16]), op=ALU.mult), [ps, ropes], [t2])
            if dup:
                V(lambda e: e.tensor_tensor(
                    out=outap.rearrange("p (s d f) -> p s d f", d=2, f=64),
                    in0=bc(t1.ap[:, 0:n].rearrange("p (s f) -> p s f", f=64).unsqueeze(2), [128, nsub, 2, 64]),
                    in1=bc(t2.ap[:, 0:n].rearrange("p (s f) -> p s f", f=64).unsqueeze(2), [128, nsub, 2, 64]),
                    op=ALU.add), [t1, t2], [outbuf])
            else:
                V(lambda e: e.tensor_tensor(out=outap, in0=t1.ap[:, 0:n], in1=t2.ap[:, 0:n], op=ALU.add),
                  [t1, t2], [outbuf])

        def to_fm(src_buf, src_ap, nchunks, dst_buf, dst_ap, on_dve=False):
            ps = nb()
            pb = ps.ap.bitcast(BF16)
            PE([tr(pb[:, j * 128:(j + 1) * 128], src_ap[:, j * 128:(j + 1) * 128], identb.ap) for j in range(nchunks)],
               [src_buf, identb], [ps])
            if on_dve:
                V(lambda e: e.tensor_copy(out=dst_ap, in_=pb[:, 0:nchunks * 128].rearrange("p (c t) -> p c t", c=nchunks)), [ps], [dst_buf])
            else:
                AC(lambda e: e.activation(out=dst_ap, in_=pb[:, 0:nchunks * 128].rearrange("p (c t) -> p c t", c=nchunks),
                                          func=AF.Copy), [ps], [dst_buf])

        def rope2(ps, c0, nsub, t, dup=False):
            n = nsub * 64
            nd = n * (2 if dup else 1)
            buf = nxt(tmpf, "tmpf")
            bb = buf.ap.bitcast(BF16)
            A = bb[:, 0:nd]
            B = bb[:, 512:512 + nd]
            src = ps.ap[:, c0:c0 + n]
            s3 = src.rearrange("p (s f) -> p s f", f=64)
            s5 = src.rearrange("p (s r u i) -> p s r u i", r=2, u=2, i=16)
            sn = ropes.ap[:, t, :].rearrange("p (r u i) -> p r u i", r=2, u=2)
            for d in range(2 if dup else 1):
                if dup:
                    Ad = A.rearrange("p (s d f) -> p s d f", d=2, f=64)[:, :, d, :]
                    Bd = B.rearrange("p (s d r u i) -> p s d r u i", d=2, r=2, u=2, i=16)[:, :, d]
                else:
                    Ad = A.rearrange("p (s f) -> p s f", f=64)
                    Bd = B.rearrange("p (s r u i) -> p s r u i", r=2, u=2, i=16)
                V(lambda e, Ad=Ad: e.tensor_tensor(out=Ad, in0=s3, in1=bc(ropec.ap[:, t:t + 1, :], [128, nsub, 64]), op=ALU.mult),
                  [ps, ropec], [buf])
                for u in range(2):
                    V(lambda e, u=u, Bd=Bd: e.tensor_tensor(
                        out=Bd[:, :, :, u, :], in0=s5[:, :, :, 1 - u, :],
                        in1=bc(sn[:, :, u, :].unsqueeze(1), [128, nsub, 2, 16]), op=ALU.mult), [ps, ropes], [buf])
            return buf, A, B

        def to_fm2(buf, A, B, nchunks, dst_buf, dst_ap):
            ps = nb()
            fns = []
            for j in range(nchunks):
                o_ = ps.ap[:, j * 128:(j + 1) * 128]
                fns.append(mm(o_, A[:, j * 128:(j + 1) * 128], identb.ap, start=True, stop=False))
                fns.append(mm(o_, B[:, j * 128:(j + 1) * 128], identb.ap, start=False, stop=True))
            PE(fns, [buf, identb], [ps])
            AC(lambda e: e.activation(out=dst_ap, in_=ps.ap[:, 0:nchunks * 128].rearrange("p (c t) -> p c t", c=nchunks), func=AF.Copy),
               [ps], [dst_buf])

        def to_fm2_dup(buf, A, B, dst_buf, dst_ap):
            ps = nb()
            fns = []
            for g_ in range(2):
                for d_ in range(2):
                    o_ = ps.ap[d_ * 64:(d_ + 1) * 64, g_ * 128:(g_ + 1) * 128]
                    fns.append(mm(o_, A[:, g_ * 64:(g_ + 1) * 64], identb.ap, start=True, stop=False))
                    fns.append(mm(o_, B[:, g_ * 64:(g_ + 1) * 64], identb.ap, start=False, stop=True))
            PE(fns, [buf, identb], [ps])
            AC(lambda e: e.activation(out=dst_ap, in_=ps.ap[:, 0:256].rearrange("p (c t) -> p c t", c=2), func=AF.Copy), [ps], [dst_buf])

        deferred = []
        obuf_ctr = [0]

        def defer(fn):
            deferred.append(fn)

        def flush(keep=0):
            while len(deferred) > keep:
                deferred.pop(0)()

        def rsqrt(out_ap, in_ap, scale, rbufs, wbufs):
            AC(lambda e: e.activation(out=out_ap, in_=in_ap, func=AF.Ln, scale=scale, bias=sm.ap[:, 8:9]), list(rbufs) + [sm], wbufs)
            AC(lambda e: e.activation(out=out_ap, in_=out_ap, func=AF.Exp, scale=-0.5), wbufs, wbufs)

        def out_fp32(ps, c0, n, dram_ap):
            st = nxt(stage4, "stage4")
            AC(lambda e: e.activation(out=st.ap[:, 0:n], in_=ps.ap[:, c0:c0 + n], func=AF.Copy), [ps], [st])
            store(dram_ap, st.ap[:, 0:n], st)
            return st

        def mixer_A(l, lam_init):
            tsl = slice(None)
            lv = dlam.ap[:, l * 256:(l + 1) * 256]
            s = sm
            V(lambda e: e.tensor_tensor(out=tmpf[0].ap[:, 0:64], in0=lv[:, 0:64], in1=lv[:, 64:128], op=ALU.mult), [dlam], [tmpf[0]])
            V(lambda e: e.tensor_reduce(out=s.ap[:, 0:1], in_=tmpf[0].ap[:, 0:64], axis=AX.X, op=ALU.add), [tmpf[0]], [s])
            V(lambda e: e.tensor_tensor(out=tmpf[0].ap[:, 0:64], in0=lv[:, 128:192], in1=lv[:, 192:256], op=ALU.mult), [dlam], [tmpf[0]])
            V(lambda e: e.tensor_reduce(out=s.ap[:, 1:2], in_=tmpf[0].ap[:, 0:64], axis=AX.X, op=ALU.add), [tmpf[0]], [s])
            AC(lambda e: e.activation(out=s.ap[:, 2:4], in_=s.ap[:, 0:2], func=AF.Exp), [s], [s])
            V(lambda e: e.tensor_tensor(out=s.ap[:, 4:5], in0=s.ap[:, 3:4], in1=s.ap[:, 2:3], op=ALU.subtract), [s], [s])
            V(lambda e: e.tensor_scalar(out=s.ap[:, 5:6], in0=s.ap[:, 4:5], scalar1=-lam_init, scalar2=None, op0=ALU.add), [s], [s])
            V(lambda e: e.tensor_scalar(out=gA.ap, in0=dng.ap[:, l * 128:(l + 1) * 128], scalar1=(1.0 - lam_init), scalar2=None,
                                        op0=ALU.mult), [dng], [gA])
            v4 = vTM.ap[:, 0:5160].rearrange("p (j h e) -> p j h e", j=10, h=4)
            V(lambda e: e.memset(v4[:, :, :, 128:129], 1.0), [], [vTM])
            for j in range(2):
                st = nxt(stage, "stage")
                load("sp", st.ap, cdk_d[l, j * 128:(j + 1) * 128, :], st)
                tb = nxt(tmpb, "tmpb")
                V(lambda e, st=st, tb=tb: e.tensor_copy(out=tb.ap, in_=st.ap), [st], [tb])
                to_fm(tb, tb.ap, 4, kT, kT.ap[:, :, 1024 + j * 128:1024 + (j + 1) * 128])
                st2 = nxt(stage, "stage")
                load("sp", st2.ap, cdv_d[l, j * 128:(j + 1) * 128, :], st2)
                V(lambda e, st2=st2, j=j: e.tensor_copy(out=v4[:, 8 + j, :, 0:128],
                                                        in_=st2.ap.rearrange("p (h e) -> p h e", h=4)), [st2], [vTM])
            for gi in range(3):
                mod_hook()
                slot, wv = W("w_in", l, 0, 8, gi * 512, 512)
                for t in range(NT):
                    ps = zproj(slot, wv, t, 512)
                    if gi != 1:
                        flush()
                    if gi == 0:
                        rb, rA, rB = rope2(ps, 0, 8, t)
                        defer(lambda rb=rb, rA=rA, rB=rB, t=t: to_fm2(rb, rA, rB, 4, qT, qT.ap[:, :, t * 128:(t + 1) * 128]))
                    elif gi == 1:
                        out_fp32(ps, 0, 512, nk_d[l, t * 128:(t + 1) * 128, :])
                        rb, rA, rB = rope2(ps, 0, 8, t)
                        flush()
                        defer(lambda rb=rb, rA=rA, rB=rB, t=t: to_fm2(rb, rA, rB, 4, kT, kT.ap[:, :, t * 128:(t + 1) * 128]))
                    else:
                        out_fp32(ps, 0, 512, nv_d[l, t * 128:(t + 1) * 128, :])
                        V(lambda e, ps=ps, t=t: e.tensor_copy(out=v4[:, t, :, 0:128],
                                                              in_=ps.ap.rearrange("p (h e) -> p h e", h=4)), [ps], [vTM])
            flush()
            if debug and l == 0:
                dump("qTa", qT.ap, [128, 4, 1024], qT)
                dump("kTa", kT.ap, [128, 4, 1280], kT)
            if stop == "A0":
                return

            def qk_steps(r, h, E):
                return [lambda jp=jp: qk_step(r, h, E, jp) for jp in range(5)]

            def qk(r, h, E):
                for f in qk_steps(r, h, E):
                    f()

            def qk_step(r, h, E, jp):
                if True:
                    b0, b1, pr = nb2()
                    PE([mm((b0, b1)[m].ap[:, jj * 256:(jj + 1) * 256], kT.ap[m * 64:(m + 1) * 64, h, (2 * jp + jj) * 128:(2 * jp + jj + 1) * 128],
                           qT.ap[m * 64:(m + 1) * 64, h, r * 256:(r + 1) * 256]) for jj in range(2) for m in range(2)], [kT, qT], [b0, b1])
                    AC(lambda e, pr=pr, jp=jp: e.activation(
                        out=E.ap[:, 2 * jp:2 * jp + 2, :].rearrange("p j (m q) -> p m j q", m=2),
                        in_=pr.ap.rearrange("p (m j q) -> p m j q", m=2, j=2),
                        func=AF.Exp, scale=0.125, bias=dbias.ap[:, r * 10 + 2 * jp:r * 10 + 2 * jp + 1]), [pr, dbias], [E])

            def pv_groups(r, h, E):
                pacc = [bsp, psm]

                def grp(m, sblk):
                    PE([mm(pacc[m].ap[:, sblk * 129:(sblk + 1) * 129],
                           E.ap[:, j, m * 256 + sblk * 128:m * 256 + (sblk + 1) * 128],
                           v4[:, j, h, :], start=(j == 0), stop=(j == 9)) for j in range(10)], [E, vTM], [pacc[m]])
                return [lambda m=m, sblk=sblk: grp(m, sblk) for m in range(2) for sblk in range(2)]

            def pv(r, h, E, ot):
                pacc = [bsp, psm]
                sf = nxt(smallf, "smallf")
                pa3 = pacc[0].ap[:, 0:258].rearrange("p (s e) -> p s e", s=2)
                pb3 = pacc[1].ap[:, 0:258].rearrange("p (s e) -> p s e", s=2)
                V(lambda e: e.reciprocal(out=sf.ap[:, 0:2], in_=pa3[:, :, 128]), [pacc[0]], [sf])
                V(lambda e: e.reciprocal(out=sf.ap[:, 2:4], in_=pb3[:, :, 128]), [pacc[1]], [sf])
                V(lambda e: e.tensor_scalar(out=sf.ap[:, 2:4], in0=sf.ap[:, 2:4], scalar1=sm.ap[:, 5:6], scalar2=None, op0=ALU.mult),
                  [sf, sm], [sf])
                o1 = stage[obuf_ctr[0] % 2]
                obuf_ctr[0] += 1
                o2 = nxt(tmpf, "tmpf")
                o13 = o1.ap[:, 0:256].rearrange("p (s e) -> p s e", s=2)
                o23 = o2.ap[:, 0:256].rearrange("p (s e) -> p s e", s=2)
                V(lambda e: e.tensor_tensor(out=o13, in0=pa3[:, :, 0:128], in1=bc(sf.ap[:, 0:2].unsqueeze(2), [128, 2, 128]),
                                            op=ALU.mult), [pacc[0], sf], [o1])
                V(lambda e: e.tensor_tensor(out=o23, in0=pb3[:, :, 0:128], in1=bc(sf.ap[:, 2:4].unsqueeze(2), [128, 2, 128]),
                                            op=ALU.mult), [pacc[1], sf], [o2])
                V(lambda e: e.tensor_tensor(out=o1.ap[:, 0:256], in0=o1.ap[:, 0:256], in1=o2.ap[:, 0:256], op=ALU.add), [o1, o2], [o1])
                V(lambda e: e.tensor_tensor(out=o2.ap[:, 0:256], in0=o1.ap[:, 0:256], in1=o1.ap[:, 0:256], op=ALU.mult), [o1], [o2])
                V(lambda e: e.tensor_reduce(out=sf.ap[:, 4:6], in_=o23, axis=AX.X, op=ALU.add), [o2], [sf])

                def tail():
                    rsqrt(sf.ap[:, 8:10], sf.ap[:, 4:6], 1.0 / 128.0, [sf], [sf])
                    V(lambda e: e.tensor_tensor(out=o13, in0=o13, in1=bc(sf.ap[:, 8:10].unsqueeze(2), [128, 2, 128]), op=ALU.mult),
                      [o1, sf], [o1])
                    V(lambda e: e.tensor_tensor(out=ot.ap[:, :, h * 128:(h + 1) * 128], in0=o13,
                                                in1=bc(gA.ap.unsqueeze(1), [128, 2, 128]), op=ALU.mult), [o1, gA], [ot])
                    if h == 3:
                        for sblk in range(2):
                            qb = r * 2 + sblk
                            defer(lambda sblk=sblk, qb=qb: to_fm(ot, ot.ap[:, sblk, :], 4, oT, oT.ap[:, 0, :, qb * 128:(qb + 1) * 128], on_dve=True))
                return tail

            seq = [(r, h) for r in range(4) for h in range(4)]
            qk(seq[0][0], seq[0][1], Eb[0])
            pend_tail = None
            for i, (r, h) in enumerate(seq):
                qs = qk_steps(seq[i + 1][0], seq[i + 1][1], Eb[(i + 1) % 2]) if i + 1 < len(seq) else []
                pgs = pv_groups(r, h, Eb[i % 2])
                for k_ in range(5):
                    if k_ < len(qs):
                        qs[k_]()
                    if k_ == 2 and pend_tail is not None:
                        pend_tail()
                        pend_tail = None
                    if k_ < 4:
                        pgs[k_]()
                if pend_tail is not None:
                    pend_tail()
                ot = otm[r % 2]
                pend_tail = pv(r, h, Eb[i % 2], ot)
                flush()
            pend_tail()
            flush()
            flush()

        def mixer_B(l):
            v3 = vTM.ap[:, 0:1300].rearrange("p (j g e) -> p j g e", j=10, g=2)
            V(lambda e: e.memset(v3[:, :, :, 64:65], 1.0), [], [vTM])
            AC(lambda e: e.activation(out=sinkexp.ap, in_=wsink.ap[:, l * 8:(l + 1) * 8], func=AF.Exp), [wsink], [sinkexp])
            for j in range(2):
                st = nxt(stage, "stage")
                load("sp", st.ap[:, 0:128], cwk_d[l, j * 128:(j + 1) * 128, :], st)
                tb = nxt(tmpb, "tmpb")
                V(lambda e, st=st, tb=tb: e.tensor_copy(
                    out=tb.ap[:, 0:256].rearrange("p (s d f) -> p s d f", d=2, f=64),
                    in_=bc(st.ap[:, 0:128].rearrange("p (s f) -> p s f", f=64).unsqueeze(2), [128, 2, 2, 64])), [st], [tb])
                to_fm(tb, tb.ap, 2, kT, kT.ap[:, 0:2, 1024 + j * 128:1024 + (j + 1) * 128])
                st2 = nxt(stage, "stage")
                load("sp", st2.ap[:, 0:128], cwv_d[l, j * 128:(j + 1) * 128, :], st2)
                V(lambda e, st2=st2, j=j: e.tensor_copy(out=v3[:, 8 + j, :, 0:64],
                                                        in_=st2.ap[:, 0:128].rearrange("p (g e) -> p g e", g=2)), [st2], [vTM])
            mod_hook()
            slot, wv = W("w_in", l, 0, 8, 1536, 512)
            for t in range(NT):
                ps = zproj(slot, wv, t, 512)
                flush()
                rb, rA, rB = rope2(ps, 0, 8, t)
                defer(lambda rb=rb, rA=rA, rB=rB, t=t: to_fm2(rb, rA, rB, 4, qT, qT.ap[:, :, t * 128:(t + 1) * 128]))
            mod_hook()
            slot, wv = W("w_in", l, 0, 8, 2048, 256)
            for t in range(NT):
                ps = zproj(slot, wv, t, 256)
                out_fp32(ps, 0, 128, nwk_d[l, t * 128:(t + 1) * 128, :])
                out_fp32(ps, 128, 128, nwv_d[l, t * 128:(t + 1) * 128, :])
                rb, rA, rB = rope2(ps, 0, 2, t)
                flush()
                defer(lambda rb=rb, rA=rA, rB=rB, t=t: to_fm2_dup(rb, rA, rB, kT, kT.ap[:, 0:2, t * 128:(t + 1) * 128]))
                V(lambda e, ps=ps, t=t: e.tensor_copy(out=v3[:, t, :, 0:64],
                                                      in_=ps.ap[:, 128:256].rearrange("p (g e) -> p g e", g=2)), [ps], [vTM])
            flush()
            if debug and l == 0:
                dump("qTb", qT.ap, [128, 4, 1024], qT)
                dump("kTb", kT.ap, [128, 4, 1280], kT)

            def tiles_of(n):
                tl = []
                if n >= 1:
                    tl.append((n - 1, 0))
                tl.append((n, None))
                if n <= 6:
                    tl.append((n + 1, 1))
                tl.append((8, "c"))
                tl.append((9, "c"))
                return tl

            def groups_of(tl):
                own = [i for i, (ch, kind) in enumerate(tl) if kind != "c"]
                ctx = [i for i, (ch, kind) in enumerate(tl) if kind == "c"]
                return [own[i:i + 2] for i in range(0, len(own), 2)] + [ctx]

            def ecol(tl, ti, hh):
                for grp in groups_of(tl):
                    if ti in grp:
                        ng = len(grp)
                        return grp[0] * 512 + (hh % 2) * ng * 256 + grp.index(ti) * 256 + (hh // 2) * 128
                raise AssertionError

            def qk_steps(n, g, E):
                tl = tiles_of(n)
                Ef = E.ap.rearrange("p j q -> p (j q)")
                own = [i for i, (ch, kind) in enumerate(tl) if kind != "c"]
                ctx = [i for i, (ch, kind) in enumerate(tl) if kind == "c"]
                groups = [own[i:i + 2] for i in range(0, len(own), 2)] + [ctx]
                return [lambda grp=grp: qk_group(n, g, E, tl, Ef, grp) for grp in groups]

            def qk(n, g, E):
                for f in qk_steps(n, g, E):
                    f()

            def qk_group(n, g, E, tl, Ef, grp):
                if True:
                    isctx = tl[grp[0]][1] == "c"
                    ng = len(grp)
                    b0, b1, pr = nb2()
                    fns = []
                    for gi2, ti in enumerate(grp):
                        ch = tl[ti][0]
                        for i2 in range(2):
                            for half in range(2):
                                head = 4 * g + 2 * i2 + half
                                c = head // 2
                                fns.append(mm((b0, b1)[half].ap[:, (gi2 * 2 + i2) * 128:(gi2 * 2 + i2 + 1) * 128],
                                              kT.ap[half * 64:(half + 1) * 64, g, ch * 128:(ch + 1) * 128],
                                              qT.ap[half * 64:(half + 1) * 64, c, n * 128:(n + 1) * 128]))
                    PE(fns, [kT, qT], [b0, b1])
                    eo = Ef[:, grp[0] * 512:grp[0] * 512 + ng * 512].rearrange("p (h x) -> p h x", h=2)
                    pin = pr.ap.rearrange("p (h x) -> p h x", h=2)[:, :, 0:ng * 256]
                    if isctx:
                        AC(lambda e, pin=pin, eo=eo: e.activation(out=eo, in_=pin, func=AF.Exp, scale=0.125, bias=tsm.ap[:, 0:1]),
                           [pr, tsm], [E])
                    else:
                        AC(lambda e, pin=pin, eo=eo: e.activation(out=eo, in_=pin, func=AF.Exp, scale=0.125), [pr], [E])
                    for gi2, ti in enumerate(grp):
                        kind = tl[ti][1]
                        if kind is not None and kind != "c":
                            blk = Ef[:, grp[0] * 512:grp[0] * 512 + ng * 512].rearrange("p (h x) -> p h x", h=2)[
                                :, :, gi2 * 256:(gi2 + 1) * 256].rearrange("p h (i q) -> p h i q", i=2)
                            V(lambda e, blk=blk, kind=kind: e.tensor_tensor(
                                out=blk, in0=blk, in1=bc(wmask.ap[:, kind, n:n + 1, :].unsqueeze(1), [128, 2, 2, 128]), op=ALU.mult),
                              [E, wmask], [E])

            def pv_groups(n, g, E, pacc):
                tl = tiles_of(n)

                def grp(h2):
                    fns = []
                    for hh in (2 * h2, 2 * h2 + 1):
                        for ti, (ch, kind) in enumerate(tl):
                            c0 = ecol(tl, ti, hh)
                            fns.append(mm(pacc.ap[:, hh * 65:(hh + 1) * 65], E.ap.rearrange("p j q -> p (j q)")[:, c0:c0 + 128],
                                          v3[:, ch, g, :], start=(ti == 0), stop=(ti == len(tl) - 1)))
                    PE(fns, [E, vTM], [pacc])
                return [lambda h2=h2: grp(h2) for h2 in range(2)]

            def pv(n, g, E, ot, pacc):
                tl = tiles_of(n)
                sf = nxt(smallf, "smallf")
                p3 = pacc.ap[:, 0:260].rearrange("p (h e) -> p h e", h=4)
                V(lambda e: e.tensor_tensor(out=sf.ap[:, 0:4], in0=p3[:, :, 64], in1=sinkexp.ap[:, 4 * g:4 * g + 4], op=ALU.add),
                  [pacc, sinkexp], [sf])
                V(lambda e: e.reciprocal(out=sf.ap[:, 4:8], in_=sf.ap[:, 0:4]), [sf], [sf])
                V(lambda e: e.tensor_tensor(out=ot.ap[:, 0, g * 256:(g + 1) * 256].rearrange("p (h e) -> p h e", h=4),
                                            in0=p3[:, :, 0:64], in1=bc(sf.ap[:, 4:8].unsqueeze(2), [128, 4, 64]), op=ALU.mult),
                  [pacc, sf], [ot])

            seq = [(n, g) for n in range(8) for g in range(2)]
            qk(0, 0, Eb[0])
            for i, (n, g) in enumerate(seq):
                qs = qk_steps(seq[i + 1][0], seq[i + 1][1], Eb[(i + 1) % 2]) if i + 1 < len(seq) else []
                pacc_i = (bsp, psm)[i % 2]
                pgs = pv_groups(n, g, Eb[i % 2], pacc_i)
                for k_ in range(max(len(qs), 2)):
                    if k_ < len(qs):
                        qs[k_]()
                    if k_ < 2:
                        pgs[k_]()
                ot = otm[n % 2]
                pv(n, g, Eb[i % 2], ot, pacc_i)
                flush()
                if g == 1:
                    defer(lambda ot=ot, n=n: to_fm(ot, ot.ap[:, 0, :], 4, oT, oT.ap[:, 1, :, n * 128:(n + 1) * 128], on_dve=True))
            flush()

        def mixer_C(l):
            rd = rdec.ap[:, l * 8:(l + 1) * 8]
            AC(lambda e: e.activation(out=lgb.ap, in_=rd, func=AF.Exp, scale=-1.0), [rdec], [lgb])
            AC(lambda e: e.activation(out=lgb.ap, in_=lgb.ap, func=AF.Ln, bias=sm.ap[:, 9:10]), [lgb, sm], [lgb])
            V(lambda e: e.tensor_scalar(out=lgb.ap, in0=lgb.ap, scalar1=-1.0, scalar2=None, op0=ALU.mult), [lgb], [lgb])
            lg4 = lgb.ap.rearrange("p (d r m) -> p d r m", d=2, r=2)
            lc3 = lgcol.ap.rearrange("p (d r) -> p d r", d=2)
            V(lambda e: e.tensor_copy(out=lc3[0:64, :, :], in_=lg4[0:64, :, :, 0]), [lgb], [lgcol])
            V(lambda e: e.tensor_copy(out=lc3[64:128, :, :], in_=lg4[64:128, :, :, 1]), [lgb], [lgcol])
            relf = rel.ap[:, 0:128]
            relb = rel.ap[:, 128:256]
            mf = rel.ap[:, 256:384]
            mb = rel.ap[:, 384:512]
            dtb = nxt(tmpf, "tmpf")
            dtv = dtb.ap.rearrange("p (h q) -> p h q", h=4)
            for h in range(4):
                AC(lambda e, h=h: e.activation(out=Dsum.ap[:, h, :], in_=relf, func=AF.Exp, scale=lgb.ap[:, h:h + 1]), [rel, lgb], [Dsum])
                AC(lambda e, h=h: e.activation(out=dtv[:, h, :], in_=relb, func=AF.Exp, scale=lgb.ap[:, 4 + h:5 + h]), [rel, lgb], [dtb])
            V(lambda e: e.tensor_tensor(out=Dsum.ap, in0=Dsum.ap, in1=bc(mf.unsqueeze(1), [128, 4, 128]), op=ALU.mult), [Dsum, rel], [Dsum])
            V(lambda e: e.tensor_tensor(out=dtv, in0=dtv, in1=bc(mb.unsqueeze(1), [128, 4, 128]), op=ALU.mult), [dtb, rel], [dtb])
            V(lambda e: e.tensor_tensor(out=Dsum.ap, in0=Dsum.ap, in1=dtv, op=ALU.add), [Dsum, dtb], [Dsum])
            for r in range(2):
                AC(lambda e, r=r: e.activation(out=xif.ap[:, r, :], in_=qrow.ap[:, 0:128], func=AF.Exp, scale=lgcol.ap[:, r:r + 1]),
                   [qrow, lgcol], [xif])
                AC(lambda e, r=r: e.activation(out=xib.ap[:, r, :], in_=qrow.ap[:, 128:256], func=AF.Exp, scale=lgcol.ap[:, 2 + r:3 + r]),
                   [qrow, lgcol], [xib])
            AC(lambda e: e.activation(out=zet.ap[:, 0, :], in_=lgb.ap[:, 0:4], func=AF.Exp, scale=tsm.ap[:, 17:18]), [lgb, tsm], [zet])
            AC(lambda e: e.activation(out=zet.ap[:, 1, :], in_=lgb.ap[:, 4:8], func=AF.Exp, scale=tsm.ap[:, 18:19]), [lgb, tsm], [zet])
            AC(lambda e: e.activation(out=sm.ap[:, 12:16], in_=lgcol.ap, func=AF.Exp, scale=128.0), [lgcol], [sm])
            car3 = tsm.ap[:, 1:17].rearrange("p (d n) -> p d n", d=2)
            V(lambda e: e.tensor_tensor(out=gccar.ap, in0=bc(sm.ap[:, 12:16].rearrange("p (d r) -> p d r", d=2).unsqueeze(2), [128, 2, 8, 2]),
                                        in1=bc(car3.unsqueeze(3), [128, 2, 8, 2]), op=ALU.mult), [sm, tsm], [gccar])
            mod_hook()
            slot, wv = W("w_in", l, 0, 8, 2304, 512)
            kz = {0: rKzf, 1: rKzb}
            for t in range(NT):
                ps = zproj(slot, wv, t, 512)
                tb = nxt(tmpb, "tmpb")
                AC(lambda e, ps=ps, tb=tb: e.activation(out=tb.ap[:, 0:256], in_=ps.ap[:, 0:256], func=AF.Copy), [ps], [tb])
                AC(lambda e, ps=ps, tb=tb: e.activation(out=tb.ap[:, 256:512], in_=ps.ap[:, 256:512], func=AF.Copy, scale=0.125), [ps], [tb])
                flush()
                defer(lambda tb=tb, t=t: to_fm(tb, tb.ap[:, 0:256], 2, qT, qT.ap[:, 0:2, t * 128:(t + 1) * 128]))
                defer(lambda tb=tb, t=t: to_fm(tb, tb.ap[:, 256:512], 2, kT, kT.ap[:, 0:2, t * 128:(t + 1) * 128]))
                for d in range(2):
                    V(lambda e, tb=tb, t=t, d=d: e.tensor_tensor(
                        out=kz[d].ap[:, t, :].rearrange("p (h f) -> p h f", h=4),
                        in0=tb.ap[:, 256:512].rearrange("p (h f) -> p h f", h=4),
                        in1=bc(zet.ap[:, d, :].unsqueeze(2), [128, 4, 64]), op=ALU.mult), [tb, zet], [kz[d]])
            slot, wv = W("w_in", l, 0, 8, 2816, 512)
            vr = vTM.ap[:, 0:4096].rearrange("p (n f) -> p n f", n=8)
            for t in range(NT):
                ps = zproj(slot, wv, t, 512)
                AC(lambda e, ps=ps, t=t: e.activation(out=vr[:, t, :], in_=ps.ap, func=AF.Copy), [ps], [vTM])
            flush()
            slot_g, wv_g = W("w_in", l, 0, 8, 3328, 512)
            cgq = []
            for t in range(NT):
                def cgf(t=t):
                    psg = zproj(slot_g, wv_g, t, 512)
                    AC(lambda e: e.activation(out=cgs_all.ap[:, t, :], in_=psg.ap, func=AF.Silu), [psg], [cgs_all])
                    V(lambda e: e.tensor_tensor(out=cgs_all.ap[:, t, :].rearrange("p (h e) -> p h e", h=4),
                                                in0=cgs_all.ap[:, t, :].rearrange("p (h e) -> p h e", h=4),
                                                in1=bc(rng.ap[:, l * 128:(l + 1) * 128].unsqueeze(1), [128, 4, 128]), op=ALU.mult),
                      [cgs_all, rng], [cgs_all])
                cgq.append(cgf)
            for (dst, xi) in ((rQxf, xif), (rQxb, xib)):
                for r in range(2):
                    V(lambda e, dst=dst, xi=xi, r=r: e.tensor_tensor(
                        out=dst.ap[:, r, :].rearrange("p (n q) -> p n q", n=8),
                        in0=qT.ap[:, r, :].rearrange("p (n q) -> p n q", n=8),
                        in1=bc(xi.ap[:, r:r + 1, :], [128, 8, 128]), op=ALU.mult), [qT, xi], [dst])
            for d in range(2):
                order = list(range(8)) if d == 0 else list(range(7, -1, -1))
                sprev = nxt(Sfp, "Sfp")
                load("sp", sprev.ap.rearrange("p r e -> p (r e)"), stin_d[l, d], sprev)
                for n in order:
                    V(lambda e, sprev=sprev, d=d, n=n: e.tensor_scalar(
                        out=rSall.ap[:, d, n], in0=sprev.ap, scalar1=tsm.ap[:, 1 + d * 8 + n:2 + d * 8 + n], scalar2=None,
                        op0=ALU.mult), [sprev, tsm], [rSall])
                    ps = nb()
                    fns = []
                    for r in range(2):
                        for m in range(2):
                            hd = 2 * r + m
                            fns.append(mm(ps.ap[m * 64:(m + 1) * 64, r * 128:(r + 1) * 128],
                                          kz[d].ap[:, n, hd * 64:(hd + 1) * 64], vr[:, n, hd * 128:(hd + 1) * 128]))
                    PE(fns, [kz[d], vTM], [ps])
                    snew = nxt(Sfp, "Sfp")
                    for r_ in range(2):
                        V(lambda e, sprev=sprev, snew=snew, d=d, n=n, r_=r_, ps=ps: e.scalar_tensor_tensor(
                            out=snew.ap[:, r_, :], in0=sprev.ap[:, r_, :], scalar=gccar.ap[:, d, n, r_:r_ + 1],
                            in1=ps.ap[:, r_ * 128:(r_ + 1) * 128], op0=ALU.mult, op1=ALU.add), [sprev, gccar, ps], [snew])
                    if (d == 0 and n % 2 == 1) or (d == 1 and n % 2 == 0):
                        store(nst_d[l, d, n // 2], snew.ap.rearrange("p r e -> p (r e)"), snew)
                    sprev = snew
                    if n % 2 == 1 and cgq:
                        cgq.pop(0)()
            while cgq:
                cgq.pop(0)()
            def qk_c(n):
                ad = adt[n % 2]
                for m in range(2):
                    psa = nb()
                    PE([mm(psa.ap[:, r * 128:(r + 1) * 128], kT.ap[m * 64:(m + 1) * 64, r, n * 128:(n + 1) * 128],
                           qT.ap[m * 64:(m + 1) * 64, r, n * 128:(n + 1) * 128]) for r in range(2)], [kT, qT], [psa])
                    V(lambda e, psa=psa, m=m: e.tensor_tensor(
                        out=ad.ap.rearrange("p (r t) q -> p r t q", t=2)[:, :, m, :],
                        in0=psa.ap[:, 0:256].rearrange("p (r q) -> p r q", r=2),
                        in1=Dsum.ap.rearrange("p (r t) q -> p r t q", t=2)[:, :, m, :], op=ALU.mult), [psa, Dsum], [ad])

            def pv_c(n):
                ad = adt[n % 2]
                psy = bsp
                fns = []
                for hd in range(4):
                    r, m = hd // 2, hd % 2
                    o_ = psy.ap[:, hd * 128:(hd + 1) * 128]
                    fns.append(mm(o_, ad.ap[:, hd, :], vr[:, n, hd * 128:(hd + 1) * 128], start=True, stop=False))
                    fns.append(mm(o_, rQxf.ap[m * 64:(m + 1) * 64, r, n * 128:(n + 1) * 128], rSall.ap[m * 64:(m + 1) * 64, 0, n, r, :],
                                  start=False, stop=False))
                    fns.append(mm(o_, rQxb.ap[m * 64:(m + 1) * 64, r, n * 128:(n + 1) * 128], rSall.ap[m * 64:(m + 1) * 64, 1, n, r, :],
                                  start=False, stop=True))
                PE(fns, [ad, vTM, rQxf, rQxb, rSall], [psy])
                return psy

            def ln_c(n, psy):
                y3 = psy.ap.rearrange("p (h e) -> p h e", h=4)
                sf = nxt(smallf, "smallf")
                sq = nxt(tmpf, "tmpf")
                t1 = nxt(tmpf, "tmpf")
                V(lambda e: e.tensor_reduce(out=sf.ap[:, 0:4], in_=y3, axis=AX.X, op=ALU.add), [psy], [sf])
                AC(lambda e: e.activation(out=sq.ap, in_=psy.ap, func=AF.Square), [psy], [sq])
                t13 = t1.ap.rearrange("p (h e) -> p h e", h=4)
                V(lambda e: e.tensor_scalar(out=sf.ap[:, 0:4], in0=sf.ap[:, 0:4], scalar1=1.0 / 128.0, scalar2=None, op0=ALU.mult), [sf], [sf])
                V(lambda e: e.tensor_tensor(out=t13, in0=y3, in1=bc(sf.ap[:, 0:4].unsqueeze(2), [128, 4, 128]), op=ALU.subtract),
                  [psy, sf], [t1])
                flush()
                V(lambda e: e.tensor_reduce(out=sf.ap[:, 4:8], in_=sq.ap.rearrange("p (h e) -> p h e", h=4), axis=AX.X, op=ALU.add),
                  [sq], [sf])
                V(lambda e: e.tensor_scalar(out=sf.ap[:, 4:8], in0=sf.ap[:, 4:8], scalar1=1.0 / 128.0, scalar2=None, op0=ALU.mult), [sf], [sf])
                V(lambda e: e.tensor_tensor(out=sf.ap[:, 8:12], in0=sf.ap[:, 0:4], in1=sf.ap[:, 0:4], op=ALU.mult), [sf], [sf])
                V(lambda e: e.tensor_tensor(out=sf.ap[:, 4:8], in0=sf.ap[:, 4:8], in1=sf.ap[:, 8:12], op=ALU.subtract), [sf], [sf])
                rsqrt(sf.ap[:, 12:16], sf.ap[:, 4:8], 1.0, [sf], [sf])
                V(lambda e: e.tensor_tensor(out=t13, in0=t13, in1=bc(sf.ap[:, 12:16].unsqueeze(2), [128, 4, 128]), op=ALU.mult),
                  [t1, sf], [t1])
                ot = otm[n % 2]
                V(lambda e: e.tensor_tensor(out=ot.ap[:, 0, :], in0=t1.ap, in1=cgs_all.ap[:, n, :], op=ALU.mult), [t1, cgs_all], [ot])
                defer(lambda: to_fm(ot, ot.ap[:, 0, :], 4, oT, oT.ap[:, 2, :, n * 128:(n + 1) * 128]))

            mod_drain()
            if l + 1 < DEPTH:
                modq.extend(mod_tiles(l + 1, (0, 1, 3, 4)))
            qk_c(0)
            for n in range(8):
                mod_hook()
                psy = pv_c(n)
                if n + 1 < 8:
                    qk_c(n + 1)
                flush()
                ln_c(n, psy)
            flush()

        lnm = [lnmv, lnmv2]

        def ln_load(l, gname, bname):
            load("sp", rts[2].ap, ln_d[gname][l:l + 1, :].partition_broadcast(128), rts[2])
            load("sp", rts[3].ap, ln_d[bname][l:l + 1, :].partition_broadcast(128), rts[3])

        def ln_stats(t):
            mv = lnm[t % 2]
            st = lnstats[t % 2]
            for hf in range(2):
                V(lambda e, hf=hf: e.bn_stats(out=st.ap[:, hf, :], in_=xb.ap[:, t, hf * 512:(hf + 1) * 512]), [xt[t]], [st])
            V(lambda e: e.bn_aggr(out=mv.ap[:, 0:2], in_=st.ap.rearrange("p a b -> p (a b)")), [st], [mv])
            rsqrt(mv.ap[:, 2:3], mv.ap[:, 1:2], 1.0, [mv], [mv])
            V(lambda e: e.scalar_tensor_tensor(out=mv.ap[:, 3:4], in0=mv.ap[:, 0:1], scalar=-1.0, in1=mv.ap[:, 2:3],
                                               op0=ALU.mult, op1=ALU.mult), [mv], [mv])
            AC(lambda e: e.activation(out=xb.ap[:, t, :], in_=xb.ap[:, t, :], func=AF.Identity, scale=mv.ap[:, 2:3],
                                      bias=mv.ap[:, 3:4]), [xt[t], mv], [xt[t]])

        def ln_affine(t):
            V(lambda e: e.tensor_tensor(out=xb.ap[:, t, :], in0=xb.ap[:, t, :], in1=rts[2].ap, op=ALU.mult), [xt[t], rts[2]], [xt[t]])
            V(lambda e: e.tensor_tensor(out=xb.ap[:, t, :], in0=xb.ap[:, t, :], in1=rts[3].ap, op=ALU.add), [xt[t], rts[3]], [xt[t]])

        def ln_step(t):
            ln_stats(t)
            if t >= 1:
                ln_affine(t - 1)
            if t == NT - 1:
                ln_affine(t)

        def merge_phase(l):
            ln_load(l, "ln1_g", "ln1_b")
            mod_drain()
            modq.extend(mod_tiles(l, (5,)))
            units = [(G, b) for G in range(2) for b in range(3)]

            def p_issue(i):
                G_, b_ = units[i]
                sl = pslots[i % 2]
                load("pool", sl.ap.rearrange("p (k c) -> p k c", k=4),
                     wp_d[b_][l][0:512, G_ * 512:(G_ + 1) * 512].rearrange("(k p) n -> p k n", p=128), sl)

            p_issue(0)
            for ui, (G, b) in enumerate(units):
                if True:
                    mod_hook()
                    gslot, gwv = W("w_gate", l, 0, 8, b * D + G * 512, 512)
                    if ui + 1 < len(units):
                        p_issue(ui + 1)
                    pslot = pslots[ui % 2]
                    pwv = pslot.ap.rearrange("p (k c) -> p k c", k=4)
                    for j in range(4):
                        col = l * 24 + b * 8 + G * 4 + j
                        for hf in range(2):
                            tsel = slice(hf * 512, (hf + 1) * 512)
                            psg = nb()
                            PE([mm(psg.ap, gwv[:, k, j * 128:(j + 1) * 128], hT.ap[:, k, tsel], start=(k == 0), stop=(k == 7))
                                for k in range(8)], [gslot, hT], [psg])
                            gt = nxt(tmpb, "tmpb")
                            AC(lambda e, psg=psg, gt=gt, col=col: e.activation(out=gt.ap, in_=psg.ap, func=AF.Sigmoid,
                                                                               bias=bgatec.ap[:, col:col + 1]), [psg, bgatec], [gt])
                            psp = nb()
                            PE([mm(psp.ap, pwv[:, k, j * 128:(j + 1) * 128], oT.ap[:, b, k, tsel], start=(k == 0), stop=(k == 3))
                                for k in range(4)], [pslot, oT], [psp])
                            if b == 0:
                                V(lambda e, psp=psp, gt=gt, j=j, tsel=tsel: e.tensor_tensor(out=macc.ap[:, j, tsel], in0=psp.ap, in1=gt.ap,
                                                                                           op=ALU.mult), [psp, gt], [macc])
                            else:
                                tmq = nxt(tmpf, "tmpf")
                                V(lambda e, psp=psp, gt=gt, tmq=tmq: e.tensor_tensor(out=tmq.ap, in0=psp.ap, in1=gt.ap, op=ALU.mult),
                                  [psp, gt], [tmq])
                                if b == 1:
                                    V(lambda e, tmq=tmq, j=j, tsel=tsel: e.tensor_tensor(out=macc.ap[:, j, tsel], in0=macc.ap[:, j, tsel],
                                                                                        in1=tmq.ap, op=ALU.add), [macc, tmq], [macc])
                                else:
                                    V(lambda e, tmq=tmq, j=j, tsel=tsel, G=G: e.tensor_tensor(
                                        out=mergedT.ap[:, G * 4 + j, tsel], in0=macc.ap[:, j, tsel], in1=tmq.ap, op=ALU.add),
                                      [macc, tmq], [mergedT])
            if debug and l == 0:
                dump("mergedT", mergedT.ap, [128, 8, 1024], mergedT)
            mod_drain()
            wo_t = [W("w_o", l, 0, 8, cg * 512, 512, deep=(cg == 0)) for cg in range(2)]
            for half in range(2):
                for cg in range(2):
                    slot, wv = wo_t[cg]
                    for t in range(half * 4, half * 4 + 4):
                        ps = nb()
                        PE([mm(ps.ap, mergedT.ap[:, k, t * 128:(t + 1) * 128], wv[:, k, :], start=(k == 0), stop=(k == 7)) for k in range(8)],
                           [slot, mergedT], [ps])
                        tmq = nxt(tmpf, "tmpf")
                        csl = slice(cg * 512, (cg + 1) * 512)
                        V(lambda e, ps=ps, tmq=tmq, csl=csl: e.tensor_tensor(out=tmq.ap, in0=ps.ap, in1=rts[0].ap[:, csl], op=ALU.mult),
                          [ps, rts[0]], [tmq])
                        V(lambda e, tmq=tmq, t=t, csl=csl: e.scalar_tensor_tensor(out=xb.ap[:, t, csl], in0=xb.ap[:, t, csl], scalar=ALPHA,
                                                                                  in1=tmq.ap, op0=ALU.mult, op1=ALU.add), [xt[t], tmq], [xt[t]])
                        if cg == 1:
                            ln_step(t)

        def mlp_phase(l):
            make_hT(l % 2, 4, 3)
            ln_load(l, "ln2_g", "ln2_b")
            if l + 1 < DEPTH:
                modq.extend(mod_tiles(l + 1, (2,)))
            for cgp in range(8):
                slot, wv = W("w_ff1", l, 0, 8, cgp * 512, 512)
                for j in range(4):
                    for hf in range(2):
                        ps = nb()
                        PE([mm(ps.ap, wv[:, k, j * 128:(j + 1) * 128], hT.ap[:, k, hf * 512:(hf + 1) * 512], start=(k == 0), stop=(k == 7))
                            for k in range(8)], [slot, hT], [ps])
                        tmq = nxt(tmpf, "tmpf")
                        AC(lambda e, ps=ps, tmq=tmq: e.activation(out=tmq.ap, in_=ps.ap, func=AF.Relu), [ps], [tmq])
                        V(lambda e, tmq=tmq, c=cgp * 4 + j, hf=hf: e.tensor_tensor(out=fT.ap[:, c, hf * 512:(hf + 1) * 512], in0=tmq.ap,
                                                                                  in1=tmq.ap, op=ALU.mult), [tmq], [fT])
                mod_hook()
            def ff2_pass(slot, wv, cg, hg, tiles, last):
                csl = slice(cg * 512, (cg + 1) * 512)
                for t in tiles:
                    ps = nb()
                    PE([mm(ps.ap, fT.ap[:, hg * 8 + k, t * 128:(t + 1) * 128], wv[:, k, :], start=(k == 0), stop=(k == 7)) for k in range(8)],
                       [slot, fT], [ps])
                    tmq = nxt(tmpf, "tmpf")
                    V(lambda e, ps=ps, tmq=tmq: e.tensor_tensor(out=tmq.ap, in0=ps.ap, in1=rts[1].ap[:, csl], op=ALU.mult),
                      [ps, rts[1]], [tmq])
                    if hg == 0:
                        V(lambda e, tmq=tmq, t=t: e.scalar_tensor_tensor(out=xb.ap[:, t, csl], in0=xb.ap[:, t, csl], scalar=ALPHA,
                                                                         in1=tmq.ap, op0=ALU.mult, op1=ALU.add), [xt[t], tmq], [xt[t]])
                    else:
                        V(lambda e, tmq=tmq, t=t: e.tensor_tensor(out=xb.ap[:, t, csl], in0=xb.ap[:, t, csl], in1=tmq.ap, op=ALU.add),
                          [xt[t], tmq], [xt[t]])
                    if last:
                        ln_step(t)

            for cg in range(2):
                for hg in range(3):
                    mod_hook()
                    slot, wv = W("w_ff2", l, hg * 1024, 8, cg * 512, 512)
                    ff2_pass(slot, wv, cg, hg, range(NT), False)
            mod_hook()
            mod_hook()
            last_t = [W("w_ff2", l, 3 * 1024, 8, cg * 512, 512, deep=(cg == 0)) for cg in range(2)]
            for half in range(2):
                for cg in range(2):
                    ff2_pass(last_t[cg][0], last_t[cg][1], cg, 3, range(half * 4, half * 4 + 4), cg == 1)
            mod_drain()

        V(lambda e: e.memset(modcol.ap, 0.0), [], [modcol])
        V(lambda e: e.memset(sm.ap, 0.0), [], [sm])
        V(lambda e: e.memset(sm.ap[:, 8:9], LN_EPS), [sm], [sm])
        V(lambda e: e.memset(sm.ap[:, 9:10], 1.0), [sm], [sm])
        import math
        def mark(name):
            PHASES.append((name, sum(len(c) for (_, c, _) in tk.ops["pe"])))

        def run_layer(l):
            lam_init = 0.8 - 0.6 * math.exp(-0.3 * l)
            mark("L%d mod" % l)
            if l == 0:
                for fi, f in enumerate(mod_tiles(0, (0, 1))):
                    f()
                    if fi == 0:
                        for i in range(NT):
                            load("act", xt[i].ap, x_d[i * 128:(i + 1) * 128, :], xt[i], after=[wslots[0]])
                modq.extend(mod_tiles(0, (3, 4, 2)))
            if stop == "mod":
                mod_drain()
                dump("modcol", modcol.ap[:, 0, :], [128, 48], modcol)
                dump("g1", rts[0].ap, [128, 1024], rts[0])
                return False
            mark("L%d hT" % l)
            make_hT(l % 2, 1, 0)
            if debug and l == 0:
                dump("hT", hT.ap, [128, 8, 1024], hT)
            if stop == "hT":
                return False
            mark("L%d A" % l)
            mixer_A(l, lam_init)
            if stop == "A0":
                return False
            if stop == "A":
                dump("oTa", oT.ap[:, 0], [128, 4, 1024], oT)
                return False
            mark("L%d B" % l)
            mixer_B(l)
            if stop == "B":
                dump("oTb", oT.ap[:, 1], [128, 4, 1024], oT)
                return False
            mark("L%d C" % l)
            mixer_C(l)
            if stop == "C":
                dump("oT", oT.ap, [128, 3, 4, 1024], oT)
                return False
            mark("L%d merge" % l)
            merge_phase(l)
            if debug and l == 0:
                dump("x1", xb.ap, [128, 8, 1024], xb)
            if stop == "merge":
                return False
            mark("L%d mlp" % l)
            mlp_phase(l)
            if debug and l == 0:
                dump("x2", xb.ap, [128, 8, 1024], xb)
            return True

        for l in range(DEPTH):
            if not run_layer(l):
                break
        mark("end")
        for i in range(NT):
            store(y_d[i * 128:(i + 1) * 128, :], xt[i].ap, xt[i])
        if plan is not None:
            tk.emit()
    return nc, dbg_out, wreqs


def build_program(debug=False, stop=None):
    _, _, reqs = _build(debug, stop, None)
    PHASES.clear()
    nc, dbg, _ = _build(debug, stop, reqs)
    return nc, dbg


def _const_tables(role):
    f = np.float32
    tabs = {}
    tabs["ident"] = np.eye(128, dtype=f)
    p = np.arange(128)
    tt = (np.arange(8)[None, :] * 128 + p[:, None])
    if role == "sample":
        row = (tt // 64).astype(f)
        col = (tt % 64).astype(f)
        inv = (np.float32(10000.0) ** (-(np.arange(16, dtype=f)) / np.float32(16))).astype(f)
        ang_r = (row[..., None] * inv).astype(f)
        ang_c = (col[..., None] * inv).astype(f)
        cos = np.zeros((128, 8, 64), f)
        sin = np.zeros((128, 8, 64), f)
        for rc, ang in enumerate((ang_r, ang_c)):
            for u in range(2):
                sl = slice(rc * 32 + u * 16, rc * 32 + u * 16 + 16)
                cos[:, :, sl] = np.cos(ang)
                sin[:, :, sl] = (-np.sin(ang)) if u == 0 else np.sin(ang)
        tabs["ropec"], tabs["ropes"] = cos, sin
    else:
        tabs["ropec"] = np.ones((128, 8, 64), f)
        tabs["ropes"] = np.zeros((128, 8, 64), f)
    db = np.zeros((128, 40), f)
    if role == "prompt":
        for r in range(4):
            for j in range(10):
                if j not in (2 * r, 2 * r + 1):
                    db[:, r * 10 + j] = NEG
    tabs["dbias"] = db
    wm = np.zeros((128, 2, 8, 128), f)
    kk = p[:, None]
    qq = p[None, :]
    for n in range(8):
        if role == "sample":
            if n >= 1:
                wm[:, 0, n, :] = (kk >= qq)
            if n <= 6:
                wm[:, 1, n, :] = (kk <= qq)
        else:
            if n % 2 == 1:
                wm[:, 0, n, :] = 1.0
            else:
                wm[:, 1, n, :] = 1.0
    tabs["wmask"] = wm.reshape(128, -1)
    tsm = np.zeros((128, 32), f)
    tsm[:, 0] = 0.0 if role == "sample" else NEG
    for n in range(8):
        if role == "sample":
            tsm[:, 1 + n] = 1.0
            tsm[:, 9 + n] = 1.0
        else:
            tsm[:, 1 + n] = 1.0 if n % 2 == 1 else 0.0
            tsm[:, 9 + n] = 1.0 if n % 2 == 0 else 0.0
    tsm[:, 17] = 127.0 - p
    tsm[:, 18] = p
    tabs["tsm"] = tsm
    rel = np.zeros((128, 4, 128), f)
    rel[:, 0] = np.maximum(qq - kk, 0)
    rel[:, 1] = np.maximum(kk - qq, 0)
    rel[:, 2] = (qq >= kk)
    rel[:, 3] = (kk >= qq)
    tabs["rel"] = rel.reshape(128, -1)
    qr = np.zeros((128, 2, 128), f)
    qr[:, 0] = (qq + 1)
    qr[:, 1] = (128 - qq)
    tabs["qrow"] = qr.reshape(128, -1)
    return tabs


_PROG = {}


def _get_prog(debug=False):
    if debug not in _PROG:
        _PROG[debug] = build_program(debug)
    return _PROG[debug]


def make_in_maps(inputs):
    f = np.float32
    g = {k: np.ascontiguousarray(np.asarray(v, dtype=f)) for k, v in inputs.items()}
    shared = {}
    for n in ("w_mod", "w_in", "w_gate", "w_pa", "w_pb", "w_pc", "w_o", "w_ff1", "w_ff2", "b_mod",
              "ln1_g", "ln1_b", "ln2_g", "ln2_b"):
        shared[n] = g[n]
    bmodc = np.ascontiguousarray(g["b_mod"].reshape(DEPTH, 48, 128).transpose(2, 0, 1).reshape(128, DEPTH * 48))
    bgatec = np.ascontiguousarray(g["b_gate"].reshape(DEPTH, 24, 128).transpose(2, 0, 1).reshape(128, DEPTH * 24))
    shared["pvec"] = np.concatenate([g[n].reshape(-1) for n in ("diff_lam", "diff_norm_g", "win_sink", "ret_decay", "ret_norm_g")]).reshape(1, -1)
    ctab = {r: _const_tables(r) for r in ("prompt", "sample")}
    maps = []
    for core in range(8):
        m = dict(shared)
        if core < 4:
            m["x"] = g["x_prompt"][4 * core:4 * core + 4].reshape(T, D)
            cv = g["c_ctx"]
            m["cdk"] = np.zeros((DEPTH, 256, 512), f)
            m["cdv"] = np.zeros((DEPTH, 256, 512), f)
            m["cwk"] = np.zeros((DEPTH, 256, 128), f)
            m["cwv"] = np.zeros((DEPTH, 256, 128), f)
            m["stin"] = np.zeros((DEPTH, 2, 128, 256), f)
        else:
            b = core - 4
            m["x"] = g["x_sample"][b]
            cv = g["c"][b]
            m["cdk"] = g["cache_diff_k"][b].reshape(DEPTH, 256, 512)
            m["cdv"] = g["cache_diff_v"][b].reshape(DEPTH, 256, 512)
            m["cwk"] = g["cache_win_k"][b].reshape(DEPTH, 256, 128)
            m["cwv"] = g["cache_win_v"][b].reshape(DEPTH, 256, 128)
            s = g["state_ret"][b].reshape(DEPTH, 2, 2, 2, 64, 128).transpose(0, 1, 3, 4, 2, 5)
            m["stin"] = np.ascontiguousarray(s.reshape(DEPTH, 2, 128, 256))
        cvec = np.ascontiguousarray(cv.reshape(8, 128).T)
        ct = ctab["prompt" if core < 4 else "sample"]
        parts = {"ident": ct["ident"], "ropec": ct["ropec"].reshape(128, -1), "ropes": ct["ropes"].reshape(128, -1), "dbias": ct["dbias"],
                 "tsm": ct["tsm"], "rel": ct["rel"], "qrow": ct["qrow"], "bmodc": bmodc, "bgatec": bgatec, "cvec": cvec}
        m["tabsA"] = np.concatenate([parts[k] for k in ("ident", "ropec", "ropes", "dbias", "tsm", "rel", "qrow", "bmodc", "bgatec", "cvec")], axis=1)
        m["wmask"] = ct["wmask"]
        maps.append({k: np.ascontiguousarray(v, dtype=np.float32) for k, v in m.items()})
    return maps


def assemble(results):
    f = np.float32
    y_prompt = np.concatenate([results[c]["y"].reshape(4, 256, D) for c in range(4)], axis=0)
    y_sample = np.stack([results[4 + b]["y"] for b in range(4)], axis=0)

    def gath(name, tail):
        parts = []
        for c in range(4):
            a = results[c][name].reshape(DEPTH, 4, 256, *tail).transpose(1, 0, 2, *range(3, 3 + len(tail)))
            parts.append(a)
        return np.ascontiguousarray(np.concatenate(parts, axis=0).astype(f))

    ndk = gath("nk", (4, 128))
    ndv = gath("nv", (4, 128))
    nwk = gath("nwk", (2, 64))
    nwv = gath("nwv", (2, 64))
    st = []
    for c in range(4):
        a = results[c]["nst"].reshape(DEPTH, 2, 4, 2, 64, 2, 128)
        a = a.transpose(2, 0, 1, 5, 3, 4, 6).reshape(4, DEPTH, 2, 4, 64, 128)
        st.append(a)
    nst = np.ascontiguousarray(np.concatenate(st, axis=0).astype(f))
    return (y_prompt.astype(f), y_sample.astype(f), ndk, ndv, nwk, nwv, nst)


def kernel(**inputs):
    nc, _ = _get_prog(False)
    maps = make_in_maps(inputs)
    res = run_bass_kernel_spmd(nc, maps, core_ids=list(range(8)))
    return assemble(res.results)
```
